# Optimizing a Trainium2 kernel written in Bass

```python
import math
import jax, jax.numpy as jnp
from jax import lax
import numpy as np

D_MODEL = 2048
BATCH = 32
SEQ = 256
DEPTH = 4
DEC_BATCH = 8
DEC_SEQ = 1024
PAST_LEN = 512

GRID_W = 64
N_MIXERS = 3
N_HYENA = (DEPTH + 2) // 3
N_ATTN = (DEPTH + 1) // 3
N_SSM = DEPTH // 3
D_FF = 4 * D_MODEL
N_MOD = 6
NORM_EPS = 1e-6
HY_PE_BANDS = 16
HY_PE_DIM = 2 * HY_PE_BANDS
HY_PE_MIN_PERIOD = 2.0
HY_PE_MAX_PERIOD = 4096.0
HY_FILTER_W = 64
HY_SHORT = 3
N_HEADS = 8
HEAD_DIM = D_MODEL // N_HEADS // 2
V_DIM = 2 * HEAD_DIM
ROPE_BASE = 10000.0
Q_BLOCK = 128
S5_GROUP = 16
S5_GROUPS = D_MODEL // S5_GROUP
S5_STATE = 64

kernel_name = 'hybrid_diffusion_prefix_step'

F32 = jnp.float32


def rms_norm(x, g):
    xf = x.astype(F32)
    y = xf * lax.rsqrt(jnp.mean(xf * xf, axis=-1, keepdims=True) + NORM_EPS)
    return (y * g.astype(F32)).astype(x.dtype)


def adaln_params(cond, w, b):
    m = (jax.nn.silu(cond) @ w + b).reshape(cond.shape[0], N_MOD, D_MODEL)
    return [m[:, j, None, :] for j in range(N_MOD)]


def pre_mod(x, g, shift, scale):
    return rms_norm(x, g) * (1 + scale) + shift


def post_add(x, o, g, gate):
    return x + gate * rms_norm(o, g)


def sq_relu_mlp(h, w1, w2):
    return jnp.square(jax.nn.relu(h @ w1)) @ w2


def short_conv(x, w, b):
    xp = jnp.pad(x, ((0, 0), (1, 1), (0, 0)))
    return xp[:, :-2] * w[0] + xp[:, 1:-1] * w[1] + xp[:, 2:] * w[2] + b


def hyena_filter(L, w1, b1, fr1, w2, b2, fr2, w3, log_alpha):
    t = jnp.arange(L, dtype=F32)
    periods = HY_PE_MIN_PERIOD * (HY_PE_MAX_PERIOD / HY_PE_MIN_PERIOD) ** (
        jnp.arange(HY_PE_BANDS, dtype=F32) / (HY_PE_BANDS - 1))
    ang = t[:, None] * (2.0 * math.pi / periods)[None]
    pe = jnp.concatenate([jnp.sin(ang), jnp.cos(ang)], axis=-1)
    h = jnp.sin(fr1.astype(F32) * (pe @ w1.astype(F32) + b1.astype(F32)))
    h = jnp.sin(fr2.astype(F32) * (h @ w2.astype(F32) + b2.astype(F32)))
    h = (h @ w3.astype(F32)).reshape(L, 2, D_MODEL)
    h = h * jnp.exp(-jnp.exp(log_alpha.astype(F32))[None, None] * t[:, None, None])
    k = jnp.concatenate([h[:, 0], jnp.zeros((1, D_MODEL), F32), h[:0:-1, 1]], axis=0)
    return k / (jnp.sum(jnp.abs(k), axis=0, keepdims=True) + 1e-6)


def long_conv(u, k):
    L = u.shape[1]
    U = jnp.fft.rfft(u.astype(F32), n=2 * L, axis=1)
    K = jnp.fft.rfft(k, axis=0)
    return jnp.fft.irfft(U * K[None], n=2 * L, axis=1)[:, :L]


def hyena_mixer(h, w_in, b_in, w_sh, b_sh, filt, skip, w_out, b_out):
    L = h.shape[1]
    z = short_conv(h @ w_in + b_in, w_sh, b_sh)
    x0, x1, v = jnp.split(z, 3, axis=-1)
    v = v * x1
    y = long_conv(v, hyena_filter(L, *filt)).astype(h.dtype) + v * skip
    return (x0 * y) @ w_out + b_out


def axial_rope(L):
    rows = L // GRID_W
    row = jnp.repeat(jnp.arange(rows), GRID_W).astype(F32)
    col = jnp.tile(jnp.arange(GRID_W), rows).astype(F32)
    half = HEAD_DIM // 2
    inv = ROPE_BASE ** (-jnp.arange(0, half, 2, dtype=F32) / half)
    ar = row[:, None] * inv
    ac = col[:, None] * inv
    ang = jnp.concatenate([ar, ar, ac, ac], axis=-1)
    return jnp.cos(ang), jnp.sin(ang)


def rotate_half_axial(x):
    a, b, c, d = jnp.split(x, 4, axis=-1)
    return jnp.concatenate([-b, a, -d, c], axis=-1)


def apply_rope(x, cos, sin):
    cos = cos[None, :, None, None]
    sin = sin[None, :, None, None]
    xf = x.astype(F32)
    return (xf * cos + rotate_half_axial(xf) * sin).astype(x.dtype)


def diff_qkv(h, w_qkv):
    B, L, _ = h.shape
    q, k, v = jnp.split(h @ w_qkv, 3, axis=-1)
    return (q.reshape(B, L, N_HEADS, 2, HEAD_DIM), k.reshape(B, L, N_HEADS, 2, HEAD_DIM),
            v.reshape(B, L, N_HEADS, V_DIM))


def diff_lambda(lam_p, lam_init):
    lp = lam_p.astype(F32)
    return jnp.exp(jnp.sum(lp[0] * lp[1])) - jnp.exp(jnp.sum(lp[2] * lp[3])) + lam_init


def diff_attend(q, k, v, lam):
    B, S = q.shape[:2]
    nb = S // Q_BLOCK
    qb = jnp.moveaxis(q.reshape(B, nb, Q_BLOCK, N_HEADS, 2, HEAD_DIM), 1, 0)
    kf = k.astype(F32)
    vf = v.astype(F32)
    scale = HEAD_DIM ** -0.5

    def block(qblk):
        s = jnp.einsum('bqhcd,bkhcd->bhcqk', qblk.astype(F32), kf) * scale
        p = jax.nn.softmax(s, axis=-1)
        w = p[:, :, 0] - lam * p[:, :, 1]
        return jnp.einsum('bhqk,bkhe->bqhe', w, vf)

    o = lax.map(block, qb)
    return jnp.moveaxis(o, 0, 1).reshape(B, S, N_HEADS, V_DIM)


def diff_output(o, g_sub, lam_init, w_o, dtype):
    B, S = o.shape[:2]
    o = rms_norm(o, g_sub) * (1.0 - lam_init)
    return o.reshape(B, S, D_MODEL).astype(dtype) @ w_o


def _lin_combine(e1, e2):
    a1, b1 = e1
    a2, b2 = e2
    return a1 * a2, a2 * b1 + b2


def s5_discretize(lam_re, lam_im, log_dt, b_re, b_im):
    lam = lax.complex(jnp.minimum(lam_re.astype(F32), -1e-4), lam_im.astype(F32))
    lam_dt = lam * jnp.exp(log_dt.astype(F32))[:, None]
    lam_bar = jnp.exp(lam_dt)
    b = lax.complex(b_re.astype(F32), b_im.astype(F32))
    b_bar = ((lam_bar - 1.0) / lam)[..., None] * b
    return lam_dt, lam_bar, b_bar


def s5_scan(u, lam_dt, lam_bar, b_bar, s0, reverse):
    L = u.shape[1]
    bu = jnp.einsum('blgh,gph->blgp', u.astype(jnp.complex64), b_bar)
    if reverse:
        bu = jnp.flip(bu, axis=1)
    a = jnp.broadcast_to(lam_bar, (1, L) + lam_bar.shape)
    _, xs = lax.associative_scan(_lin_combine, (a, bu), axis=1)
    steps = jnp.arange(1, L + 1, dtype=F32)
    xs = xs + jnp.exp(lam_dt[None] * steps[:, None, None])[None] * s0[:, None]
    final = xs[:, -1]
    if reverse:
        xs = jnp.flip(xs, axis=1)
    return xs, final


def s5_mixer(h, s0, lam_re, lam_im, log_dt, b_re, b_im, c_re, c_im, d_skip, w_glu, b_glu):
    B, L, _ = h.shape
    u = h.astype(F32).reshape(B, L, S5_GROUPS, S5_GROUP)
    y = d_skip.astype(F32).reshape(S5_GROUPS, S5_GROUP) * u
    finals = []
    for di in range(2):
        lam_dt, lam_bar, b_bar = s5_discretize(lam_re[di], lam_im[di], log_dt[di], b_re[di], b_im[di])
        xs, fin = s5_scan(u, lam_dt, lam_bar, b_bar, s0[:, di], di == 1)
        cm = lax.complex(c_re[di].astype(F32), c_im[di].astype(F32))
        y = y + jnp.einsum('blgp,ghp->blgh', xs, cm).real
        finals.append(fin)
    y = jax.nn.gelu(y.reshape(B, L, D_MODEL)).astype(h.dtype)
    a, g = jnp.split(y @ w_glu + b_glu, 2, axis=-1)
    return a * jax.nn.sigmoid(g), jnp.stack(finals, axis=1)


def setup_inputs(seed: int = 0) -> dict:
    key = jax.random.key(seed)
    ks = iter(jax.random.split(key, 64))

    def nrm(shape, s=1.0):
        return s * jax.random.normal(next(ks), shape, F32)

    D = D_MODEL
    G, P, H = S5_GROUPS, S5_STATE, S5_GROUP
    n = jnp.arange(P, dtype=F32)
    lengths = jnp.geomspace(8.0, 1024.0, D)
    return {
        'x_prompt': nrm((BATCH, SEQ, D)),
        'x_sample': nrm((DEC_BATCH, DEC_SEQ, D)),
        'cache_attn_k': nrm((DEC_BATCH, N_ATTN, PAST_LEN, N_HEADS, 2, HEAD_DIM)),
        'cache_attn_v': nrm((DEC_BATCH, N_ATTN, PAST_LEN, N_HEADS, V_DIM)),
        'state_s5_re': nrm((DEC_BATCH, N_SSM, 2, G, P), 0.1),
        'state_s5_im': nrm((DEC_BATCH, N_SSM, 2, G, P), 0.1),
        'c': nrm((DEC_BATCH, D)),
        'c_ctx': nrm((D,)),
        'w_mod': nrm((DEPTH, D, N_MOD * D), 0.5 * D ** -0.5),
        'b_mod': nrm((DEPTH, N_MOD * D), 0.02),
        'g_norm': 1.0 + nrm((DEPTH, 4, D), 0.02),
        'w_mlp_in': nrm((DEPTH, D, D_FF), D ** -0.5),
        'w_mlp_out': nrm((DEPTH, D_FF, D), D_FF ** -0.5),
        'hy_w_in': nrm((N_HYENA, D, 3 * D), D ** -0.5),
        'hy_b_in': nrm((N_HYENA, 3 * D), 0.02),
        'hy_w_short': nrm((N_HYENA, HY_SHORT, 3 * D), HY_SHORT ** -0.5),
        'hy_b_short': nrm((N_HYENA, 3 * D), 0.02),
        'hy_f_w1': nrm((N_HYENA, HY_PE_DIM, HY_FILTER_W), HY_PE_DIM ** -0.5),
        'hy_f_b1': nrm((N_HYENA, HY_FILTER_W), 0.1),
        'hy_f_freq1': 1.0 + nrm((N_HYENA, HY_FILTER_W), 0.1),
        'hy_f_w2': nrm((N_HYENA, HY_FILTER_W, HY_FILTER_W), HY_FILTER_W ** -0.5),
        'hy_f_b2': nrm((N_HYENA, HY_FILTER_W), 0.1),
        'hy_f_freq2': 1.0 + nrm((N_HYENA, HY_FILTER_W), 0.1),
        'hy_f_w3': nrm((N_HYENA, HY_FILTER_W, 2 * D), HY_FILTER_W ** -0.5),
        'hy_log_alpha': jnp.log(math.log(100.0) / lengths)[None] + nrm((N_HYENA, D), 0.01),
        'hy_skip': nrm((N_HYENA, D)),
        'hy_w_out': nrm((N_HYENA, D, D), D ** -0.5),
        'hy_b_out': nrm((N_HYENA, D), 0.02),
        'at_w_qkv': nrm((N_ATTN, D, 3 * D), D ** -0.5),
        'at_lam': nrm((N_ATTN, 4, HEAD_DIM), 0.1),
        'at_g_sub': 1.0 + nrm((N_ATTN, V_DIM), 0.02),
        'at_w_o': nrm((N_ATTN, D, D), D ** -0.5),
        's5_lam_re': -0.5 + nrm((N_SSM, 2, G, P), 0.01),
        's5_lam_im': math.pi * n + nrm((N_SSM, 2, G, P), 0.01),
        's5_log_dt': jax.random.uniform(next(ks), (N_SSM, 2, G), F32, math.log(1e-3), math.log(1e-1)),
        's5_b_re': nrm((N_SSM, 2, G, P, H), (2 * H) ** -0.5),
        's5_b_im': nrm((N_SSM, 2, G, P, H), (2 * H) ** -0.5),
        's5_c_re': nrm((N_SSM, 2, G, H, P), P ** -0.5),
        's5_c_im': nrm((N_SSM, 2, G, H, P), P ** -0.5),
        's5_d': nrm((N_SSM, D)),
        's5_w_glu': nrm((N_SSM, D, 2 * D), D ** -0.5),
        's5_b_glu': nrm((N_SSM, 2 * D), 0.02),
    }


def reference(x_prompt, x_sample, cache_attn_k, cache_attn_v, state_s5_re, state_s5_im, c, c_ctx,
              w_mod, b_mod, g_norm, w_mlp_in, w_mlp_out,
              hy_w_in, hy_b_in, hy_w_short, hy_b_short, hy_f_w1, hy_f_b1, hy_f_freq1,
              hy_f_w2, hy_f_b2, hy_f_freq2, hy_f_w3, hy_log_alpha, hy_skip, hy_w_out, hy_b_out,
              at_w_qkv, at_lam, at_g_sub, at_w_o,
              s5_lam_re, s5_lam_im, s5_log_dt, s5_b_re, s5_b_im, s5_c_re, s5_c_im, s5_d,
              s5_w_glu, s5_b_glu):
    xc, xl = x_prompt, x_sample
    cos, sin = axial_rope(x_sample.shape[1])
    new_k, new_v, new_s = [], [], []
    for i in range(DEPTH):
        kind, j = i % N_MIXERS, i // N_MIXERS
        mc = adaln_params(c_ctx[None], w_mod[i], b_mod[i])
        ml = adaln_params(c, w_mod[i], b_mod[i])
        g = g_norm[i]
        hc = pre_mod(xc, g[0], mc[0], mc[1])
        hl = pre_mod(xl, g[0], ml[0], ml[1])
        if kind == 0:
            filt = (hy_f_w1[j], hy_f_b1[j], hy_f_freq1[j], hy_f_w2[j], hy_f_b2[j], hy_f_freq2[j],
                    hy_f_w3[j], hy_log_alpha[j])
            hp = (hy_w_in[j], hy_b_in[j], hy_w_short[j], hy_b_short[j], filt, hy_skip[j],
                  hy_w_out[j], hy_b_out[j])
            oc = hyena_mixer(hc, *hp)
            ol = hyena_mixer(hl, *hp)
        elif kind == 1:
            lam_init = 0.8 - 0.6 * math.exp(-0.3 * i)
            lam = diff_lambda(at_lam[j], lam_init)
            qc, kc, vc = diff_qkv(hc, at_w_qkv[j])
            new_k.append(kc)
            new_v.append(vc)
            oc = diff_output(diff_attend(qc, kc, vc, lam), at_g_sub[j], lam_init, at_w_o[j], xc.dtype)
            ql, kl, vl = diff_qkv(hl, at_w_qkv[j])
            ql = apply_rope(ql, cos, sin)
            kl = apply_rope(kl, cos, sin)
            k_all = jnp.concatenate([cache_attn_k[:, j].astype(kl.dtype), kl], axis=1)
            v_all = jnp.concatenate([cache_attn_v[:, j].astype(vl.dtype), vl], axis=1)
            ol = diff_output(diff_attend(ql, k_all, v_all, lam), at_g_sub[j], lam_init, at_w_o[j], xl.dtype)
        else:
            sp = (s5_lam_re[j], s5_lam_im[j], s5_log_dt[j], s5_b_re[j], s5_b_im[j], s5_c_re[j],
                  s5_c_im[j], s5_d[j], s5_w_glu[j], s5_b_glu[j])
            s0c = jnp.zeros((xc.shape[0], 2, S5_GROUPS, S5_STATE), jnp.complex64)
            oc, fin = s5_mixer(hc, s0c, *sp)
            new_s.append(fin)
            s0l = lax.complex(state_s5_re[:, j].astype(F32), state_s5_im[:, j].astype(F32))
            ol, _ = s5_mixer(hl, s0l, *sp)
        xc = post_add(xc, oc, g[1], mc[2])
        xl = post_add(xl, ol, g[1], ml[2])
        xc = post_add(xc, sq_relu_mlp(pre_mod(xc, g[2], mc[3], mc[4]), w_mlp_in[i], w_mlp_out[i]), g[3], mc[5])
        xl = post_add(xl, sq_relu_mlp(pre_mod(xl, g[2], ml[3], ml[4]), w_mlp_in[i], w_mlp_out[i]), g[3], ml[5])
    new_cache_attn_k = jnp.stack(new_k, axis=1)
    new_cache_attn_v = jnp.stack(new_v, axis=1)
    s_all = jnp.stack(new_s, axis=1)
    return (xc, xl, new_cache_attn_k, new_cache_attn_v, s_all.real, s_all.imag)
```

```python
import math
from contextlib import ExitStack

import numpy as np
import concourse.bass as bass
import concourse.mybir as mybir
from concourse.bass_utils import run_bass_kernel_spmd

F32 = mybir.dt.float32
BF16 = mybir.dt.bfloat16
I32 = mybir.dt.int32
AF = mybir.ActivationFunctionType
ALU = mybir.AluOpType
AX = mybir.AxisListType

D = 2048
KC = 16
T = 1024
NT = 8
DFF = 8192
EPS = 1e-6
PI = math.pi
TWO_PI = 2.0 * math.pi

CFG = {"layers": [0, 1, 2, 3], "ncores": 8}


class Tok:
    __slots__ = ("lw", "rd")

    def __init__(self):
        self.lw = None
        self.rd = []


class Prog:
    NDMA = 8
    ENG = ("pe", "act", "dve", "pool", "sp")

    def __init__(self, nc, stack):
        self.nc = nc
        self.stack = stack
        self.epoch = {e: 0 for e in self.ENG}
        self.ekey = {e: e for e in self.ENG}
        self.depoch = 0
        self.dkey = {}
        self.stream = {e: [] for e in self.ENG}
        self.sem = {}
        self.cnt = {}
        self.seen = {e: {} for e in self.ENG}
        for e in ("pe", "act", "dve", "pool"):
            self.sem[e] = stack.enter_context(nc.semaphore("s_" + e))
            self.cnt[e] = 0
        self.dsem = {}
        self.dcnt = {}
        self.dnext = {}
        self.nd = {"sp": self.NDMA, "pool": 4}
        for q in ("sp", "pool"):
            self.dsem[q] = [stack.enter_context(nc.semaphore(f"d_{q}{i}")) for i in range(self.nd[q])]
            self.dcnt[q] = [0] * self.nd[q]
            self.dnext[q] = 0
        self.nops = 0

    def _wait(self, e, ev):
        sem, val, key = ev
        s = self.seen[e]
        if s.get(key, 0) >= val:
            return
        self.stream[e].append(("w", sem, val))
        s[key] = val

    def _deps(self, e, reads, writes, skip_same_pe=False):
        for t in reads:
            if t.lw is not None:
                self._wait(e, t.lw)
        for t in writes:
            if t.lw is not None and not (skip_same_pe and t.lw[2].startswith("pe")):
                self._wait(e, t.lw)
            for ev in t.rd:
                self._wait(e, ev)

    def _mark(self, ev, reads, writes):
        for t in reads:
            t.rd = [x for x in t.rd if x[2] != ev[2]] + [ev]
        for t in writes:
            t.lw = ev
            t.rd = []

    def op(self, e, meth, kw, reads=(), writes=(), acc=False):
        self._deps(e, reads, writes, skip_same_pe=(e == "pe" and acc))
        self.cnt[e] += 1
        self.stream[e].append(("i", meth, kw, self.sem[e], 1))
        ev = (self.sem[e], self.cnt[e], self.ekey[e])
        self._mark(ev, reads, writes)
        self.nops += 1

    def dma(self, q, out, in_, reads=(), writes=(), **kw):
        i = self.dnext[q]
        self.dnext[q] = (i + 1) % self.nd[q]
        sem = self.dsem[q][i]
        key = self.dkey.get((q, i), f"d_{q}{i}")
        if self.dcnt[q][i] > 0:
            self._wait(q, (sem, 16 * self.dcnt[q][i], key))
        self._deps(q, reads, writes)
        self.dcnt[q][i] += 1
        kw = dict(kw, out=out, in_=in_)
        self.stream[q].append(("i", "dma_start", kw, sem, 16))
        ev = (sem, 16 * self.dcnt[q][i], key)
        self._mark(ev, reads, writes)
        self.nops += 1

    def _all_events(self):
        evs = [(self.sem[e], self.cnt[e], self.ekey[e]) for e in ("pe", "act", "dve", "pool") if self.cnt[e] > 0]
        for q in ("sp", "pool"):
            for i in range(self.nd[q]):
                if self.dcnt[q][i]:
                    evs.append((self.dsem[q][i], 16 * self.dcnt[q][i], self.dkey.get((q, i), f"d_{q}{i}")))
        return evs

    def barrier(self):
        evs = self._all_events()
        for e in self.ENG:
            for ev in evs:
                self._wait(e, ev)
        for e in ("pe", "act", "dve", "pool"):
            if self.cnt[e] > CFG.get('semthr', 12000):
                self.epoch[e] += 1
                self.sem[e] = self.stack.enter_context(self.nc.semaphore(f"s_{e}_{self.epoch[e]}"))
                self.cnt[e] = 0
                self.ekey[e] = f"{e}#{self.epoch[e]}"
        for q in ("sp", "pool"):
            for i in range(self.nd[q]):
                if self.dcnt[q][i] * 16 > CFG.get('semthr', 12000):
                    self.depoch += 1
                    self.dsem[q][i] = self.stack.enter_context(self.nc.semaphore(f"d_{q}{i}_{self.depoch}"))
                    self.dcnt[q][i] = 0
                    self.dkey[(q, i)] = f"d_{q}{i}#{self.depoch}"

    def emit(self):
        for ev in self._all_events():
            self._wait("sp", ev)
        nc = self.nc

        def run(engobj, items):
            for it in items:
                if it[0] == "w":
                    engobj.wait_ge(it[1], it[2])
                else:
                    getattr(engobj, it[1])(**it[2]).then_inc(it[3], it[4])

        with nc.Block() as block:
            @block.tensor
            def _(e):
                run(e, self.stream["pe"])

            @block.scalar
            def _(e):
                run(e, self.stream["act"])

            @block.vector
            def _(e):
                run(e, self.stream["dve"])

            @block.gpsimd
            def _(e):
                run(e, self.stream["pool"])

            @block.sync
            def _(e):
                run(e, self.stream["sp"])


class Buf:
    def __init__(self, t, ntok=1):
        self.t = t
        self.toks = [Tok() for _ in range(ntok)]

    @property
    def tok(self):
        return self.toks[0]

    def __getitem__(self, k):
        return self.t[k]


class Rot:
    def __init__(self, bufs):
        self.bufs = bufs
        self.i = 0

    def next(self):
        b = self.bufs[self.i]
        self.i = (self.i + 1) % len(self.bufs)
        return b


def group_info(g):
    return (1, 1024) if g == 0 else (4, 256)


def make_consts():
    c = {}
    c["ident"] = np.eye(128, dtype=np.float32)
    for L in (1024, 256):
        n = np.arange(L, dtype=np.float64)
        ang = 2.0 * np.pi * np.outer(n, n) / (2 * L)
        c[f"cm{L}"] = np.cos(ang).astype(np.float32)
        c[f"sm{L}"] = np.sin(ang).astype(np.float32)
        t = np.arange(L, dtype=np.float32)
        periods = (2.0 * (4096.0 / 2.0) ** (np.arange(16, dtype=np.float32) / 15.0)).astype(np.float32)
        a = t[:, None] * (np.float32(2.0 * math.pi) / periods)[None].astype(np.float32)
        pe = np.concatenate([np.sin(a), np.cos(a)], axis=-1).astype(np.float32)
        c[f"pet{L}"] = np.ascontiguousarray(pe.T)
        wf = np.full((128, L // 128), 2.0 / (2 * L), np.float32)
        wf[0, 0] = 1.0 / (2 * L)
        c[f"wf{L}"] = wf
    c["negt"] = -(np.arange(8)[None, :] * 128 + np.arange(128)[:, None]).astype(np.float32)
    alt = ((-1.0) ** np.arange(1024)).astype(np.float32)
    c["altrow"] = alt[None, :].copy()
    c["altcol"] = alt[:128, None].copy()
    rows = 1024 // 64
    row = np.repeat(np.arange(rows), 64).astype(np.float32)
    col = np.tile(np.arange(64), rows).astype(np.float32)
    half = 64
    inv = (10000.0 ** (-np.arange(0, half, 2, dtype=np.float32) / half)).astype(np.float32)
    ar = row[:, None] * inv
    ac = col[:, None] * inv
    ang = np.concatenate([ar, ar, ac, ac], axis=-1)
    c["cost"] = np.ascontiguousarray(np.cos(ang).T.astype(np.float32))
    c["sint"] = np.ascontiguousarray(np.sin(ang).T.astype(np.float32))
    R = np.zeros((128, 128), np.float32)
    for m in range(128):
        q = m // 32
        if q in (0, 2):
            R[m + 32, m] = -1.0
        else:
            R[m - 32, m] = 1.0
    c["rmat"] = R
    c["iota"] = np.arange(1024, dtype=np.float32)[None, :].copy()
    for L in (1024, 256):
        tm = (np.arange(1024) % L).astype(np.float32)
        c[f"iota{L}f"] = tm[None, :].copy()
        c[f"iota{L}b"] = (L - 1 - tm)[None, :].copy()
    p = np.arange(128)
    mb = np.zeros((128, 2, 16), np.float32)
    for e in range(2):
        mb[p // 64 == e, e, :] = 1.0
    c["maskb"] = mb.reshape(128, 32)
    mc = np.zeros((128, 2, 64), np.float32)
    ee = (p % 32) // 16
    for e in range(2):
        mc[ee == e, e, :] = 1.0
    c["maskc"] = mc.reshape(128, 128)
    hb = np.zeros((128, 2), np.float32)
    for j2 in range(2):
        hb[(p % 64) // 32 == j2, j2] = 1.0
    c["hmb"] = hb
    hc = np.zeros((1, 2, 64), np.float32)
    hc[0, 0, :32] = 1.0
    hc[0, 1, 32:] = 1.0
    c["hmc"] = hc.reshape(1, 128)
    return c


CONST_SHAPES = None


class Builder:
    def __init__(self, layers):
        self.layers = layers
        self.nc = bass.Bass("TRN2", target_bir_lowering=False)
        self.st = ExitStack()
        self.in_names = []
        self.out_names = []

    def din(self, name, shape):
        self.in_names.append(name)
        return self.nc.dram_tensor(name, list(shape), F32, kind="ExternalInput").ap()

    def dout(self, name, shape):
        self.out_names.append(name)
        return self.nc.dram_tensor(name, list(shape), F32, kind="ExternalOutput").ap()

    def sb(self, stack, name, shape, dt=F32, ntok=1):
        self._uid = getattr(self, "_uid", 0) + 1
        return Buf(stack.enter_context(self.nc.sbuf_tensor(f"{name}_{self._uid}", list(shape), dt)), ntok)

    def bankA(self, n=1):
        if self.pa_i + n > 6:
            self.pa_i = 0
        if n == 2 and self.pa_i % 2:
            self.pa_i += 1
            if self.pa_i + n > 6:
                self.pa_i = 0
        b = self.pa_i
        self.pa_i += n
        return b, self.pa_tok[b:b + n]

    def bankT(self):
        b = self.pt_i
        self.pt_i = (self.pt_i + 1) % 2
        return b, self.pt_tok[b]

    def op(self, e, meth, reads=(), writes=(), acc=False, **kw):
        self.P.op(e, meth, kw, reads, writes, acc)

    def run_stage(self, fn, *a, **k):
        self._sn = getattr(self, "_sn", 0) + 1
        if self._sn > CFG.get("nstages", 10 ** 9):
            return
        fn(*a, **k)

    def build(self):
        nc, st = self.nc, self.st
        with st:
            self.P = Prog(nc, st)
            P = self.P
            L = self.layers
            kinds = sorted(set(i % 3 for i in L))
            self.xin = [self.din("xs", [T, D]), self.din("xp", [T, D])]
            self.y = [self.dout("ys", [T, D]), self.dout("yp", [T, D])]
            self.ytok = [[Tok() for _ in range(NT)] for _ in range(2)]
            self.cvec = self.din("cvec", [2, D])
            self.w_mod = {i: self.din(f"w_mod{i}", [D, 6 * D]) for i in L}
            self.b_mod = self.din("b_mod", [4, 6 * D])
            self.g_norm = self.din("g_norm", [16, D])
            self.w1 = {i: self.din(f"w_mlp_in{i}", [D, DFF]) for i in L}
            self.w2 = {i: self.din(f"w_mlp_out{i}", [DFF, D]) for i in L}
            self.cst = {}
            for k, v in make_consts().items():
                self.cst[k] = self.din("c_" + k, v.shape)
            if 0 in kinds:
                self.hy = {
                    "w_in": {i // 3: self.din(f"hy_w_in{i // 3}", [D, 3 * D]) for i in L if i % 3 == 0}, "b_in": self.din("hy_b_in", [2 * 48, 128]),
                    "w_short": self.din("hy_w_short", [6 * 48, 128]), "b_short": self.din("hy_b_short", [2 * 48, 128]),
                    "f_w1": self.din("hy_f_w1", [64, 64]), "f_b1": self.din("hy_f_b1", [2, 64]),
                    "f_fr1": self.din("hy_f_freq1", [2, 64]), "f_w2": self.din("hy_f_w2", [128, 64]),
                    "f_b2": self.din("hy_f_b2", [2, 64]), "f_fr2": self.din("hy_f_freq2", [2, 64]),
                    "f_w3": self.din("hy_f_w3", [128, 2 * D]), "la": self.din("hy_log_alpha", [2, D]),
                    "skip": self.din("hy_skip", [2 * 16, 128]), "w_out": {i // 3: self.din(f"hy_w_out{i // 3}", [D, D]) for i in L if i % 3 == 0},
                    "b_out": self.din("hy_b_out", [2, D]),
                }
            if 1 in kinds:
                self.at = {
                    "w_qkv": self.din("at_w_qkv", [D, 3 * D]), "lam": self.din("at_lam", [1, 512]),
                    "g_sub": self.din("at_g_sub", [1, 256]), "w_o": self.din("at_w_o", [D, D]),
                    "ck": self.din("ck", [512, D]), "cv": self.din("cv", [512, D]),
                }
            if 1 in kinds or True:
                self.nk = self.dout("nk", [T, D])
                self.nv = self.dout("nv", [T, D])
                self.nsre = self.dout("nsre", [512, 128])
                self.nsim = self.dout("nsim", [512, 128])
            if 2 in kinds:
                self.s5 = {
                    "lam_re": self.din("s5_lam_re", [256, 64]), "lam_im": self.din("s5_lam_im", [256, 64]),
                    "log_dt": self.din("s5_log_dt", [256, 64]),
                    "bz_re": self.din("s5_bz_re", [16384, 128]), "bz_im": self.din("s5_bz_im", [16384, 128]),
                    "cz_re": self.din("s5_cz_re", [16384, 128]), "cz_im": self.din("s5_cz_im", [16384, 128]),
                    "d": self.din("s5_d", [16, 128]), "w_glu": self.din("s5_w_glu", [D, 2 * D]),
                    "b_glu": self.din("s5_b_glu", [1, 2 * D]),
                    "s0re": self.din("s0re", [256, 64]), "s0im": self.din("s0im", [256, 64]),
                }
            self.modtab = nc.dram_tensor("modtab", [8, 6 * D], F32).ap()
            self.modtok = [Tok() for _ in range(4)]
            self.osc = nc.dram_tensor("osc", [T, D], F32).ap()
            self.otok = [Tok() for _ in range(NT)]

            self.psA = st.enter_context(nc.psum_tensor("psA", [128, 6, 512], F32))
            self.psT = st.enter_context(nc.psum_tensor("psT", [128, 2, 1024], BF16))
            self.pa_tok = [Tok() for _ in range(6)]
            self.pt_tok = [Tok() for _ in range(2)]
            self.pa_i = 0
            self.pt_i = 0
            self.identf = self.sb(st, "identf", [128, 128])
            self.identb = self.sb(st, "identb", [128, 128], BF16)
            self.epsc = self.sb(st, "epsc", [128, 1])
            self.hpic = self.sb(st, "hpic", [128, 1])
            self.HT = self.sb(st, "HT", [128, KC, T], BF16, ntok=KC)
            P.dma("sp", self.identf[:], self.cst["ident"], writes=[self.identf.tok])
            P.dma("pool", self.identb[:], self.cst["ident"], writes=[self.identb.tok])
            self.op("dve", "memset", writes=[self.epsc.tok], ap=self.epsc[:], constant=EPS)
            self.op("dve", "memset", writes=[self.hpic.tok], ap=self.hpic[:], constant=PI / 2)

            self.run_stage(self.stage_adaln)
            for g in CFG.get("groups", [0, 1]):
                prev = None
                for li, i in enumerate(L):
                    kind = i % 3
                    if prev is None:
                        self.run_stage(self.stage_post_pre, None, None, g, i, 0)
                    else:
                        self.run_stage(self.stage_post_pre, prev, 1, g, i, 0)
                    mix = {0: self.stage_hyena, 1: self.stage_attn, 2: self.stage_s5}[kind]
                    self.run_stage(mix, i, g)
                    self.run_stage(self.stage_post_pre, i, 0, g, i, 1)
                    self.run_stage(self.stage_mlp, i, g)
                    if li == len(L) - 1:
                        self.run_stage(self.stage_post_pre, i, 1, g, None, None)
                    prev = i
            if CFG.get("dbg") == "osc":
                dbg = self.dout("dbg", [T, D])
                P.barrier()
                for tt in range(NT):
                    P.dma("sp", dbg[tt * 128:(tt + 1) * 128, :], self.osc[tt * 128:(tt + 1) * 128, :], reads=[self.otok[tt]], writes=[Tok()])
            P.emit()
        return nc

    def stage_zero_outputs(self, kinds):
        with ExitStack() as s:
            z = self.sb(s, "z", [128, D])
            self.op("dve", "memset", writes=[z.tok], ap=z[:], constant=0.0)
            for tt in range(NT):
                if 1 not in kinds:
                    self.P.dma("sp", self.nk[tt * 128:(tt + 1) * 128, :], z[:], reads=[z.tok], writes=[Tok()])
                    self.P.dma("sp", self.nv[tt * 128:(tt + 1) * 128, :], z[:], reads=[z.tok], writes=[Tok()])
                if 2 not in kinds:
                    self.P.dma("sp", self.nsre[tt * 128:(tt + 1) * 128, :], z[:, 0:128], reads=[z.tok], writes=[Tok()])
                    self.P.dma("sp", self.nsim[tt * 128:(tt + 1) * 128, :], z[:, 0:128], reads=[z.tok], writes=[Tok()])
            self.P.barrier()

    def load_cols(self, s, dst_ap, src_rows_ap, n, dst_tok):
        tmp = self.sb(s, "lc", [128, 128])
        self.P.dma("sp", tmp[0:n, :], src_rows_ap, writes=[tmp.tok])
        b, tk = self.bankA()
        self.op("pe", "transpose", reads=[tmp.tok, self.identf.tok], writes=tk,
                out=self.psA[:, b, 0:n], in_=tmp[0:n, :], identity=self.identf[0:n, 0:n])
        self.op("dve", "tensor_copy", reads=tk, writes=[dst_tok], out=dst_ap, in_=self.psA[:, b, 0:n])

    def wtile(self, wrot, src_ap, ncols, kc=KC, mapping="kcp"):
        wt = wrot.next()
        if mapping == "kcp":
            src = src_ap.rearrange("(kc p) n -> p kc n", p=128)
        else:
            src = src_ap.rearrange("(p kc) n -> p kc n", kc=kc)
        self.P.dma("pool", wt[:, 0:kc, 0:ncols], src, writes=[wt.tok])
        return wt

    def stage_adaln(self):
        P = self.P
        with ExitStack() as s:
            cv = self.sb(s, "cv", [128, 2, 16])
            cs = self.sb(s, "cs", [128, 2, 16], BF16)
            P.dma("sp", cv[:], self.cvec.rearrange("g (p kc) -> p g kc", kc=16), writes=[cv.tok])
            self.op("act", "activation", reads=[cv.tok], writes=[cs.tok], out=cs[:], in_=cv[:], func=AF.Silu)
            wrot = Rot([self.sb(s, "wm", [128, KC, 512], BF16) for _ in range(3)])
            modrow = self.sb(s, "modrow", [2, 6 * D])
            brow = self.sb(s, "brow", [2, 6 * D])
            for i in self.layers:
                P.dma("sp", brow[:], self.b_mod[i:i + 1, :].to_broadcast([2, 6 * D]), writes=[brow.tok])
                for cb in range(24):
                    wt = self.wtile(wrot, self.w_mod[i][:, cb * 512:(cb + 1) * 512], 512, mapping="pkc")
                    b, tk = self.bankA()
                    for kc in range(KC):
                        self.op("pe", "matmul", reads=[cs.tok, wt.tok], writes=tk, acc=(kc > 0),
                                out=self.psA[0:2, b, :], lhsT=cs[:, :, kc], rhs=wt[:, kc, :],
                                start=(kc == 0), stop=(kc == KC - 1))
                    self.op("dve", "tensor_tensor", reads=tk + [brow.tok], writes=[modrow.tok],
                            out=modrow[:, cb * 512:(cb + 1) * 512], in0=self.psA[0:2, b, :],
                            in1=brow[:, cb * 512:(cb + 1) * 512], op=ALU.add)
                P.dma("sp", self.modtab[2 * i:2 * i + 2, :], modrow[:], reads=[modrow.tok], writes=[self.modtok[i]])
            P.barrier()

    def mod_row(self, i, g, j):
        return self.modtab[2 * i + g:2 * i + g + 1, j * D:(j + 1) * D]

    def load_sg_sh(self, s, i, g, sub):
        P = self.P
        SG = self.sb(s, "SG", [128, D])
        SH = self.sb(s, "SH", [128, D])
        tmp = self.sb(s, "tabtmp", [128, D])
        jsh, jsc, gi = (0, 1, 0) if sub == 0 else (3, 4, 2)
        P.dma("sp", SH[:], self.mod_row(i, g, jsh).to_broadcast([128, D]), reads=[self.modtok[i]], writes=[SH.tok])
        P.dma("sp", SG[:], self.mod_row(i, g, jsc).to_broadcast([128, D]), reads=[self.modtok[i]], writes=[SG.tok])
        P.dma("sp", tmp[:], self.g_norm[4 * i + gi:4 * i + gi + 1, :].to_broadcast([128, D]), writes=[tmp.tok])
        self.op("dve", "scalar_tensor_tensor", reads=[SG.tok, tmp.tok], writes=[SG.tok],
                out=SG[:], in0=SG[:], scalar=1.0, in1=tmp[:], op0=ALU.add, op1=ALU.mult)
        return SG, SH

    def load_gg(self, s, i, g, sub):
        P = self.P
        GG = self.sb(s, "GG", [128, D])
        tmp = self.sb(s, "ggtmp", [128, D])
        jg, gi = (2, 1) if sub == 0 else (5, 3)
        P.dma("sp", GG[:], self.mod_row(i, g, jg).to_broadcast([128, D]), reads=[self.modtok[i]], writes=[GG.tok])
        P.dma("sp", tmp[:], self.g_norm[4 * i + gi:4 * i + gi + 1, :].to_broadcast([128, D]), writes=[tmp.tok])
        self.op("dve", "tensor_tensor", reads=[GG.tok, tmp.tok], writes=[GG.tok],
                out=GG[:], in0=GG[:], in1=tmp[:], op=ALU.mult)
        return GG

    def rstd_of(self, src_ap, src_toks, junk, ss, rs, n):
        self.op("act", "activation", reads=src_toks, writes=[junk.tok, ss.tok],
                out=junk[:, 0:n], in_=src_ap, func=AF.Square, accum_out=ss[:])
        self.op("act", "activation", reads=[ss.tok, self.epsc.tok], writes=[rs.tok],
                out=rs[:], in_=ss[:], func=AF.Sqrt, scale=1.0 / n, bias=self.epsc[:])
        self.op("dve", "reciprocal", reads=[rs.tok], writes=[rs.tok], out=rs[:], in_=rs[:])

    def stage_post_pre(self, ip, subp, g, i, sub, both=True):
        P = self.P
        do_post = ip is not None and both
        do_pre = i is not None
        with ExitStack() as s:
            if do_post:
                GG = self.load_gg(s, ip, g, subp)
            if do_pre:
                SG, SH = self.load_sg_sh(s, i, g, sub)
            xrot = Rot([self.sb(s, "xt", [128, D]) for _ in range(2)])
            orot = Rot([self.sb(s, "ot", [128, D]) for _ in range(2)])
            junk = self.sb(s, "junk", [128, D])
            t1 = self.sb(s, "t1", [128, D])
            hbrot = Rot([self.sb(s, "hb", [128, D], BF16) for _ in range(2)])
            ss = self.sb(s, "ss", [128, 1])
            rs = self.sb(s, "rs", [128, 1])
            for tt in range(NT):
                rows = slice(tt * 128, (tt + 1) * 128)
                xt = xrot.next()
                ytk = self.ytok[g][tt]
                if ip is None:
                    P.dma("sp", xt[:], self.xin[g][rows, :], writes=[xt.tok])
                    P.dma("sp", self.y[g][rows, :], xt[:], reads=[xt.tok], writes=[ytk])
                else:
                    P.dma("sp", xt[:], self.y[g][rows, :], reads=[ytk], writes=[xt.tok])
                if do_post:
                    ot = orot.next()
                    P.dma("sp", ot[:], self.osc[rows, :], reads=[self.otok[tt]], writes=[ot.tok])
                    self.rstd_of(ot[:], [ot.tok], junk, ss, rs, D)
                    self.op("dve", "scalar_tensor_tensor", reads=[ot.tok, rs.tok, GG.tok], writes=[t1.tok],
                            out=t1[:], in0=ot[:], scalar=rs[:, 0:1], in1=GG[:], op0=ALU.mult, op1=ALU.mult)
                    self.op("dve", "tensor_tensor", reads=[t1.tok, xt.tok], writes=[xt.tok],
                            out=xt[:], in0=t1[:], in1=xt[:], op=ALU.add)
                    P.dma("sp", self.y[g][rows, :], xt[:], reads=[xt.tok], writes=[ytk])
                if do_pre:
                    self.rstd_of(xt[:], [xt.tok], junk, ss, rs, D)
                    self.op("dve", "scalar_tensor_tensor", reads=[xt.tok, rs.tok, SG.tok], writes=[t1.tok],
                            out=t1[:], in0=xt[:], scalar=rs[:, 0:1], in1=SG[:], op0=ALU.mult, op1=ALU.mult)
                    hb = hbrot.next()
                    self.op("dve", "tensor_tensor", reads=[t1.tok, SH.tok], writes=[hb.tok],
                            out=hb[:], in0=t1[:], in1=SH[:], op=ALU.add)
                    for q4 in range(4):
                        b, tk = self.bankT()
                        for j in range(4):
                            kc = q4 * 4 + j
                            self.op("pe", "transpose", reads=[hb.tok, self.identb.tok], writes=[tk], acc=(j > 0),
                                    out=self.psT[:, b, j * 128:(j + 1) * 128], in_=hb[:, kc * 128:(kc + 1) * 128],
                                    identity=self.identb[:])
                        self.op("act" if q4 % 2 == 0 else "dve",
                                "activation" if q4 % 2 == 0 else "tensor_copy",
                                reads=[tk], writes=self.HT.toks[q4 * 4:q4 * 4 + 4],
                                out=self.HT[:, q4 * 4:q4 * 4 + 4, rows],
                                in_=self.psT[:, b, 0:512].rearrange("p (j t) -> p j t", j=4),
                                **({"func": AF.Identity} if q4 % 2 == 0 else {}))
            P.barrier()

    def out_proj(self, s, MT, w_ap, wrot, bias_tab=None):
        P = self.P
        orot = Rot([self.sb(s, "oe", [128, 512]) for _ in range(3)])
        for cb in range(4):
            wt = self.wtile(wrot, w_ap[:, cb * 512:(cb + 1) * 512], 512)
            for tt in range(NT):
                b, tk = self.bankA()
                for kc in range(KC):
                    self.op("pe", "matmul", reads=[MT.toks[kc], wt.tok], writes=tk, acc=(kc > 0),
                            out=self.psA[:, b, :], lhsT=MT[:, kc, tt * 128:(tt + 1) * 128], rhs=wt[:, kc, 0:512],
                            start=(kc == 0), stop=(kc == KC - 1))
                oe = orot.next()
                if bias_tab is not None:
                    self.op("dve", "tensor_tensor", reads=tk + [bias_tab.tok], writes=[oe.tok],
                            out=oe[:], in0=self.psA[:, b, :], in1=bias_tab[:, cb * 512:(cb + 1) * 512], op=ALU.add)
                else:
                    self.op("act", "activation", reads=tk, writes=[oe.tok], out=oe[:], in_=self.psA[:, b, :],
                            func=AF.Identity)
                P.dma("sp", self.osc[tt * 128:(tt + 1) * 128, cb * 512:(cb + 1) * 512], oe[:],
                      reads=[oe.tok], writes=[self.otok[tt]])

    def stage_mlp(self, i, g):
        P = self.P
        HT = self.HT
        with ExitStack() as s:
            HID = self.sb(s, "HID", [128, 64, 512], BF16, ntok=64)
            wrot = Rot([self.sb(s, "wml", [128, KC, 512], BF16) for _ in range(3)])
            rrot = Rot([self.sb(s, "rl", [128, 512], BF16) for _ in range(3)])
            orot = Rot([self.sb(s, "oe", [128, 512]) for _ in range(4)])
            for half in range(2):
                t0 = half * 512
                for hb in range(16):
                    wt = self.wtile(wrot, self.w1[i][:, hb * 512:(hb + 1) * 512], 512)
                    for j in range(4):
                        b, tk = self.bankA()
                        for kc in range(KC):
                            self.op("pe", "matmul", reads=[HT.toks[kc], wt.tok], writes=tk, acc=(kc > 0),
                                    out=self.psA[:, b, :], lhsT=wt[:, kc, j * 128:(j + 1) * 128],
                                    rhs=HT[:, kc, t0:t0 + 512], start=(kc == 0), stop=(kc == KC - 1))
                        rl = rrot.next()
                        self.op("act", "activation", reads=tk, writes=[rl.tok], out=rl[:], in_=self.psA[:, b, :],
                                func=AF.Relu)
                        hc = hb * 4 + j
                        self.op("dve", "tensor_tensor", reads=[rl.tok], writes=[HID.toks[hc]],
                                out=HID[:, hc, :], in0=rl[:], in1=rl[:], op=ALU.mult)
                for cb in range(4):
                    banks = [0, 1, 2, 3] if cb % 2 == 0 else [2, 3, 4, 5]
                    banks = [0, 1, 2, 3]
                    for kq in range(4):
                        wt = self.wtile(wrot, self.w2[i][kq * D:(kq + 1) * D, cb * 512:(cb + 1) * 512], 512)
                        for tt in range(4):
                            b = banks[tt]
                            for kc in range(KC):
                                hc = kq * 16 + kc
                                first = (kq == 0 and kc == 0)
                                self.op("pe", "matmul", reads=[HID.toks[hc], wt.tok], writes=[self.pa_tok[b]],
                                        acc=not first, out=self.psA[:, b, :],
                                        lhsT=HID[:, hc, tt * 128:(tt + 1) * 128], rhs=wt[:, kc, 0:512],
                                        start=first, stop=(kq == 3 and kc == KC - 1))
                    for tt in range(4):
                        b = banks[tt]
                        oe = orot.next()
                        self.op("act" if tt % 2 == 0 else "dve", "activation" if tt % 2 == 0 else "tensor_copy",
                                reads=[self.pa_tok[b]], writes=[oe.tok], out=oe[:], in_=self.psA[:, b, :],
                                **({"func": AF.Identity} if tt % 2 == 0 else {}))
                        gt = half * 4 + tt
                        P.dma("sp", self.osc[gt * 128:(gt + 1) * 128, cb * 512:(cb + 1) * 512], oe[:],
                              reads=[oe.tok], writes=[self.otok[gt]])
            self.pa_i = 0
            P.barrier()

    def sin_reduced(self, ni, out_ap, out_tok, arg, n, cols):
        self.op("dve", "tensor_scalar", reads=[arg.tok], writes=[ni.tok], out=ni[0:n, 0:cols], in0=arg[0:n, 0:cols],
                scalar1=1.0 / TWO_PI, scalar2=None, op0=ALU.mult)
        self.op("dve", "scalar_tensor_tensor", reads=[ni.tok, arg.tok], writes=[arg.tok], out=arg[0:n, 0:cols],
                in0=ni[0:n, 0:cols], scalar=-TWO_PI, in1=arg[0:n, 0:cols], op0=ALU.mult, op1=ALU.add)
        self.op("dve", "tensor_scalar", reads=[arg.tok], writes=[arg.tok], out=arg[0:n, 0:cols], in0=arg[0:n, 0:cols],
                scalar1=3.1415925, scalar2=-3.1415925, op0=ALU.min, op1=ALU.max)
        self.op("act", "activation", reads=[arg.tok], writes=[out_tok], out=out_ap, in_=arg[0:n, 0:cols], func=AF.Sin)

    def stage_hyena(self, i, g):
        P = self.P
        j = i // 3
        hy = self.hy
        nseq, L = group_info(g)
        TC = L // 128
        HT = self.HT
        cst = self.cst
        with ExitStack() as s:
            MT = self.sb(s, "MT", [128, KC, T], BF16, ntok=KC)
            h2 = self.sb(s, "h2", [64, L], BF16)
            bin_c = self.sb(s, "bin_c", [128, 48])
            bsh_c = self.sb(s, "bsh_c", [128, 48])
            wsh_c = self.sb(s, "wsh_c", [128, 3, 48])
            skip_c = self.sb(s, "skip_c", [128, 16])
            with ExitStack() as s1:
                self.load_cols(s1, bin_c[:, :], hy["b_in"][j * 48:(j + 1) * 48, :], 48, bin_c.tok)
                self.load_cols(s1, bsh_c[:, :], hy["b_short"][j * 48:(j + 1) * 48, :], 48, bsh_c.tok)
                for tap in range(3):
                    self.load_cols(s1, wsh_c[:, tap, :], hy["w_short"][(j * 3 + tap) * 48:(j * 3 + tap + 1) * 48, :], 48, wsh_c.tok)
                self.load_cols(s1, skip_c[:, :], hy["skip"][j * 16:(j + 1) * 16, :], 16, skip_c.tok)
                pet = self.sb(s1, "pet", [32, L])
                P.dma("sp", pet[:], cst[f"pet{L}"], writes=[pet.tok])
                fw1 = self.sb(s1, "fw1", [32, 64])
                fw2 = self.sb(s1, "fw2", [64, 64])
                P.dma("sp", fw1[:], hy["f_w1"][j * 32:(j + 1) * 32, :], writes=[fw1.tok])
                P.dma("sp", fw2[:], hy["f_w2"][j * 64:(j + 1) * 64, :], writes=[fw2.tok])
                fcol = self.sb(s1, "fcol", [64, 6])
                for k, nm in enumerate(["f_b1", "f_fr1", "f_b2", "f_fr2"]):
                    P.dma("sp", fcol[:, k:k + 1], hy[nm][j:j + 1, :].rearrange("o (p x) -> (o p) x", x=1), writes=[fcol.tok])
                self.op("dve", "tensor_tensor", reads=[fcol.tok], writes=[fcol.tok], out=fcol[:, 4:5], in0=fcol[:, 0:1],
                        in1=fcol[:, 1:2], op=ALU.mult)
                self.op("dve", "tensor_tensor", reads=[fcol.tok], writes=[fcol.tok], out=fcol[:, 5:6], in0=fcol[:, 2:3],
                        in1=fcol[:, 3:4], op=ALU.mult)
                h1 = self.sb(s1, "h1", [64, L])
                arg = self.sb(s1, "farg", [64, L])
                ni = self.sb(s1, "fni", [64, L], I32)
                for (wmat, kdim, src, frc, fbc, dst) in ((fw1, 32, pet, 1, 4, h1), (fw2, 64, h1, 3, 5, h2)):
                    for tb in range(max(1, L // 512)):
                        n = min(512, L)
                        b, tk = self.bankA()
                        self.op("pe", "matmul", reads=[wmat.tok, src.tok], writes=tk, out=self.psA[0:64, b, 0:n],
                                lhsT=wmat[0:kdim, :], rhs=src[0:kdim, tb * n:(tb + 1) * n], start=True, stop=True)
                        self.op("dve", "tensor_scalar", reads=tk + [fcol.tok], writes=[arg.tok],
                                out=arg[0:64, tb * n:(tb + 1) * n], in0=self.psA[0:64, b, 0:n],
                                scalar1=fcol[:, frc:frc + 1], scalar2=fcol[:, fbc:fbc + 1], op0=ALU.mult, op1=ALU.add)
                    self.sin_reduced(ni, dst[:, :], dst.tok, arg, 64, L)
                P.barrier()
            s2 = ExitStack()
            CM = self.sb(s2, "CM", [128, TC, L], BF16)
            SM = self.sb(s2, "SM", [128, TC, L], BF16)
            P.dma("pool", CM[:], cst[f"cm{L}"].rearrange("(tc p) f -> p tc f", p=128), writes=[CM.tok])
            P.dma("pool", SM[:], cst[f"sm{L}"].rearrange("(tc p) f -> p tc f", p=128), writes=[SM.tok])
            wf = self.sb(s2, "wf", [128, TC])
            P.dma("sp", wf[:], cst[f"wf{L}"], writes=[wf.tok])
            negt = self.sb(s2, "negt", [128, 8])
            P.dma("sp", negt[:], cst["negt"], writes=[negt.tok])
            altcol = self.sb(s2, "altcol", [128, 1], BF16)
            altrow = self.sb(s2, "altrow", [1, 1024], BF16)
            P.dma("pool", altcol[:], cst["altcol"], writes=[altcol.tok])
            P.dma("pool", altrow[:], cst["altrow"], writes=[altrow.tok])
            onesf = self.sb(s2, "onesf", [128, 128])
            self.op("dve", "memset", writes=[onesf.tok], ap=onesf[:], constant=1.0)
            w3 = self.sb(s2, "w3", [64, 2 * D], BF16)
            for hh in range(2):
                P.dma("pool", w3[:, hh * D:(hh + 1) * D], hy["f_w3"][j * 64:(j + 1) * 64, hh * D:(hh + 1) * D], writes=[w3.tok])

            wrot = Rot([self.sb(s2, "wh", [128, KC, 128], BF16) for _ in range(3)])
            Zrot = Rot([self.sb(s2, "Z", [128, 3, nseq, L + 2]) for _ in range(1)])
            for Zb in Zrot.bufs:
                self.op("dve", "memset", writes=[Zb.tok], ap=Zb[:], constant=0.0)
            ZC = self.sb(s2, "ZC", [128, 3, T])
            VB = self.sb(s2, "VB", [128, T], BF16)
            VTM = self.sb(s2, "VTM", [128, NT, 128], BF16)
            KRE = self.sb(s2, "KRE", [128, TC, 128])
            KIM = self.sb(s2, "KIM", [128, TC, 128])
            KNY = self.sb(s2, "KNY", [1, 128])
            HS = self.sb(s2, "HS", [128, TC, 128], BF16)
            HD = self.sb(s2, "HD", [128, TC, 128], BF16)
            KF = self.sb(s2, "KF", [128, TC, 128])
            KB = self.sb(s2, "KB", [128, TC, 128])
            RN = self.sb(s2, "RN", [128, 128])
            YRE = self.sb(s2, "YRE", [128, NT, 128], BF16)
            YN = self.sb(s2, "YN", [128, NT, 128], BF16)
            YNY = self.sb(s2, "YNY", [1, nseq, 128], BF16)
            TA = self.sb(s2, "TA", [128, NT, 128])
            TB_ = self.sb(s2, "TB", [128, NT, 128])
            TY = self.sb(s2, "TY", [128, T])
            ea = self.sb(s2, "ea", [128, 128])
            DEC = Buf(TY.t, 1)
            DEC.toks = TY.toks
            DECv = TY[:, :].rearrange("p (a c) -> p a c", a=NT)
            for c in range(KC):
                P.dma("sp", ea[:], hy["la"][j:j + 1, c * 128:(c + 1) * 128].to_broadcast([128, 128]), writes=[ea.tok])
                self.op("act", "activation", reads=[ea.tok], writes=[ea.tok], out=ea[:], in_=ea[:], func=AF.Exp)
                for tc in range(TC):
                    self.op("act", "activation", reads=[ea.tok, negt.tok], writes=[DEC.tok], out=DECv[:, tc, :],
                            in_=ea[:], func=AF.Exp, scale=negt[:, tc:tc + 1])
                for di, KD in ((0, KF), (1, KB)):
                    for tc in range(TC):
                        b, tk = self.bankA()
                        self.op("pe", "matmul", reads=[h2.tok, w3.tok], writes=tk, out=self.psA[:, b, 0:128],
                                lhsT=h2[:, tc * 128:(tc + 1) * 128], rhs=w3[:, di * D + c * 128:di * D + (c + 1) * 128],
                                start=True, stop=True)
                        self.op("dve", "tensor_tensor", reads=tk + [DEC.tok], writes=[KD.tok], out=KD[:, tc, :],
                                in0=self.psA[:, b, 0:128], in1=DECv[:, tc, :], op=ALU.mult)
                self.op("dve", "memset", reads=[KB.tok], writes=[KB.tok], ap=KB[0:1, 0, :], constant=0.0)
                self.op("act", "activation", reads=[KF.tok], writes=[TA.tok], out=TA[:, 0:TC, :], in_=KF[:], func=AF.Abs)
                self.op("act", "activation", reads=[KB.tok], writes=[TB_.tok], out=TB_[:, 0:TC, :], in_=KB[:], func=AF.Abs)
                b, tk = self.bankA()
                for k in range(2 * TC):
                    AB = TA if k < TC else TB_
                    self.op("pe", "matmul", reads=[onesf.tok, AB.tok], writes=tk, acc=(k > 0), out=self.psA[:, b, 0:128],
                            lhsT=onesf[:], rhs=AB[:, k % TC, :], start=(k == 0), stop=(k == 2 * TC - 1))
                self.op("dve", "tensor_scalar", reads=tk, writes=[RN.tok], out=RN[:], in0=self.psA[:, b, 0:128],
                        scalar1=1e-6, scalar2=None, op0=ALU.add)
                self.op("dve", "reciprocal", reads=[RN.tok], writes=[RN.tok], out=RN[:], in_=RN[:])
                self.op("dve", "tensor_tensor", reads=[KF.tok, KB.tok], writes=[HS.tok], out=HS[:], in0=KF[:], in1=KB[:],
                        op=ALU.add)
                self.op("dve", "tensor_tensor", reads=[KF.tok, KB.tok], writes=[HD.tok], out=HD[:], in0=KB[:], in1=KF[:],
                        op=ALU.subtract)
                for (MAT, HX, KX) in ((CM, HS, KRE), (SM, HD, KIM)):
                    for fc in range(TC):
                        b, tk = self.bankA()
                        for tc in range(TC):
                            self.op("pe", "matmul", reads=[MAT.tok, HX.tok], writes=tk, acc=(tc > 0),
                                    out=self.psA[:, b, 0:128], lhsT=MAT[:, tc, fc * 128:(fc + 1) * 128], rhs=HX[:, tc, :],
                                    start=(tc == 0), stop=(tc == TC - 1))
                        self.op("dve", "scalar_tensor_tensor", reads=tk + [wf.tok, RN.tok], writes=[KX.tok],
                                out=KX[:, fc, :], in0=self.psA[:, b, 0:128], scalar=wf[:, fc:fc + 1], in1=RN[:],
                                op0=ALU.mult, op1=ALU.mult)
                b, tk = self.bankA()
                for tc in range(TC):
                    self.op("pe", "matmul", reads=[altcol.tok, HS.tok], writes=tk, acc=(tc > 0),
                            out=self.psA[0:1, b, 0:128], lhsT=altcol[:, 0:1], rhs=HS[:, tc, :],
                            start=(tc == 0), stop=(tc == TC - 1))
                self.op("dve", "scalar_tensor_tensor", reads=tk + [RN.tok], writes=[KNY.tok], out=KNY[:],
                        in0=self.psA[0:1, b, 0:128], scalar=1.0 / (2 * L), in1=RN[0:1, :], op0=ALU.mult, op1=ALU.mult)

                Z = Zrot.next()
                for part in range(3):
                    col0 = part * D + c * 128
                    wt = self.wtile(wrot, hy["w_in"][j][:, col0:col0 + 128], 128)
                    cc = part * 16 + c
                    for tb in range(2):
                        b, tk = self.bankA()
                        for kc in range(KC):
                            self.op("pe", "matmul", reads=[HT.toks[kc], wt.tok], writes=tk, acc=(kc > 0),
                                    out=self.psA[:, b, :], lhsT=wt[:, kc, 0:128], rhs=HT[:, kc, tb * 512:(tb + 1) * 512],
                                    start=(kc == 0), stop=(kc == KC - 1))
                        ns = 512 // L if L < 512 else 1
                        if L >= 512:
                            dst = Z[:, part, 0, 1 + tb * 512:1 + (tb + 1) * 512]
                            src = self.psA[:, b, :]
                        else:
                            dst = Z[:, part, tb * ns:(tb + 1) * ns, 1:L + 1]
                            src = self.psA[:, b, :].rearrange("p (s t) -> p s t", s=ns)
                        self.op("act", "activation", reads=tk + [bin_c.tok], writes=[Z.tok], out=dst, in_=src,
                                func=AF.Identity, bias=bin_c[:, cc:cc + 1])
                    zc = ZC[:, part, :].rearrange("p (s t) -> p s t", s=nseq)
                    self.op("dve", "tensor_scalar", reads=[Z.tok, wsh_c.tok, bsh_c.tok], writes=[ZC.tok], out=zc,
                            in0=Z[:, part, :, 0:L], scalar1=wsh_c[:, 0, cc:cc + 1], scalar2=bsh_c[:, cc:cc + 1],
                            op0=ALU.mult, op1=ALU.add)
                    for tap in (1, 2):
                        self.op("dve", "scalar_tensor_tensor", reads=[Z.tok, wsh_c.tok, ZC.tok], writes=[ZC.tok], out=zc,
                                in0=Z[:, part, :, tap:tap + L], scalar=wsh_c[:, tap, cc:cc + 1], in1=zc,
                                op0=ALU.mult, op1=ALU.add)
                self.op("dve", "tensor_tensor", reads=[ZC.tok], writes=[ZC.tok], out=ZC[:, 2, :], in0=ZC[:, 2, :],
                        in1=ZC[:, 1, :], op=ALU.mult)
                self.op("act", "activation", reads=[ZC.tok], writes=[VB.tok], out=VB[:], in_=ZC[:, 2, :], func=AF.Identity)
                b, tk = self.bankT()
                for tt in range(NT):
                    self.op("pe", "transpose", reads=[VB.tok, self.identb.tok], writes=[tk], acc=(tt > 0),
                            out=self.psT[:, b, tt * 128:(tt + 1) * 128], in_=VB[:, tt * 128:(tt + 1) * 128],
                            identity=self.identb[:])
                self.op("dve", "tensor_copy", reads=[tk], writes=[VTM.tok], out=VTM[:],
                        in_=self.psT[:, b, :].rearrange("p (a c) -> p a c", a=NT))
                for sq in range(nseq):
                    bre, tkre = self.bankA(2 if TC > 4 else 1)
                    bim, tkim = self.bankA(2 if TC > 4 else 1)
                    for (MAT, b0, tks) in ((CM, bre, tkre), (SM, bim, tkim)):
                        for fc in range(TC):
                            o = self.psA[:, b0 + fc // 4, (fc % 4) * 128:(fc % 4 + 1) * 128]
                            for tc in range(TC):
                                self.op("pe", "matmul", reads=[MAT.tok, VTM.tok], writes=tks, acc=not (fc == 0 and tc == 0),
                                        out=o, lhsT=MAT[:, tc, fc * 128:(fc + 1) * 128], rhs=VTM[:, sq * TC + tc, :],
                                        start=(tc == 0), stop=(tc == TC - 1))
                    bny, tkny = self.bankA()
                    for tc in range(TC):
                        self.op("pe", "matmul", reads=[altcol.tok, VTM.tok], writes=tkny, acc=(tc > 0),
                                out=self.psA[0:1, bny, 0:128], lhsT=altcol[:, 0:1], rhs=VTM[:, sq * TC + tc, :],
                                start=(tc == 0), stop=(tc == TC - 1))
                    self.op("dve", "tensor_tensor", reads=tkny + [KNY.tok], writes=[YNY.tok], out=YNY[0:1, sq, :],
                            in0=self.psA[0:1, bny, 0:128], in1=KNY[:], op=ALU.mult)
                    nb = 2 if TC > 4 else 1
                    ure = self.psA[:, bre:bre + nb, 0:(512 if TC >= 4 else TC * 128)]
                    uim = self.psA[:, bim:bim + nb, 0:(512 if TC >= 4 else TC * 128)]
                    if nb == 1:
                        ure = self.psA[:, bre, 0:TC * 128]
                        uim = self.psA[:, bim, 0:TC * 128]
                        shp = lambda x: x.rearrange("p a c -> p (a c)")
                    else:
                        shp = lambda x: x.rearrange("p (b a) c -> p b (a c)", b=2)
                    fs = slice(sq * TC, (sq + 1) * TC)
                    self.op("dve", "tensor_tensor", reads=tkre + [KRE.tok], writes=[TA.tok], out=shp(TA[:, 0:TC, :]),
                            in0=ure, in1=shp(KRE[:]), op=ALU.mult)
                    self.op("dve", "tensor_tensor", reads=tkim + [KIM.tok], writes=[TB_.tok], out=shp(TB_[:, 0:TC, :]),
                            in0=uim, in1=shp(KIM[:]), op=ALU.mult)
                    self.op("dve", "tensor_tensor", reads=[TA.tok, TB_.tok], writes=[YRE.tok], out=YRE[:, fs, :],
                            in0=TA[:, 0:TC, :], in1=TB_[:, 0:TC, :], op=ALU.add)
                    self.op("dve", "tensor_tensor", reads=tkim + [KRE.tok], writes=[TA.tok], out=shp(TA[:, 0:TC, :]),
                            in0=uim, in1=shp(KRE[:]), op=ALU.mult)
                    self.op("dve", "tensor_tensor", reads=tkre + [KIM.tok], writes=[TB_.tok], out=shp(TB_[:, 0:TC, :]),
                            in0=ure, in1=shp(KIM[:]), op=ALU.mult)
                    self.op("dve", "tensor_tensor", reads=[TA.tok, TB_.tok], writes=[YN.tok], out=YN[:, fs, :],
                            in0=TA[:, 0:TC, :], in1=TB_[:, 0:TC, :], op=ALU.subtract)
                nblk = 2 if L >= 512 else 1
                for tb in range(2):
                    b, tk = self.bankA()
                    if L >= 512:
                        segs = [(0, tb * 512, 512, 0)]
                    else:
                        segs = [(tb * 2 + k, 0, L, k * L) for k in range(2)]
                    for (sq, t0, n, c0) in segs:
                        o = self.psA[:, b, c0:c0 + n]
                        for fc in range(TC):
                            self.op("pe", "matmul", reads=[YRE.tok, CM.tok], writes=tk, acc=not (fc == 0 and c0 == 0),
                                    out=o, lhsT=YRE[:, sq * TC + fc, :], rhs=CM[:, fc, t0:t0 + n], start=(fc == 0), stop=False)
                            self.op("pe", "matmul", reads=[YN.tok, SM.tok], writes=tk, acc=True,
                                    out=o, lhsT=YN[:, sq * TC + fc, :], rhs=SM[:, fc, t0:t0 + n], start=False, stop=False)
                        self.op("pe", "matmul", reads=[YNY.tok, altrow.tok], writes=tk, acc=True,
                                out=o, lhsT=YNY[0:1, sq, :], rhs=altrow[0:1, t0:t0 + n], start=False, stop=True)
                    ts_ = slice(tb * 512, (tb + 1) * 512)
                    self.op("dve", "scalar_tensor_tensor", reads=tk + [ZC.tok, skip_c.tok], writes=[TY.tok], out=TY[:, ts_],
                            in0=ZC[:, 2, ts_], scalar=skip_c[:, c:c + 1], in1=self.psA[:, b, :], op0=ALU.mult, op1=ALU.add)
                    self.op("dve", "tensor_tensor", reads=[TY.tok, ZC.tok], writes=[MT.toks[c]], out=MT[:, c, ts_],
                            in0=TY[:, ts_], in1=ZC[:, 0, ts_], op=ALU.mult)
            P.barrier()
            s2.close()
            with ExitStack() as s3:
                bout = self.sb(s3, "bout", [128, D])
                P.dma("sp", bout[:], hy["b_out"][j:j + 1, :].to_broadcast([128, D]), writes=[bout.tok])
                wrot2 = Rot([self.sb(s3, "wo", [128, KC, 512], BF16) for _ in range(2)])
                self.out_proj(s3, MT, hy["w_out"][j][:, :], wrot2, bias_tab=bout)
                P.barrier()

    def stage_attn(self, i, g):
        P = self.P
        j = i // 3
        at = self.at
        HT = self.HT
        cst = self.cst
        lam_init = 0.8 - 0.6 * math.exp(-0.3 * i)
        sc = 128.0 ** -0.5
        NKT = 12 if g == 0 else 8
        koff = 512 if g == 0 else 0
        voff = 4 if g == 0 else 0
        with ExitStack() as s:
            OFM = self.sb(s, "OFM", [128, KC, T], BF16, ntok=KC)
            s2 = ExitStack()
            lamb = self.sb(s2, "lamb", [128, 512])
            P.dma("sp", lamb[:], at["lam"][0:1, :].to_broadcast([128, 512]), writes=[lamb.tok])
            lt = self.sb(s2, "lt", [128, 256])
            ljunk = self.sb(s2, "ljunk", [128, 128])
            lcol = self.sb(s2, "lcol", [128, 4])
            for k in range(2):
                self.op("dve", "tensor_tensor", reads=[lamb.tok], writes=[lt.tok], out=lt[:, k * 128:(k + 1) * 128],
                        in0=lamb[:, k * 256:k * 256 + 128], in1=lamb[:, k * 256 + 128:k * 256 + 256], op=ALU.mult)
            for k in range(2):
                self.op("act", "activation", reads=[lt.tok], writes=[ljunk.tok, lcol.tok], out=ljunk[:],
                        in_=lt[:, k * 128:(k + 1) * 128], func=AF.Identity, accum_out=lcol[:, k:k + 1])
            self.op("act", "activation", reads=[lcol.tok], writes=[lcol.tok], out=lcol[:, 0:2], in_=lcol[:, 0:2], func=AF.Exp)
            self.op("dve", "scalar_tensor_tensor", reads=[lcol.tok], writes=[lcol.tok], out=lcol[:, 2:3], in0=lcol[:, 1:2],
                    scalar=-lam_init, in1=lcol[:, 0:1], op0=ALU.add, op1=ALU.subtract)
            gs = self.sb(s2, "gs", [128, 256])
            P.dma("sp", gs[:], at["g_sub"][0:1, :].to_broadcast([128, 256]), writes=[gs.tok])
            self.op("dve", "tensor_scalar", reads=[gs.tok], writes=[gs.tok], out=gs[:], in0=gs[:], scalar1=1.0 - lam_init,
                    scalar2=None, op0=ALU.mult)
            if g == 0:
                cosT = self.sb(s2, "cosT", [128, T])
                sinT = self.sb(s2, "sinT", [128, T])
                rmat = self.sb(s2, "rmat", [128, 128])
                P.dma("sp", cosT[:], cst["cost"], writes=[cosT.tok])
                P.dma("sp", sinT[:], cst["sint"], writes=[sinT.tok])
                P.dma("sp", rmat[:], cst["rmat"], writes=[rmat.tok])
                xfrot = Rot([self.sb(s2, "xf", [128, 512]) for _ in range(2)])
                t1rot = Rot([self.sb(s2, "rt1", [128, 512]) for _ in range(2)])
                t2rot = Rot([self.sb(s2, "rt2", [128, 512]) for _ in range(2)])
                ckrot = Rot([self.sb(s2, "ckb", [128, 4, 256], BF16) for _ in range(2)])
            else:
                strot = Rot([self.sb(s2, "kvst", [128, 256]) for _ in range(4)])
            wrot = Rot([self.sb(s2, "wa", [128, KC, 256], BF16) for _ in range(4)])
            QTrot = Rot([self.sb(s2, "QT", [128, 2, T], BF16) for _ in range(2)])
            KTrot = Rot([self.sb(s2, "KT", [128, 2, koff + T], BF16) for _ in range(2)])
            Vrot = Rot([self.sb(s2, "VA", [128, NKT, 272], BF16) for _ in range(2)])
            for vb in Vrot.bufs:
                self.op("dve", "memset", writes=[vb.tok], ap=vb[:, :, 256:272], constant=1.0)
            erot = Rot([self.sb(s2, "E", [128, 256], BF16) for _ in range(4)])
            rc = self.sb(s2, "rc", [128, 4])
            at1 = self.sb(s2, "at1", [128, 256])
            ao = self.sb(s2, "ao", [128, 256])
            ajunk = self.sb(s2, "ajunk", [128, 256])
            ass = self.sb(s2, "ass", [128, 1])
            ars = self.sb(s2, "ars", [128, 1])
            obrot = Rot([self.sb(s2, "ob", [128, 256], BF16) for _ in range(2)])
            scnt = 0
            for h in range(8):
                wq = self.wtile(wrot, at["w_qkv"][:, h * 256:(h + 1) * 256], 256)
                wk = self.wtile(wrot, at["w_qkv"][:, D + h * 256:D + (h + 1) * 256], 256)
                wv = self.wtile(wrot, at["w_qkv"][:, 2 * D + h * 256:2 * D + (h + 1) * 256], 256)
                QT = QTrot.next()
                KT = KTrot.next()
                VA = Vrot.next()
                self.pa_i = 0
                for (wt, dstb, off) in ((wq, QT, 0), (wk, KT, koff)):
                    for c in range(2):
                        for tb in range(2):
                            b, tk = self.bankA()
                            for kc in range(KC):
                                self.op("pe", "matmul", reads=[HT.toks[kc], wt.tok], writes=tk, acc=(kc > 0),
                                        out=self.psA[:, b, :], lhsT=wt[:, kc, c * 128:(c + 1) * 128],
                                        rhs=HT[:, kc, tb * 512:(tb + 1) * 512], start=(kc == 0), stop=(kc == KC - 1))
                            dst = dstb[:, c, off + tb * 512:off + (tb + 1) * 512]
                            if g == 0:
                                xf = xfrot.next()
                                self.op("act", "activation", reads=tk, writes=[xf.tok], out=xf[:], in_=self.psA[:, b, :],
                                        func=AF.Identity)
                                b2, tk2 = self.bankA()
                                self.op("pe", "matmul", reads=[rmat.tok, xf.tok], writes=tk2, out=self.psA[:, b2, :],
                                        lhsT=rmat[:], rhs=xf[:], start=True, stop=True)
                                t1 = t1rot.next()
                                t2 = t2rot.next()
                                self.op("dve", "tensor_tensor", reads=[xf.tok, cosT.tok], writes=[t1.tok], out=t1[:], in0=xf[:],
                                        in1=cosT[:, tb * 512:(tb + 1) * 512], op=ALU.mult)
                                self.op("dve", "tensor_tensor", reads=tk2 + [sinT.tok], writes=[t2.tok], out=t2[:],
                                        in0=self.psA[:, b2, :], in1=sinT[:, tb * 512:(tb + 1) * 512], op=ALU.mult)
                                self.op("dve", "tensor_tensor", reads=[t1.tok, t2.tok], writes=[dstb.tok], out=dst, in0=t1[:],
                                        in1=t2[:], op=ALU.add)
                            else:
                                self.op("act", "activation", reads=tk, writes=[dstb.tok], out=dst, in_=self.psA[:, b, :],
                                        func=AF.Identity)
                if g == 0:
                    ckb = ckrot.next()
                    P.dma("pool", ckb[:], at["ck"][:, h * 256:(h + 1) * 256].rearrange("(a p) n -> p a n", p=128),
                          writes=[ckb.tok])
                    bT, tkT = self.bankT()
                    for c in range(2):
                        for a in range(4):
                            idx = c * 4 + a
                            self.op("pe", "transpose", reads=[ckb.tok, self.identb.tok], writes=[tkT], acc=(idx > 0),
                                    out=self.psT[:, bT, idx * 128:(idx + 1) * 128], in_=ckb[:, a, c * 128:(c + 1) * 128],
                                    identity=self.identb[:])
                    self.op("dve", "tensor_copy", reads=[tkT], writes=[KT.tok], out=KT[:, :, 0:512],
                            in_=self.psT[:, bT, :].rearrange("p (c t) -> p c t", c=2))
                    P.dma("pool", VA[:, 0:4, 0:256], at["cv"][:, h * 256:(h + 1) * 256].rearrange("(a p) n -> p a n", p=128),
                          writes=[VA.tok])
                for tt in range(NT):
                    rows = slice(tt * 128, (tt + 1) * 128)
                    b, tk = self.bankA()
                    for kc in range(KC):
                        self.op("pe", "matmul", reads=[HT.toks[kc], wv.tok], writes=tk, acc=(kc > 0),
                                out=self.psA[:, b, 0:256], lhsT=HT[:, kc, rows], rhs=wv[:, kc, 0:256],
                                start=(kc == 0), stop=(kc == KC - 1))
                    if g == 0:
                        self.op("dve", "tensor_copy", reads=tk, writes=[VA.tok], out=VA[:, voff + tt, 0:256],
                                in_=self.psA[:, b, 0:256])
                    else:
                        vst = strot.next()
                        self.op("act", "activation", reads=tk, writes=[vst.tok], out=vst[:], in_=self.psA[:, b, 0:256],
                                func=AF.Identity)
                        self.op("dve", "tensor_copy", reads=[vst.tok], writes=[VA.tok], out=VA[:, voff + tt, 0:256], in_=vst[:])
                        P.dma("sp", self.nv[rows, h * 256:(h + 1) * 256], vst[:], reads=[vst.tok], writes=[Tok()])
                        b, tk = self.bankA()
                        for kc in range(KC):
                            self.op("pe", "matmul", reads=[HT.toks[kc], wk.tok], writes=tk, acc=(kc > 0),
                                    out=self.psA[:, b, 0:256], lhsT=HT[:, kc, rows], rhs=wk[:, kc, 0:256],
                                    start=(kc == 0), stop=(kc == KC - 1))
                        kst = strot.next()
                        self.op("act", "activation", reads=tk, writes=[kst.tok], out=kst[:], in_=self.psA[:, b, 0:256],
                                func=AF.Identity)
                        P.dma("sp", self.nk[rows, h * 256:(h + 1) * 256], kst[:], reads=[kst.tok], writes=[Tok()])
                for blk in range(4):
                    q0 = blk * 256
                    chunks = list(range(12)) if g == 0 else [blk * 2, blk * 2 + 1]
                    for c in range(2):
                        for idx, kt in enumerate(chunks):
                            sbk = 4 + (scnt % 2)
                            scnt += 1
                            self.op("pe", "matmul", reads=[KT.tok, QT.tok], writes=[self.pa_tok[sbk]],
                                    out=self.psA[:, sbk, 0:256], lhsT=KT[:, c, kt * 128:(kt + 1) * 128],
                                    rhs=QT[:, c, q0:q0 + 256], start=True, stop=True)
                            E = erot.next()
                            self.op("act", "activation", reads=[self.pa_tok[sbk]], writes=[E.tok], out=E[:],
                                    in_=self.psA[:, sbk, 0:256], func=AF.Exp, scale=sc)
                            for qt in range(2):
                                pb = c * 2 + qt
                                self.op("pe", "matmul", reads=[E.tok, VA.tok], writes=[self.pa_tok[pb]], acc=(idx > 0),
                                        out=self.psA[:, pb, 0:258], lhsT=E[:, qt * 128:(qt + 1) * 128], rhs=VA[:, kt, 0:258],
                                        start=(idx == 0), stop=(idx == len(chunks) - 1))
                    for qt in range(2):
                        p0, p1 = self.pa_tok[qt], self.pa_tok[2 + qt]
                        self.op("dve", "reciprocal", reads=[p0], writes=[rc.tok], out=rc[:, 0:1], in_=self.psA[:, qt, 256:257])
                        self.op("dve", "reciprocal", reads=[p1], writes=[rc.tok], out=rc[:, 1:2], in_=self.psA[:, 2 + qt, 256:257])
                        self.op("dve", "tensor_tensor", reads=[rc.tok, lcol.tok], writes=[rc.tok], out=rc[:, 2:3], in0=rc[:, 1:2],
                                in1=lcol[:, 2:3], op=ALU.mult)
                        self.op("dve", "tensor_scalar", reads=[p1, rc.tok], writes=[at1.tok], out=at1[:],
                                in0=self.psA[:, 2 + qt, 0:256], scalar1=rc[:, 2:3], scalar2=None, op0=ALU.mult)
                        self.op("dve", "scalar_tensor_tensor", reads=[p0, rc.tok, at1.tok], writes=[ao.tok], out=ao[:],
                                in0=self.psA[:, qt, 0:256], scalar=rc[:, 0:1], in1=at1[:], op0=ALU.mult, op1=ALU.add)
                        self.rstd_of(ao[:], [ao.tok], ajunk, ass, ars, 256)
                        ob = obrot.next()
                        self.op("dve", "scalar_tensor_tensor", reads=[ao.tok, ars.tok, gs.tok], writes=[ob.tok], out=ob[:],
                                in0=ao[:], scalar=ars[:, 0:1], in1=gs[:], op0=ALU.mult, op1=ALU.mult)
                        bT, tkT = self.bankT()
                        for jj in range(2):
                            self.op("pe", "transpose", reads=[ob.tok, self.identb.tok], writes=[tkT], acc=(jj > 0),
                                    out=self.psT[:, bT, jj * 128:(jj + 1) * 128], in_=ob[:, jj * 128:(jj + 1) * 128],
                                    identity=self.identb[:])
                        self.op("act", "activation", reads=[tkT], writes=OFM.toks[2 * h:2 * h + 2],
                                out=OFM[:, 2 * h:2 * h + 2, q0 + qt * 128:q0 + (qt + 1) * 128],
                                in_=self.psT[:, bT, 0:256].rearrange("p (j t) -> p j t", j=2), func=AF.Identity)
            self.pa_i = 0
            P.barrier()
            s2.close()
            with ExitStack() as s3:
                wrot2 = Rot([self.sb(s3, "wo", [128, KC, 512], BF16) for _ in range(2)])
                self.out_proj(s3, OFM, at["w_o"], wrot2)
                P.barrier()

    def sin_turns(self, u, ni, out_ap, out_tok, view=lambda a: a[:]):
        self.op("dve", "tensor_copy", reads=[u.tok], writes=[ni.tok], out=view(ni.t), in_=view(u.t))
        self.op("dve", "scalar_tensor_tensor", reads=[ni.tok, u.tok], writes=[u.tok], out=view(u.t), in0=view(ni.t),
                scalar=-1.0, in1=view(u.t), op0=ALU.mult, op1=ALU.add)
        self.op("act", "activation", reads=[u.tok], writes=[out_tok], out=out_ap, in_=view(u.t), func=AF.Sin, scale=6.28318)

    def stage_s5(self, i, g):
        P = self.P
        j = i // 3
        s5 = self.s5
        nseq, L = group_info(g)
        HT = self.HT
        cst = self.cst
        with ExitStack() as s:
            YT = self.sb(s, "YT", [128, KC, T], BF16, ntok=KC)
            s2 = ExitStack()
            W = 128

            def small(nm, dt=F32):
                return self.sb(s2, nm, [128, W], dt)

            def tt(out, a, b, op):
                self.op("dve", "tensor_tensor", reads=[a.tok, b.tok], writes=[out.tok], out=out[:], in0=a[:], in1=b[:], op=op)

            lre, lim, dtt, rr, phi, cth, sth = [small(n) for n in ("lre", "lim", "dtt", "rr", "phi", "cth", "sth")]
            ua, ub, nr, nim, den, cfr, cfi, ncfi = [small(n) for n in ("ua", "ub", "nr", "nim", "den", "cfr", "cfi", "ncfi")]
            nis = small("nis", I32)
            for di in range(2):
                cs = slice(di * 64, (di + 1) * 64)
                P.dma("sp", lre[:, cs], s5["lam_re"][di * 128:(di + 1) * 128, :], writes=[lre.tok])
                P.dma("sp", lim[:, cs], s5["lam_im"][di * 128:(di + 1) * 128, :], writes=[lim.tok])
                P.dma("sp", dtt[:, cs], s5["log_dt"][di * 128:(di + 1) * 128, :], writes=[dtt.tok])
            self.op("act", "activation", reads=[dtt.tok], writes=[dtt.tok], out=dtt[:], in_=dtt[:], func=AF.Exp)
            self.op("dve", "tensor_scalar", reads=[lre.tok], writes=[lre.tok], out=lre[:], in0=lre[:], scalar1=-1e-4, scalar2=None,
                    op0=ALU.min)
            tt(rr, lre, dtt, ALU.mult)
            self.op("act", "activation", reads=[rr.tok], writes=[rr.tok], out=rr[:], in_=rr[:], func=AF.Exp)
            tt(phi, lim, dtt, ALU.mult)
            self.op("dve", "tensor_scalar", reads=[phi.tok], writes=[phi.tok], out=phi[:], in0=phi[:], scalar1=1.0 / TWO_PI,
                    scalar2=None, op0=ALU.mult)
            self.op("dve", "tensor_scalar", reads=[phi.tok], writes=[ua.tok], out=ua[:], in0=phi[:], scalar1=0.25, scalar2=None,
                    op0=ALU.add)
            self.sin_turns(ua, nis, cth[:], cth.tok)
            self.op("dve", "tensor_copy", reads=[phi.tok], writes=[ub.tok], out=ub[:], in_=phi[:])
            self.sin_turns(ub, nis, sth[:], sth.tok)
            tt(nr, rr, cth, ALU.mult)
            self.op("dve", "tensor_scalar", reads=[nr.tok], writes=[nr.tok], out=nr[:], in0=nr[:], scalar1=-1.0, scalar2=None,
                    op0=ALU.add)
            tt(nim, rr, sth, ALU.mult)
            tt(den, lre, lre, ALU.mult)
            tt(ua, lim, lim, ALU.mult)
            tt(den, den, ua, ALU.add)
            self.op("dve", "reciprocal", reads=[den.tok], writes=[den.tok], out=den[:], in_=den[:])
            tt(ua, nr, lre, ALU.mult)
            tt(ub, nim, lim, ALU.mult)
            tt(ua, ua, ub, ALU.add)
            tt(cfr, ua, den, ALU.mult)
            tt(ua, nim, lre, ALU.mult)
            tt(ub, nr, lim, ALU.mult)
            tt(ua, ua, ub, ALU.subtract)
            tt(cfi, ua, den, ALU.mult)
            self.op("dve", "tensor_scalar", reads=[cfi.tok], writes=[ncfi.tok], out=ncfi[:], in0=cfi[:], scalar1=-1.0, scalar2=None,
                    op0=ALU.mult)
            if g == 0:
                s0r, s0i, zir, zii = [small(n) for n in ("s0r", "s0i", "zir", "zii")]
                for di in range(2):
                    cs = slice(di * 64, (di + 1) * 64)
                    P.dma("sp", s0r[:, cs], s5["s0re"][di * 128:(di + 1) * 128, :], writes=[s0r.tok])
                    P.dma("sp", s0i[:, cs], s5["s0im"][di * 128:(di + 1) * 128, :], writes=[s0i.tok])
                tt(den, cfr, cfr, ALU.mult)
                tt(ua, cfi, cfi, ALU.mult)
                tt(den, den, ua, ALU.add)
                self.op("dve", "reciprocal", reads=[den.tok], writes=[den.tok], out=den[:], in_=den[:])
                tt(ua, s0r, cfr, ALU.mult)
                tt(ub, s0i, cfi, ALU.mult)
                tt(ua, ua, ub, ALU.add)
                tt(nr, ua, den, ALU.mult)
                tt(ua, s0i, cfr, ALU.mult)
                tt(ub, s0r, cfi, ALU.mult)
                tt(ua, ua, ub, ALU.subtract)
                tt(nim, ua, den, ALU.mult)
                tt(ua, nr, cth, ALU.mult)
                tt(ub, nim, sth, ALU.mult)
                tt(zir, ua, ub, ALU.subtract)
                tt(ua, nr, sth, ALU.mult)
                tt(ub, nim, cth, ALU.mult)
                tt(zii, ua, ub, ALU.add)
            else:
                FINr = self.sb(s2, "FINr", [128, 4, 128])
                FINi = self.sb(s2, "FINi", [128, 4, 128])
            d_c = self.sb(s2, "d_c", [128, 16])
            self.load_cols(s2, d_c[:, :], s5["d"][:, :], 16, d_c.tok)
            iot = []
            for di in range(2):
                it = self.sb(s2, "iot", [128, T])
                P.dma("sp", it[:], cst[f"iota{L}{'fb'[di]}"][0:1, :].to_broadcast([128, T]), writes=[it.tok])
                iot.append(it)
            qtr = self.sb(s2, "qtr", [128, 1])
            self.op("dve", "memset", writes=[qtr.tok], ap=qtr[:], constant=0.25)
            ucrot = Rot([self.sb(s2, "uc", [128, T]) for _ in range(2)])
            usrot = Rot([self.sb(s2, "us", [128, T]) for _ in range(2)])
            ncrot = Rot([self.sb(s2, "ncb", [128, T], I32) for _ in range(2)])
            nsrot = Rot([self.sb(s2, "nsb", [128, T], I32) for _ in range(2)])
            cosrot = Rot([self.sb(s2, "cosT", [128, T]) for _ in range(2)])
            sinrot = Rot([self.sb(s2, "sinT", [128, T]) for _ in range(2)])
            m1, m2, m3, m4 = [self.sb(s2, n, [128, T]) for n in ("m1", "m2", "m3", "m4")]
            zr, zi = [self.sb(s2, n, [128, T]) for n in ("zr", "zi")]
            bmr, bmi = m1, m3
            p1rot, p2rot, p3rot, p4rot = [Rot([self.sb(s2, n, [128, T], BF16) for _ in range(2)]) for n in ("p1", "p2", "p3", "p4")]
            f1 = self.sb(s2, "f1", [128, 4])
            f2 = self.sb(s2, "f2", [128, 4])
            bzrot = Rot([self.sb(s2, "bz", [128, 2, 4, 128], BF16) for _ in range(2)])
            czrot = Rot([self.sb(s2, "cz", [128, 2, 4, 128]) for _ in range(2)])
            cbrot = Rot([self.sb(s2, "cb", [128, 2, 4, 128], BF16) for _ in range(2)])
            ctmp = self.sb(s2, "ctmp", [128, 128])
            ytmp = self.sb(s2, "ytmp", [128, 512])
            v2 = lambda a: a.rearrange("p (b t) -> p b t", b=2)
            tiles = [(kc, di, sl) for kc in range(KC) for di in range(2) for sl in range(4)]
            NTL = len(tiles)
            pre_b, tab_b = {}, {}

            def emit_pre(k):
                kc_, di_, sl_ = tiles[k]
                col_ = di_ * 64 + kc_ * 4 + sl_
                bufs = (ucrot.next(), ncrot.next(), usrot.next(), nsrot.next())
                for o, withq in zip(bufs, (True, True, False, False)):
                    kw = {"bias": qtr[:, 0:1]} if withq else {}
                    self.op("act", "activation", reads=[iot[di_].tok, phi.tok, qtr.tok], writes=[o.tok], out=o[:],
                            in_=iot[di_][:], func=AF.Identity, scale=phi[:, col_:col_ + 1], **kw)
                pre_b[k] = bufs

            def emit_tab(k):
                uc, ncb, us, nsb = pre_b.pop(k)
                cosT_, sinT_ = cosrot.next(), sinrot.next()
                for u_, n_, o_ in ((uc, ncb, cosT_), (us, nsb, sinT_)):
                    self.op("dve", "scalar_tensor_tensor", reads=[n_.tok, u_.tok], writes=[u_.tok], out=u_[:], in0=n_[:],
                            scalar=-1.0, in1=u_[:], op0=ALU.mult, op1=ALU.add)
                    self.op("act", "activation", reads=[u_.tok], writes=[o_.tok], out=o_[:], in_=u_[:], func=AF.Sin,
                            scale=6.28318)
                tab_b[k] = (cosT_, sinT_)

            emit_pre(0)
            emit_tab(0)
            emit_pre(1)
            bz = cz = cb = None
            for k, (kc, di, sl) in enumerate(tiles):
                if k + 1 < NTL:
                    emit_tab(k + 1)
                if k + 2 < NTL:
                    emit_pre(k + 2)
                if sl == 0:
                    bz = bzrot.next()
                    cz = czrot.next()
                    cb = cbrot.next()
                    r0 = (di * 16 + kc) * 512
                    for comp, nm in enumerate(("bz_re", "bz_im")):
                        P.dma("pool", bz[:, comp, :, :], s5[nm][r0:r0 + 512, :].rearrange("(s r) c -> r s c", r=128), writes=[bz.tok])
                    for comp, nm in enumerate(("cz_re", "cz_im")):
                        P.dma("sp", cz[:, comp, :, :], s5[nm][r0:r0 + 512, :].rearrange("(s r) c -> r s c", r=128), writes=[cz.tok])
                    for sl2 in range(4):
                        col2 = di * 64 + kc * 4 + sl2
                        c2 = slice(col2, col2 + 1)
                        self.op("dve", "tensor_scalar", reads=[cz.tok, cfi.tok], writes=[ctmp.tok], out=ctmp[:], in0=cz[:, 1, sl2, :],
                                scalar1=cfi[:, c2], scalar2=None, op0=ALU.mult)
                        self.op("dve", "scalar_tensor_tensor", reads=[cz.tok, cfr.tok, ctmp.tok], writes=[cb.tok], out=cb[:, 0, sl2, :],
                                in0=cz[:, 0, sl2, :], scalar=cfr[:, c2], in1=ctmp[:], op0=ALU.mult, op1=ALU.subtract)
                        self.op("dve", "tensor_scalar", reads=[cz.tok, cfr.tok], writes=[ctmp.tok], out=ctmp[:], in0=cz[:, 1, sl2, :],
                                scalar1=cfr[:, c2], scalar2=None, op0=ALU.mult)
                        self.op("dve", "scalar_tensor_tensor", reads=[cz.tok, ncfi.tok, ctmp.tok], writes=[cb.tok], out=cb[:, 1, sl2, :],
                                in0=cz[:, 0, sl2, :], scalar=ncfi[:, c2], in1=ctmp[:], op0=ALU.mult, op1=ALU.subtract)
                col = di * 64 + kc * 4 + sl
                cc = slice(col, col + 1)
                cosT, sinT = tab_b.pop(k)
                for comp in range(2):
                    for tb in range(2):
                        bk = 2 + comp * 2 + tb
                        self.op("pe", "matmul", reads=[bz.tok, HT.toks[kc]], writes=[self.pa_tok[bk]],
                                out=self.psA[:, bk, :], lhsT=bz[:, comp, sl, :], rhs=HT[:, kc, tb * 512:(tb + 1) * 512],
                                start=True, stop=True)
                pre, pim = self.psA[:, 2:4, :], self.psA[:, 4:6, :]
                tre, tim = self.pa_tok[2:4], self.pa_tok[4:6]
                self.op("dve", "tensor_tensor", reads=tre + [cosT.tok], writes=[m1.tok], out=v2(m1[:]), in0=pre, in1=v2(cosT[:]), op=ALU.mult)
                self.op("dve", "tensor_tensor", reads=tim + [sinT.tok], writes=[m2.tok], out=v2(m2[:]), in0=pim, in1=v2(sinT[:]), op=ALU.mult)
                self.op("dve", "tensor_tensor", reads=tim + [cosT.tok], writes=[m3.tok], out=v2(m3[:]), in0=pim, in1=v2(cosT[:]), op=ALU.mult)
                self.op("dve", "tensor_tensor", reads=tre + [sinT.tok], writes=[m4.tok], out=v2(m4[:]), in0=pre, in1=v2(sinT[:]), op=ALU.mult)
                tt(bmr, m1, m2, ALU.add)
                tt(bmi, m3, m4, ALU.subtract)
                for sq in range(nseq):
                    def vw(a):
                        x = a[:, sq * L:(sq + 1) * L]
                        return x[:, ::-1] if di == 1 else x
                    for (zz, bb, zin) in ((zr, bmr, "zir"), (zi, bmi, "zii")):
                        if g == 0:
                            zb = zir if zin == "zir" else zii
                            init, rd = zb[:, cc], [zb.tok]
                        else:
                            init, rd = 0.0, []
                        self.op("dve", "tensor_tensor_scan", reads=[bb.tok, rr.tok] + rd, writes=[zz.tok], out=vw(zz.t),
                                data0=rr[:, cc].to_broadcast([128, L]), data1=vw(bb.t), initial=init,
                                op0=ALU.mult, op1=ALU.add)
                p1, p2, p3, p4 = p1rot.next(), p2rot.next(), p3rot.next(), p4rot.next()
                tt(p1, zr, cosT, ALU.mult)
                tt(p4, zr, sinT, ALU.mult)
                self.op("dve", "scalar_tensor_tensor", reads=[zi.tok, sinT.tok], writes=[p2.tok], out=p2[:], in0=zi[:], scalar=-1.0,
                        in1=sinT[:], op0=ALU.mult, op1=ALU.mult)
                tt(p3, zi, cosT, ALU.mult)
                if g == 1:
                    c0 = (L - 1) if di == 0 else 0
                    fcol = di * 64 + kc * 4 + sl
                    pick = lambda a: a[:, c0:T:L]
                    for (fo_, za, ta, zb_, tb_, op_) in ((FINr, zr, cosT, zi, sinT, ALU.subtract), (FINi, zi, cosT, zr, sinT, ALU.add)):
                        self.op("dve", "tensor_tensor", reads=[za.tok, ta.tok], writes=[f1.tok], out=f1[:], in0=pick(za.t), in1=pick(ta.t), op=ALU.mult)
                        self.op("dve", "tensor_tensor", reads=[zb_.tok, tb_.tok], writes=[f2.tok], out=f2[:], in0=pick(zb_.t), in1=pick(tb_.t), op=ALU.mult)
                        self.op("dve", "tensor_tensor", reads=[f1.tok, f2.tok], writes=[fo_.tok], out=fo_[:, :, fcol], in0=f1[:], in1=f2[:], op=op_)
                for tb in range(2):
                    first = (di == 0 and sl == 0)
                    last = (di == 1 and sl == 3)
                    tsl = slice(tb * 512, (tb + 1) * 512)
                    terms = ((0, p1), (0, p2), (1, p3), (1, p4))
                    for ti, (ci, pp) in enumerate(terms):
                        self.op("pe", "matmul", reads=[cb.tok, pp.tok], writes=[self.pa_tok[tb]], acc=not (first and ti == 0),
                                out=self.psA[:, tb, :], lhsT=cb[:, ci, sl, :], rhs=pp[:, tsl],
                                start=(first and ti == 0), stop=(last and ti == 3))
                if di == 1 and sl == 3:
                    for tb in range(2):
                        ts_ = slice(tb * 512, (tb + 1) * 512)
                        self.op("dve", "scalar_tensor_tensor", reads=[HT.toks[kc], d_c.tok, self.pa_tok[tb]], writes=[ytmp.tok], out=ytmp[:],
                                in0=HT[:, kc, ts_], scalar=d_c[:, kc:kc + 1], in1=self.psA[:, tb, :], op0=ALU.mult, op1=ALU.add)
                        self.op("act", "activation", reads=[ytmp.tok], writes=[YT.toks[kc]], out=YT[:, kc, ts_], in_=ytmp[:],
                                func=AF.Gelu_apprx_tanh)
            if g == 1:
                bc = lambda a: a[:, :].unsqueeze(1).to_broadcast([128, 4, 128])
                fa = self.sb(s2, "fa", [128, 4, 128])
                fb = self.sb(s2, "fb", [128, 4, 128])
                fo = [self.sb(s2, "fo", [128, 4, 128]) for _ in range(2)]
                self.op("dve", "tensor_tensor", reads=[FINr.tok, cfr.tok], writes=[fa.tok], out=fa[:], in0=FINr[:], in1=bc(cfr), op=ALU.mult)
                self.op("dve", "tensor_tensor", reads=[FINi.tok, cfi.tok], writes=[fb.tok], out=fb[:], in0=FINi[:], in1=bc(cfi), op=ALU.mult)
                self.op("dve", "tensor_tensor", reads=[fa.tok, fb.tok], writes=[fo[0].tok], out=fo[0][:], in0=fa[:], in1=fb[:], op=ALU.subtract)
                self.op("dve", "tensor_tensor", reads=[FINr.tok, cfi.tok], writes=[fa.tok], out=fa[:], in0=FINr[:], in1=bc(cfi), op=ALU.mult)
                self.op("dve", "tensor_tensor", reads=[FINi.tok, cfr.tok], writes=[fb.tok], out=fb[:], in0=FINi[:], in1=bc(cfr), op=ALU.mult)
                self.op("dve", "tensor_tensor", reads=[fa.tok, fb.tok], writes=[fo[1].tok], out=fo[1][:], in0=fa[:], in1=fb[:], op=ALU.add)
                fst = Rot([self.sb(s2, "fst", [128, 128]) for _ in range(2)])
                self.pa_i = 2
                for comp, dst in enumerate((self.nsre, self.nsim)):
                    for b_ in range(4):
                        bk, tk = self.bankA()
                        self.op("pe", "transpose", reads=[fo[comp].tok, self.identf.tok], writes=tk, out=self.psA[:, bk, 0:128],
                                in_=fo[comp][:, b_, :], identity=self.identf[:])
                        ft = fst.next()
                        self.op("dve", "tensor_copy", reads=tk, writes=[ft.tok], out=ft[:], in_=self.psA[:, bk, 0:128])
                        P.dma("sp", dst[b_ * 128:(b_ + 1) * 128, :], ft[:], reads=[ft.tok], writes=[Tok()])
            self.pa_i = 0
            P.barrier()
            s2.close()
            with ExitStack() as s3:
                bg = self.sb(s3, "bglu", [128, 2 * D])
                P.dma("sp", bg[:], s5["b_glu"][0:1, :].to_broadcast([128, 2 * D]), writes=[bg.tok])
                wrot = Rot([self.sb(s3, "wg", [128, KC, 512], BF16) for _ in range(4)])
                arot = Rot([self.sb(s3, "ga", [128, 512]) for _ in range(2)])
                grot = Rot([self.sb(s3, "gg", [128, 512]) for _ in range(2)])
                orot = Rot([self.sb(s3, "go", [128, 512]) for _ in range(3)])
                for cbk in range(4):
                    wa = self.wtile(wrot, s5["w_glu"][:, cbk * 512:(cbk + 1) * 512], 512)
                    wg = self.wtile(wrot, s5["w_glu"][:, D + cbk * 512:D + (cbk + 1) * 512], 512)
                    for tt_ in range(NT):
                        rows = slice(tt_ * 128, (tt_ + 1) * 128)
                        res = []
                        for wt in (wa, wg):
                            bk, tk = self.bankA()
                            for kc in range(KC):
                                self.op("pe", "matmul", reads=[YT.toks[kc], wt.tok], writes=tk, acc=(kc > 0),
                                        out=self.psA[:, bk, :], lhsT=YT[:, kc, rows], rhs=wt[:, kc, 0:512],
                                        start=(kc == 0), stop=(kc == KC - 1))
                            res.append((bk, tk))
                        ga = arot.next()
                        gg = grot.next()
                        self.op("dve", "tensor_tensor", reads=res[0][1] + [bg.tok], writes=[ga.tok], out=ga[:],
                                in0=self.psA[:, res[0][0], :], in1=bg[:, cbk * 512:(cbk + 1) * 512], op=ALU.add)
                        self.op("dve", "tensor_tensor", reads=res[1][1] + [bg.tok], writes=[gg.tok], out=gg[:],
                                in0=self.psA[:, res[1][0], :], in1=bg[:, D + cbk * 512:D + (cbk + 1) * 512], op=ALU.add)
                        self.op("act", "activation", reads=[gg.tok], writes=[gg.tok], out=gg[:], in_=gg[:], func=AF.Sigmoid)
                        oe = orot.next()
                        self.op("dve", "tensor_tensor", reads=[ga.tok, gg.tok], writes=[oe.tok], out=oe[:], in0=ga[:], in1=gg[:],
                                op=ALU.mult)
                        P.dma("sp", self.osc[rows, cbk * 512:(cbk + 1) * 512], oe[:], reads=[oe.tok], writes=[self.otok[tt_]])
                P.barrier()


_CACHE = {}


def _get_nc(layers):
    key = tuple(layers)
    if key not in _CACHE:
        b = Builder(list(layers))
        nc = b.build()
        _CACHE[key] = (nc, b)
    return _CACHE[key]


def kernel(**inp):
    layers = CFG["layers"]
    ncores = CFG["ncores"]
    nc, b = _get_nc(layers)
    consts = make_consts()
    f = lambda a: np.ascontiguousarray(np.asarray(a, dtype=np.float32))
    shared = {
        "b_mod": f(inp["b_mod"]),
        "g_norm": f(inp["g_norm"]).reshape(16, D),
        "hy_b_in": f(inp["hy_b_in"]).reshape(96, 128),
        "hy_w_short": f(inp["hy_w_short"]).reshape(6 * 48, 128), "hy_b_short": f(inp["hy_b_short"]).reshape(96, 128),
        "hy_f_w1": f(inp["hy_f_w1"]).reshape(64, 64), "hy_f_b1": f(inp["hy_f_b1"]),
        "hy_f_freq1": f(inp["hy_f_freq1"]), "hy_f_w2": f(inp["hy_f_w2"]).reshape(128, 64),
        "hy_f_b2": f(inp["hy_f_b2"]), "hy_f_freq2": f(inp["hy_f_freq2"]),
        "hy_f_w3": f(inp["hy_f_w3"]).reshape(128, 2 * D), "hy_log_alpha": f(inp["hy_log_alpha"]),
        "hy_skip": f(inp["hy_skip"]).reshape(32, 128),
        "hy_b_out": f(inp["hy_b_out"]),
        "at_w_qkv": f(inp["at_w_qkv"]).reshape(D, 3 * D), "at_lam": f(inp["at_lam"]).reshape(1, 512),
        "at_g_sub": f(inp["at_g_sub"]).reshape(1, 256), "at_w_o": f(inp["at_w_o"]).reshape(D, D),
        "s5_d": f(inp["s5_d"]).reshape(16, 128), "s5_w_glu": f(inp["s5_w_glu"]).reshape(D, 2 * D),
        "s5_b_glu": f(inp["s5_b_glu"]).reshape(1, 2 * D),
    }
    if any(i % 3 == 2 for i in layers):
        def smaj(a):
            return np.ascontiguousarray(f(a).reshape(2, 64, 2, 64).transpose(0, 2, 3, 1).reshape(256, 64))
        shared["s5_lam_re"] = smaj(inp["s5_lam_re"][0])
        shared["s5_lam_im"] = smaj(inp["s5_lam_im"][0])
        ld = f(inp["s5_log_dt"][0]).reshape(2, 64, 2)
        shared["s5_log_dt"] = np.ascontiguousarray(np.broadcast_to(ld.transpose(0, 2, 1)[:, :, None, :], (2, 2, 64, 64)).reshape(256, 64))
        for nm in ("re", "im"):
            b5 = f(inp["s5_b_" + nm][0]).reshape(2, 16, 8, 64, 16)
            c5 = f(inp["s5_c_" + nm][0]).reshape(2, 16, 8, 16, 64)
            BZ = np.zeros((2, 16, 4, 128, 128), np.float32)
            CZ = np.zeros((2, 16, 4, 128, 128), np.float32)
            for sl in range(4):
                for j2 in range(2):
                    gl = 2 * sl + j2
                    BZ[:, :, sl, gl * 16:(gl + 1) * 16, j2 * 64:(j2 + 1) * 64] = b5[:, :, gl].transpose(0, 1, 3, 2)
                    CZ[:, :, sl, j2 * 64:(j2 + 1) * 64, gl * 16:(gl + 1) * 16] = c5[:, :, gl].transpose(0, 1, 3, 2)
            shared["s5_bz_" + nm] = BZ.reshape(16384, 128)
            shared["s5_cz_" + nm] = CZ.reshape(16384, 128)
    for i in layers:
        shared[f"w_mod{i}"] = f(inp["w_mod"][i])
        shared[f"w_mlp_in{i}"] = f(inp["w_mlp_in"][i])
        shared[f"w_mlp_out{i}"] = f(inp["w_mlp_out"][i])
        if i % 3 == 0:
            shared[f"hy_w_in{i // 3}"] = f(inp["hy_w_in"][i // 3])
            shared[f"hy_w_out{i // 3}"] = f(inp["hy_w_out"][i // 3])
    for k, v in consts.items():
        shared["c_" + k] = v
    xp = f(inp["x_prompt"])
    xs = f(inp["x_sample"])
    in_maps = []
    for c in range(ncores):
        m = {
            "xs": xs[c].reshape(T, D), "xp": xp[4 * c:4 * c + 4].reshape(T, D),
            "cvec": np.stack([f(inp["c"])[c], f(inp["c_ctx"])]),
            "ck": f(inp["cache_attn_k"])[c, 0].reshape(512, D), "cv": f(inp["cache_attn_v"])[c, 0].reshape(512, D),
            "s0re": np.ascontiguousarray(f(inp["state_s5_re"])[c, 0].reshape(2, 64, 2, 64).transpose(0, 2, 3, 1).reshape(256, 64)),
            "s0im": np.ascontiguousarray(f(inp["state_s5_im"])[c, 0].reshape(2, 64, 2, 64).transpose(0, 2, 3, 1).reshape(256, 64)),
        }
        m.update(shared)
        in_maps.append({k: m[k] for k in b.in_names})
    res = run_bass_kernel_spmd(nc, in_maps, core_ids=list(range(ncores)))
    R = res.results
    CFG["last"] = R
    nB = 4 * ncores
    y_prompt = np.concatenate([r["yp"].reshape(4, 256, D) for r in R], axis=0)
    y_sample = np.stack([r["ys"].reshape(1024, D) for r in R], axis=0)
    new_k = np.concatenate([r["nk"].reshape(4, 1, 256, 8, 2, 128) for r in R], axis=0)
    new_v = np.concatenate([r["nv"].reshape(4, 1, 256, 8, 256) for r in R], axis=0)
    ns_re = np.concatenate([r["nsre"].reshape(4, 1, 2, 128, 64) for r in R], axis=0)
    ns_im = np.concatenate([r["nsim"].reshape(4, 1, 2, 128, 64) for r in R], axis=0)
    return (y_prompt, y_sample, new_k, new_v, ns_re, ns_im)
```

```python
import math
from contextlib import ExitStack

import numpy as np
import concourse.bass as bass
import concourse.mybir as mybir
from concourse.bass_utils import run_bass_kernel_spmd

F32 = mybir.dt.float32
BF16 = mybir.dt.bfloat16
I32 = mybir.dt.int32
AF = mybir.ActivationFunctionType
ALU = mybir.AluOpType
AX = mybir.AxisListType

D = 2048
KC = 16
T = 1024
NT = 8
DFF = 8192
EPS = 1e-6
PI = math.pi
TWO_PI = 2.0 * math.pi

CFG = {"layers": [0, 1, 2, 3], "ncores": 8}


class Tok:
    __slots__ = ("lw", "rd")

    def __init__(self):
        self.lw = None
        self.rd = []


class Prog:
    NDMA = 8
    ENG = ("pe", "act", "dve", "pool", "sp")

    def __init__(self, nc, stack):
        self.nc = nc
        self.stack = stack
        self.epoch = {e: 0 for e in self.ENG}
        self.ekey = {e: e for e in self.ENG}
        self.depoch = 0
        self.dkey = {}
        self.stream = {e: [] for e in self.ENG}
        self.sem = {}
        self.cnt = {}
        self.seen = {e: {} for e in self.ENG}
        for e in ("pe", "act", "dve", "pool"):
            self.sem[e] = stack.enter_context(nc.semaphore("s_" + e))
            self.cnt[e] = 0
        self.dsem = {}
        self.dcnt = {}
        self.dnext = {}
        self.nd = {"sp": self.NDMA, "pool": 4}
        for q in ("sp", "pool"):
            self.dsem[q] = [stack.enter_context(nc.semaphore(f"d_{q}{i}")) for i in range(self.nd[q])]
            self.dcnt[q] = [0] * self.nd[q]
            self.dnext[q] = 0
        self.nops = 0

    def _wait(self, e, ev):
        sem, val, key = ev
        s = self.seen[e]
        if s.get(key, 0) >= val:
            return
        self.stream[e].append(("w", sem, val))
        s[key] = val

    def _deps(self, e, reads, writes, skip_same_pe=False):
        for t in reads:
            if t.lw is not None:
                self._wait(e, t.lw)
        for t in writes:
            if t.lw is not None and not (skip_same_pe and t.lw[2].startswith("pe")):
                self._wait(e, t.lw)
            for ev in t.rd:
                self._wait(e, ev)

    def _mark(self, ev, reads, writes):
        for t in reads:
            t.rd = [x for x in t.rd if x[2] != ev[2]] + [ev]
        for t in writes:
            t.lw = ev
            t.rd = []

    def op(self, e, meth, kw, reads=(), writes=(), acc=False):
        self._deps(e, reads, writes, skip_same_pe=(e == "pe" and acc))
        self.cnt[e] += 1
        self.stream[e].append(("i", meth, kw, self.sem[e], 1))
        ev = (self.sem[e], self.cnt[e], self.ekey[e])
        self._mark(ev, reads, writes)
        self.nops += 1

    def dma(self, q, out, in_, reads=(), writes=(), **kw):
        i = self.dnext[q]
        self.dnext[q] = (i + 1) % self.nd[q]
        sem = self.dsem[q][i]
        key = self.dkey.get((q, i), f"d_{q}{i}")
        if self.dcnt[q][i] > 0:
            self._wait(q, (sem, 16 * self.dcnt[q][i], key))
        self._deps(q, reads, writes)
        self.dcnt[q][i] += 1
        kw = dict(kw, out=out, in_=in_)
        self.stream[q].append(("i", "dma_start", kw, sem, 16))
        ev = (sem, 16 * self.dcnt[q][i], key)
        self._mark(ev, reads, writes)
        self.nops += 1

    def _all_events(self):
        evs = [(self.sem[e], self.cnt[e], self.ekey[e]) for e in ("pe", "act", "dve", "pool") if self.cnt[e] > 0]
        for q in ("sp", "pool"):
            for i in range(self.nd[q]):
                if self.dcnt[q][i]:
                    evs.append((self.dsem[q][i], 16 * self.dcnt[q][i], self.dkey.get((q, i), f"d_{q}{i}")))
        return evs

    def barrier(self):
        evs = self._all_events()
        for e in self.ENG:
            for ev in evs:
                self._wait(e, ev)
        for e in ("pe", "act", "dve", "pool"):
            if self.cnt[e] > CFG.get('semthr', 12000):
                self.epoch[e] += 1
                self.sem[e] = self.stack.enter_context(self.nc.semaphore(f"s_{e}_{self.epoch[e]}"))
                self.cnt[e] = 0
                self.ekey[e] = f"{e}#{self.epoch[e]}"
        for q in ("sp", "pool"):
            for i in range(self.nd[q]):
                if self.dcnt[q][i] * 16 > CFG.get('semthr', 12000):
                    self.depoch += 1
                    self.dsem[q][i] = self.stack.enter_context(self.nc.semaphore(f"d_{q}{i}_{self.depoch}"))
                    self.dcnt[q][i] = 0
                    self.dkey[(q, i)] = f"d_{q}{i}#{self.depoch}"

    def emit(self):
        for ev in self._all_events():
            self._wait("sp", ev)
        nc = self.nc

        def run(engobj, items):
            for it in items:
                if it[0] == "w":
                    engobj.wait_ge(it[1], it[2])
                else:
                    getattr(engobj, it[1])(**it[2]).then_inc(it[3], it[4])

        with nc.Block() as block:
            @block.tensor
            def _(e):
                run(e, self.stream["pe"])

            @block.scalar
            def _(e):
                run(e, self.stream["act"])

            @block.vector
            def _(e):
                run(e, self.stream["dve"])

            @block.gpsimd
            def _(e):
                run(e, self.stream["pool"])

            @block.sync
            def _(e):
                run(e, self.stream["sp"])


class Buf:
    def __init__(self, t, ntok=1):
        self.t = t
        self.toks = [Tok() for _ in range(ntok)]

    @property
    def tok(self):
        return self.toks[0]

    def __getitem__(self, k):
        return self.t[k]


class Rot:
    def __init__(self, bufs):
        self.bufs = bufs
        self.i = 0

    def next(self):
        b = self.bufs[self.i]
        self.i = (self.i + 1) % len(self.bufs)
        return b


def group_info(g):
    return (1, 1024) if g == 0 else (4, 256)


def make_consts():
    c = {}
    c["ident"] = np.eye(128, dtype=np.float32)
    for L in (1024, 256):
        n = np.arange(L, dtype=np.float64)
        ang = 2.0 * np.pi * np.outer(n, n) / (2 * L)
        c[f"cm{L}"] = np.cos(ang).astype(np.float32)
        c[f"sm{L}"] = np.sin(ang).astype(np.float32)
        t = np.arange(L, dtype=np.float32)
        periods = (2.0 * (4096.0 / 2.0) ** (np.arange(16, dtype=np.float32) / 15.0)).astype(np.float32)
        a = t[:, None] * (np.float32(2.0 * math.pi) / periods)[None].astype(np.float32)
        pe = np.concatenate([np.sin(a), np.cos(a)], axis=-1).astype(np.float32)
        c[f"pet{L}"] = np.ascontiguousarray(pe.T)
        wf = np.full((128, L // 128), 2.0 / (2 * L), np.float32)
        wf[0, 0] = 1.0 / (2 * L)
        c[f"wf{L}"] = wf
    c["negt"] = -(np.arange(8)[None, :] * 128 + np.arange(128)[:, None]).astype(np.float32)
    alt = ((-1.0) ** np.arange(1024)).astype(np.float32)
    c["altrow"] = alt[None, :].copy()
    c["altcol"] = alt[:128, None].copy()
    rows = 1024 // 64
    row = np.repeat(np.arange(rows), 64).astype(np.float32)
    col = np.tile(np.arange(64), rows).astype(np.float32)
    half = 64
    inv = (10000.0 ** (-np.arange(0, half, 2, dtype=np.float32) / half)).astype(np.float32)
    ar = row[:, None] * inv
    ac = col[:, None] * inv
    ang = np.concatenate([ar, ar, ac, ac], axis=-1)
    c["cost"] = np.ascontiguousarray(np.cos(ang).T.astype(np.float32))
    c["sint"] = np.ascontiguousarray(np.sin(ang).T.astype(np.float32))
    R = np.zeros((128, 128), np.float32)
    for m in range(128):
        q = m // 32
        if q in (0, 2):
            R[m + 32, m] = -1.0
        else:
            R[m - 32, m] = 1.0
    c["rmat"] = R
    c["iota"] = np.arange(1024, dtype=np.float32)[None, :].copy()
    for L in (1024, 256):
        tm = (np.arange(1024) % L).astype(np.float32)
        c[f"iota{L}f"] = tm[None, :].copy()
        c[f"iota{L}b"] = (L - 1 - tm)[None, :].copy()
    p = np.arange(128)
    mb = np.zeros((128, 2, 16), np.float32)
    for e in range(2):
        mb[p // 64 == e, e, :] = 1.0
    c["maskb"] = mb.reshape(128, 32)
    mc = np.zeros((128, 2, 64), np.float32)
    ee = (p % 32) // 16
    for e in range(2):
        mc[ee == e, e, :] = 1.0
    c["maskc"] = mc.reshape(128, 128)
    hb = np.zeros((128, 2), np.float32)
    for j2 in range(2):
        hb[(p % 64) // 32 == j2, j2] = 1.0
    c["hmb"] = hb
    hc = np.zeros((1, 2, 64), np.float32)
    hc[0, 0, :32] = 1.0
    hc[0, 1, 32:] = 1.0
    c["hmc"] = hc.reshape(1, 128)
    return c


CONST_SHAPES = None


class Builder:
    def __init__(self, layers):
        self.layers = layers
        self.nc = bass.Bass("TRN2", target_bir_lowering=False)
        self.st = ExitStack()
        self.in_names = []
        self.out_names = []

    def din(self, name, shape):
        self.in_names.append(name)
        return self.nc.dram_tensor(name, list(shape), F32, kind="ExternalInput").ap()

    def dout(self, name, shape):
        self.out_names.append(name)
        return self.nc.dram_tensor(name, list(shape), F32, kind="ExternalOutput").ap()

    def sb(self, stack, name, shape, dt=F32, ntok=1):
        self._uid = getattr(self, "_uid", 0) + 1
        return Buf(stack.enter_context(self.nc.sbuf_tensor(f"{name}_{self._uid}", list(shape), dt)), ntok)

    def bankA(self, n=1):
        if self.pa_i + n > 6:
            self.pa_i = 0
        if n == 2 and self.pa_i % 2:
            self.pa_i += 1
            if self.pa_i + n > 6:
                self.pa_i = 0
        b = self.pa_i
        self.pa_i += n
        return b, self.pa_tok[b:b + n]

    def bankT(self):
        b = self.pt_i
        self.pt_i = (self.pt_i + 1) % 2
        return b, self.pt_tok[b]

    def op(self, e, meth, reads=(), writes=(), acc=False, **kw):
        self.P.op(e, meth, kw, reads, writes, acc)

    def run_stage(self, fn, *a, **k):
        self._sn = getattr(self, "_sn", 0) + 1
        if self._sn > CFG.get("nstages", 10 ** 9):
            return
        fn(*a, **k)

    def build(self):
        nc, st = self.nc, self.st
        with st:
            self.P = Prog(nc, st)
            P = self.P
            L = self.layers
            kinds = sorted(set(i % 3 for i in L))
            self.xin = [self.din("xs", [T, D]), self.din("xp", [T, D])]
            self.y = [self.dout("ys", [T, D]), self.dout("yp", [T, D])]
            self.ytok = [[Tok() for _ in range(NT)] for _ in range(2)]
            self.cvec = self.din("cvec", [2, D])
            self.w_mod = {i: self.din(f"w_mod{i}", [D, 6 * D]) for i in L}
            self.b_mod = self.din("b_mod", [4, 6 * D])
            self.g_norm = self.din("g_norm", [16, D])
            self.w1 = {i: self.din(f"w_mlp_in{i}", [D, DFF]) for i in L}
            self.w2 = {i: self.din(f"w_mlp_out{i}", [DFF, D]) for i in L}
            self.cst = {}
            for k, v in make_consts().items():
                self.cst[k] = self.din("c_" + k, v.shape)
            if 0 in kinds:
                self.hy = {
                    "w_in": {i // 3: self.din(f"hy_w_in{i // 3}", [D, 3 * D]) for i in L if i % 3 == 0}, "b_in": self.din("hy_b_in", [2 * 48, 128]),
                    "w_short": self.din("hy_w_short", [6 * 48, 128]), "b_short": self.din("hy_b_short", [2 * 48, 128]),
                    "f_w1": self.din("hy_f_w1", [64, 64]), "f_b1": self.din("hy_f_b1", [2, 64]),
                    "f_fr1": self.din("hy_f_freq1", [2, 64]), "f_w2": self.din("hy_f_w2", [128, 64]),
                    "f_b2": self.din("hy_f_b2", [2, 64]), "f_fr2": self.din("hy_f_freq2", [2, 64]),
                    "f_w3": self.din("hy_f_w3", [128, 2 * D]), "la": self.din("hy_log_alpha", [2, D]),
                    "skip": self.din("hy_skip", [2 * 16, 128]), "w_out": {i // 3: self.din(f"hy_w_out{i // 3}", [D, D]) for i in L if i % 3 == 0},
                    "b_out": self.din("hy_b_out", [2, D]),
                }
            if 1 in kinds:
                self.at = {
                    "w_qkv": self.din("at_w_qkv", [D, 3 * D]), "lam": self.din("at_lam", [1, 512]),
                    "g_sub": self.din("at_g_sub", [1, 256]), "w_o": self.din("at_w_o", [D, D]),
                    "ck": self.din("ck", [512, D]), "cv": self.din("cv", [512, D]),
                }
            if 1 in kinds or True:
                self.nk = self.dout("nk", [T, D])
                self.nv = self.dout("nv", [T, D])
                self.nsre = self.dout("nsre", [512, 128])
                self.nsim = self.dout("nsim", [512, 128])
            if 2 in kinds:
                self.s5 = {
                    "lam_re": self.din("s5_lam_re", [256, 64]), "lam_im": self.din("s5_lam_im", [256, 64]),
                    "log_dt": self.din("s5_log_dt", [256, 64]),
                    "bz_re": self.din("s5_bz_re", [16384, 128]), "bz_im": self.din("s5_bz_im", [16384, 128]),
                    "cz_re": self.din("s5_cz_re", [16384, 128]), "cz_im": self.din("s5_cz_im", [16384, 128]),
                    "d": self.din("s5_d", [16, 128]), "w_glu": self.din("s5_w_glu", [D, 2 * D]),
                    "b_glu": self.din("s5_b_glu", [1, 2 * D]),
                    "s0re": self.din("s0re", [256, 64]), "s0im": self.din("s0im", [256, 64]),
                }
            self.modtab = nc.dram_tensor("modtab", [8, 6 * D], F32).ap()
            self.modtok = [Tok() for _ in range(4)]
            self.osc = nc.dram_tensor("osc", [T, D], F32).ap()
            self.otok = [Tok() for _ in range(NT)]

            self.psA = st.enter_context(nc.psum_tensor("psA", [128, 6, 512], F32))
            self.psT = st.enter_context(nc.psum_tensor("psT", [128, 2, 1024], BF16))
            self.pa_tok = [Tok() for _ in range(6)]
            self.pt_tok = [Tok() for _ in range(2)]
            self.pa_i = 0
            self.pt_i = 0
            self.identf = self.sb(st, "identf", [128, 128])
            self.identb = self.sb(st, "identb", [128, 128], BF16)
            self.epsc = self.sb(st, "epsc", [128, 1])
            self.hpic = self.sb(st, "hpic", [128, 1])
            self.HT = self.sb(st, "HT", [128, KC, T], BF16, ntok=KC)
            P.dma("sp", self.identf[:], self.cst["ident"], writes=[self.identf.tok])
            P.dma("pool", self.identb[:], self.cst["ident"], writes=[self.identb.tok])
            self.op("dve", "memset", writes=[self.epsc.tok], ap=self.epsc[:], constant=EPS)
            self.op("dve", "memset", writes=[self.hpic.tok], ap=self.hpic[:], constant=PI / 2)

            self.run_stage(self.stage_adaln)
            for g in CFG.get("groups", [0, 1]):
                prev = None
                for li, i in enumerate(L):
                    kind = i % 3
                    if prev is None:
                        self.run_stage(self.stage_post_pre, None, None, g, i, 0)
                    else:
                        self.run_stage(self.stage_post_pre, prev, 1, g, i, 0)
                    mix = {0: self.stage_hyena, 1: self.stage_attn, 2: self.stage_s5}[kind]
                    self.run_stage(mix, i, g)
                    self.run_stage(self.stage_post_pre, i, 0, g, i, 1)
                    self.run_stage(self.stage_mlp, i, g)
                    if li == len(L) - 1:
                        self.run_stage(self.stage_post_pre, i, 1, g, None, None)
                    prev = i
            if CFG.get("dbg") == "osc":
                dbg = self.dout("dbg", [T, D])
                P.barrier()
                for tt in range(NT):
                    P.dma("sp", dbg[tt * 128:(tt + 1) * 128, :], self.osc[tt * 128:(tt + 1) * 128, :], reads=[self.otok[tt]], writes=[Tok()])
            P.emit()
        return nc

    def stage_zero_outputs(self, kinds):
        with ExitStack() as s:
            z = self.sb(s, "z", [128, D])
            self.op("dve", "memset", writes=[z.tok], ap=z[:], constant=0.0)
            for tt in range(NT):
                if 1 not in kinds:
                    self.P.dma("sp", self.nk[tt * 128:(tt + 1) * 128, :], z[:], reads=[z.tok], writes=[Tok()])
                    self.P.dma("sp", self.nv[tt * 128:(tt + 1) * 128, :], z[:], reads=[z.tok], writes=[Tok()])
                if 2 not in kinds:
                    self.P.dma("sp", self.nsre[tt * 128:(tt + 1) * 128, :], z[:, 0:128], reads=[z.tok], writes=[Tok()])
                    self.P.dma("sp", self.nsim[tt * 128:(tt + 1) * 128, :], z[:, 0:128], reads=[z.tok], writes=[Tok()])
            self.P.barrier()

    def load_cols(self, s, dst_ap, src_rows_ap, n, dst_tok):
        tmp = self.sb(s, "lc", [128, 128])
        self.P.dma("sp", tmp[0:n, :], src_rows_ap, writes=[tmp.tok])
        b, tk = self.bankA()
        self.op("pe", "transpose", reads=[tmp.tok, self.identf.tok], writes=tk,
                out=self.psA[:, b, 0:n], in_=tmp[0:n, :], identity=self.identf[0:n, 0:n])
        self.op("dve", "tensor_copy", reads=tk, writes=[dst_tok], out=dst_ap, in_=self.psA[:, b, 0:n])

    def wtile(self, wrot, src_ap, ncols, kc=KC, mapping="kcp"):
        wt = wrot.next()
        if mapping == "kcp":
            src = src_ap.rearrange("(kc p) n -> p kc n", p=128)
        else:
            src = src_ap.rearrange("(p kc) n -> p kc n", kc=kc)
        self.P.dma("pool", wt[:, 0:kc, 0:ncols], src, writes=[wt.tok])
        return wt

    def stage_adaln(self):
        P = self.P
        with ExitStack() as s:
            cv = self.sb(s, "cv", [128, 2, 16])
            cs = self.sb(s, "cs", [128, 2, 16], BF16)
            P.dma("sp", cv[:], self.cvec.rearrange("g (p kc) -> p g kc", kc=16), writes=[cv.tok])
            self.op("act", "activation", reads=[cv.tok], writes=[cs.tok], out=cs[:], in_=cv[:], func=AF.Silu)
            wrot = Rot([self.sb(s, "wm", [128, KC, 512], BF16) for _ in range(3)])
            modrow = self.sb(s, "modrow", [2, 6 * D])
            brow = self.sb(s, "brow", [2, 6 * D])
            for i in self.layers:
                P.dma("sp", brow[:], self.b_mod[i:i + 1, :].to_broadcast([2, 6 * D]), writes=[brow.tok])
                for cb in range(24):
                    wt = self.wtile(wrot, self.w_mod[i][:, cb * 512:(cb + 1) * 512], 512, mapping="pkc")
                    b, tk = self.bankA()
                    for kc in range(KC):
                        self.op("pe", "matmul", reads=[cs.tok, wt.tok], writes=tk, acc=(kc > 0),
                                out=self.psA[0:2, b, :], lhsT=cs[:, :, kc], rhs=wt[:, kc, :],
                                start=(kc == 0), stop=(kc == KC - 1))
                    self.op("dve", "tensor_tensor", reads=tk + [brow.tok], writes=[modrow.tok],
                            out=modrow[:, cb * 512:(cb + 1) * 512], in0=self.psA[0:2, b, :],
                            in1=brow[:, cb * 512:(cb + 1) * 512], op=ALU.add)
                P.dma("sp", self.modtab[2 * i:2 * i + 2, :], modrow[:], reads=[modrow.tok], writes=[self.modtok[i]])
            P.barrier()

    def mod_row(self, i, g, j):
        return self.modtab[2 * i + g:2 * i + g + 1, j * D:(j + 1) * D]

    def load_sg_sh(self, s, i, g, sub):
        P = self.P
        SG = self.sb(s, "SG", [128, D])
        SH = self.sb(s, "SH", [128, D])
        tmp = self.sb(s, "tabtmp", [128, D])
        jsh, jsc, gi = (0, 1, 0) if sub == 0 else (3, 4, 2)
        P.dma("sp", SH[:], self.mod_row(i, g, jsh).to_broadcast([128, D]), reads=[self.modtok[i]], writes=[SH.tok])
        P.dma("sp", SG[:], self.mod_row(i, g, jsc).to_broadcast([128, D]), reads=[self.modtok[i]], writes=[SG.tok])
        P.dma("sp", tmp[:], self.g_norm[4 * i + gi:4 * i + gi + 1, :].to_broadcast([128, D]), writes=[tmp.tok])
        self.op("dve", "scalar_tensor_tensor", reads=[SG.tok, tmp.tok], writes=[SG.tok],
                out=SG[:], in0=SG[:], scalar=1.0, in1=tmp[:], op0=ALU.add, op1=ALU.mult)
        return SG, SH

    def load_gg(self, s, i, g, sub):
        P = self.P
        GG = self.sb(s, "GG", [128, D])
        tmp = self.sb(s, "ggtmp", [128, D])
        jg, gi = (2, 1) if sub == 0 else (5, 3)
        P.dma("sp", GG[:], self.mod_row(i, g, jg).to_broadcast([128, D]), reads=[self.modtok[i]], writes=[GG.tok])
        P.dma("sp", tmp[:], self.g_norm[4 * i + gi:4 * i + gi + 1, :].to_broadcast([128, D]), writes=[tmp.tok])
        self.op("dve", "tensor_tensor", reads=[GG.tok, tmp.tok], writes=[GG.tok],
                out=GG[:], in0=GG[:], in1=tmp[:], op=ALU.mult)
        return GG

    def rstd_of(self, src_ap, src_toks, junk, ss, rs, n):
        self.op("act", "activation", reads=src_toks, writes=[junk.tok, ss.tok],
                out=junk[:, 0:n], in_=src_ap, func=AF.Square, accum_out=ss[:])
        self.op("act", "activation", reads=[ss.tok, self.epsc.tok], writes=[rs.tok],
                out=rs[:], in_=ss[:], func=AF.Sqrt, scale=1.0 / n, bias=self.epsc[:])
        self.op("dve", "reciprocal", reads=[rs.tok], writes=[rs.tok], out=rs[:], in_=rs[:])

    def stage_post_pre(self, ip, subp, g, i, sub, both=True):
        P = self.P
        do_post = ip is not None and both
        do_pre = i is not None
        with ExitStack() as s:
            if do_post:
                GG = self.load_gg(s, ip, g, subp)
            if do_pre:
                SG, SH = self.load_sg_sh(s, i, g, sub)
            xrot = Rot([self.sb(s, "xt", [128, D]) for _ in range(2)])
            orot = Rot([self.sb(s, "ot", [128, D]) for _ in range(2)])
            junk = self.sb(s, "junk", [128, D])
            t1 = self.sb(s, "t1", [128, D])
            hbrot = Rot([self.sb(s, "hb", [128, D], BF16) for _ in range(2)])
            ss = self.sb(s, "ss", [128, 1])
            rs = self.sb(s, "rs", [128, 1])
            for tt in range(NT):
                rows = slice(tt * 128, (tt + 1) * 128)
                xt = xrot.next()
                ytk = self.ytok[g][tt]
                if ip is None:
                    P.dma("sp", xt[:], self.xin[g][rows, :], writes=[xt.tok])
                    P.dma("sp", self.y[g][rows, :], xt[:], reads=[xt.tok], writes=[ytk])
                else:
                    P.dma("sp", xt[:], self.y[g][rows, :], reads=[ytk], writes=[xt.tok])
                if do_post:
                    ot = orot.next()
                    P.dma("sp", ot[:], self.osc[rows, :], reads=[self.otok[tt]], writes=[ot.tok])
                    self.rstd_of(ot[:], [ot.tok], junk, ss, rs, D)
                    self.op("dve", "scalar_tensor_tensor", reads=[ot.tok, rs.tok, GG.tok], writes=[t1.tok],
                            out=t1[:], in0=ot[:], scalar=rs[:, 0:1], in1=GG[:], op0=ALU.mult, op1=ALU.mult)
                    self.op("dve", "tensor_tensor", reads=[t1.tok, xt.tok], writes=[xt.tok],
                            out=xt[:], in0=t1[:], in1=xt[:], op=ALU.add)
                    P.dma("sp", self.y[g][rows, :], xt[:], reads=[xt.tok], writes=[ytk])
                if do_pre:
                    self.rstd_of(xt[:], [xt.tok], junk, ss, rs, D)
                    self.op("dve", "scalar_tensor_tensor", reads=[xt.tok, rs.tok, SG.tok], writes=[t1.tok],
                            out=t1[:], in0=xt[:], scalar=rs[:, 0:1], in1=SG[:], op0=ALU.mult, op1=ALU.mult)
                    hb = hbrot.next()
                    self.op("dve", "tensor_tensor", reads=[t1.tok, SH.tok], writes=[hb.tok],
                            out=hb[:], in0=t1[:], in1=SH[:], op=ALU.add)
                    for q4 in range(4):
                        b, tk = self.bankT()
                        for j in range(4):
                            kc = q4 * 4 + j
                            self.op("pe", "transpose", reads=[hb.tok, self.identb.tok], writes=[tk], acc=(j > 0),
                                    out=self.psT[:, b, j * 128:(j + 1) * 128], in_=hb[:, kc * 128:(kc + 1) * 128],
                                    identity=self.identb[:])
                        self.op("act" if q4 % 2 == 0 else "dve",
                                "activation" if q4 % 2 == 0 else "tensor_copy",
                                reads=[tk], writes=self.HT.toks[q4 * 4:q4 * 4 + 4],
                                out=self.HT[:, q4 * 4:q4 * 4 + 4, rows],
                                in_=self.psT[:, b, 0:512].rearrange("p (j t) -> p j t", j=4),
                                **({"func": AF.Identity} if q4 % 2 == 0 else {}))
            P.barrier()

    def out_proj(self, s, MT, w_ap, wrot, bias_tab=None):
        P = self.P
        orot = Rot([self.sb(s, "oe", [128, 512]) for _ in range(3)])
        for cb in range(4):
            wt = self.wtile(wrot, w_ap[:, cb * 512:(cb + 1) * 512], 512)
            for tt in range(NT):
                b, tk = self.bankA()
                for kc in range(KC):
                    self.op("pe", "matmul", reads=[MT.toks[kc], wt.tok], writes=tk, acc=(kc > 0),
                            out=self.psA[:, b, :], lhsT=MT[:, kc, tt * 128:(tt + 1) * 128], rhs=wt[:, kc, 0:512],
                            start=(kc == 0), stop=(kc == KC - 1))
                oe = orot.next()
                if bias_tab is not None:
                    self.op("dve", "tensor_tensor", reads=tk + [bias_tab.tok], writes=[oe.tok],
                            out=oe[:], in0=self.psA[:, b, :], in1=bias_tab[:, cb * 512:(cb + 1) * 512], op=ALU.add)
                else:
                    self.op("act", "activation", reads=tk, writes=[oe.tok], out=oe[:], in_=self.psA[:, b, :],
                            func=AF.Identity)
                P.dma("sp", self.osc[tt * 128:(tt + 1) * 128, cb * 512:(cb + 1) * 512], oe[:],
                      reads=[oe.tok], writes=[self.otok[tt]])

    def stage_mlp(self, i, g):
        P = self.P
        HT = self.HT
        with ExitStack() as s:
            HID = self.sb(s, "HID", [128, 64, 512], BF16, ntok=64)
            wrot = Rot([self.sb(s, "wml", [128, KC, 512], BF16) for _ in range(3)])
            rrot = Rot([self.sb(s, "rl", [128, 512], BF16) for _ in range(3)])
            orot = Rot([self.sb(s, "oe", [128, 512]) for _ in range(4)])
            for half in range(2):
                t0 = half * 512
                for hb in range(16):
                    wt = self.wtile(wrot, self.w1[i][:, hb * 512:(hb + 1) * 512], 512)
                    for j in range(4):
                        b, tk = self.bankA()
                        for kc in range(KC):
                            self.op("pe", "matmul", reads=[HT.toks[kc], wt.tok], writes=tk, acc=(kc > 0),
                                    out=self.psA[:, b, :], lhsT=wt[:, kc, j * 128:(j + 1) * 128],
                                    rhs=HT[:, kc, t0:t0 + 512], start=(kc == 0), stop=(kc == KC - 1))
                        rl = rrot.next()
                        self.op("act", "activation", reads=tk, writes=[rl.tok], out=rl[:], in_=self.psA[:, b, :],
                                func=AF.Relu)
                        hc = hb * 4 + j
                        self.op("dve", "tensor_tensor", reads=[rl.tok], writes=[HID.toks[hc]],
                                out=HID[:, hc, :], in0=rl[:], in1=rl[:], op=ALU.mult)
                for cb in range(4):
                    banks = [0, 1, 2, 3] if cb % 2 == 0 else [2, 3, 4, 5]
                    banks = [0, 1, 2, 3]
                    for kq in range(4):
                        wt = self.wtile(wrot, self.w2[i][kq * D:(kq + 1) * D, cb * 512:(cb + 1) * 512], 512)
                        for tt in range(4):
                            b = banks[tt]
                            for kc in range(KC):
                                hc = kq * 16 + kc
                                first = (kq == 0 and kc == 0)
                                self.op("pe", "matmul", reads=[HID.toks[hc], wt.tok], writes=[self.pa_tok[b]],
                                        acc=not first, out=self.psA[:, b, :],
                                        lhsT=HID[:, hc, tt * 128:(tt + 1) * 128], rhs=wt[:, kc, 0:512],
                                        start=first, stop=(kq == 3 and kc == KC - 1))
                    for tt in range(4):
                        b = banks[tt]
                        oe = orot.next()
                        self.op("act" if tt % 2 == 0 else "dve", "activation" if tt % 2 == 0 else "tensor_copy",
                                reads=[self.pa_tok[b]], writes=[oe.tok], out=oe[:], in_=self.psA[:, b, :],
                                **({"func": AF.Identity} if tt % 2 == 0 else {}))
                        gt = half * 4 + tt
                        P.dma("sp", self.osc[gt * 128:(gt + 1) * 128, cb * 512:(cb + 1) * 512], oe[:],
                              reads=[oe.tok], writes=[self.otok[gt]])
            self.pa_i = 0
            P.barrier()

    def sin_reduced(self, ni, out_ap, out_tok, arg, n, cols):
        self.op("dve", "tensor_scalar", reads=[arg.tok], writes=[ni.tok], out=ni[0:n, 0:cols], in0=arg[0:n, 0:cols],
                scalar1=1.0 / TWO_PI, scalar2=None, op0=ALU.mult)
        self.op("dve", "scalar_tensor_tensor", reads=[ni.tok, arg.tok], writes=[arg.tok], out=arg[0:n, 0:cols],
                in0=ni[0:n, 0:cols], scalar=-TWO_PI, in1=arg[0:n, 0:cols], op0=ALU.mult, op1=ALU.add)
        self.op("dve", "tensor_scalar", reads=[arg.tok], writes=[arg.tok], out=arg[0:n, 0:cols], in0=arg[0:n, 0:cols],
                scalar1=3.1415925, scalar2=-3.1415925, op0=ALU.min, op1=ALU.max)
        self.op("act", "activation", reads=[arg.tok], writes=[out_tok], out=out_ap, in_=arg[0:n, 0:cols], func=AF.Sin)

    def stage_hyena(self, i, g):
        P = self.P
        j = i // 3
        hy = self.hy
        nseq, L = group_info(g)
        TC = L // 128
        HT = self.HT
        cst = self.cst
        with ExitStack() as s:
            MT = self.sb(s, "MT", [128, KC, T], BF16, ntok=KC)
            h2 = self.sb(s, "h2", [64, L], BF16)
            bin_c = self.sb(s, "bin_c", [128, 48])
            bsh_c = self.sb(s, "bsh_c", [128, 48])
            wsh_c = self.sb(s, "wsh_c", [128, 3, 48])
            skip_c = self.sb(s, "skip_c", [128, 16])
            with ExitStack() as s1:
                self.load_cols(s1, bin_c[:, :], hy["b_in"][j * 48:(j + 1) * 48, :], 48, bin_c.tok)
                self.load_cols(s1, bsh_c[:, :], hy["b_short"][j * 48:(j + 1) * 48, :], 48, bsh_c.tok)
                for tap in range(3):
                    self.load_cols(s1, wsh_c[:, tap, :], hy["w_short"][(j * 3 + tap) * 48:(j * 3 + tap + 1) * 48, :], 48, wsh_c.tok)
                self.load_cols(s1, skip_c[:, :], hy["skip"][j * 16:(j + 1) * 16, :], 16, skip_c.tok)
                pet = self.sb(s1, "pet", [32, L])
                P.dma("sp", pet[:], cst[f"pet{L}"], writes=[pet.tok])
                fw1 = self.sb(s1, "fw1", [32, 64])
                fw2 = self.sb(s1, "fw2", [64, 64])
                P.dma("sp", fw1[:], hy["f_w1"][j * 32:(j + 1) * 32, :], writes=[fw1.tok])
                P.dma("sp", fw2[:], hy["f_w2"][j * 64:(j + 1) * 64, :], writes=[fw2.tok])
                fcol = self.sb(s1, "fcol", [64, 6])
                for k, nm in enumerate(["f_b1", "f_fr1", "f_b2", "f_fr2"]):
                    P.dma("sp", fcol[:, k:k + 1], hy[nm][j:j + 1, :].rearrange("o (p x) -> (o p) x", x=1), writes=[fcol.tok])
                self.op("dve", "tensor_tensor", reads=[fcol.tok], writes=[fcol.tok], out=fcol[:, 4:5], in0=fcol[:, 0:1],
                        in1=fcol[:, 1:2], op=ALU.mult)
                self.op("dve", "tensor_tensor", reads=[fcol.tok], writes=[fcol.tok], out=fcol[:, 5:6], in0=fcol[:, 2:3],
                        in1=fcol[:, 3:4], op=ALU.mult)
                h1 = self.sb(s1, "h1", [64, L])
                arg = self.sb(s1, "farg", [64, L])
                ni = self.sb(s1, "fni", [64, L], I32)
                for (wmat, kdim, src, frc, fbc, dst) in ((fw1, 32, pet, 1, 4, h1), (fw2, 64, h1, 3, 5, h2)):
                    for tb in range(max(1, L // 512)):
                        n = min(512, L)
                        b, tk = self.bankA()
                        self.op("pe", "matmul", reads=[wmat.tok, src.tok], writes=tk, out=self.psA[0:64, b, 0:n],
                                lhsT=wmat[0:kdim, :], rhs=src[0:kdim, tb * n:(tb + 1) * n], start=True, stop=True)
                        self.op("dve", "tensor_scalar", reads=tk + [fcol.tok], writes=[arg.tok],
                                out=arg[0:64, tb * n:(tb + 1) * n], in0=self.psA[0:64, b, 0:n],
                                scalar1=fcol[:, frc:frc + 1], scalar2=fcol[:, fbc:fbc + 1], op0=ALU.mult, op1=ALU.add)
                    self.sin_reduced(ni, dst[:, :], dst.tok, arg, 64, L)
                P.barrier()
            s2 = ExitStack()
            CM = self.sb(s2, "CM", [128, TC, L], BF16)
            SM = self.sb(s2, "SM", [128, TC, L], BF16)
            P.dma("pool", CM[:], cst[f"cm{L}"].rearrange("(tc p) f -> p tc f", p=128), writes=[CM.tok])
            P.dma("pool", SM[:], cst[f"sm{L}"].rearrange("(tc p) f -> p tc f", p=128), writes=[SM.tok])
            wf = self.sb(s2, "wf", [128, TC])
            P.dma("sp", wf[:], cst[f"wf{L}"], writes=[wf.tok])
            negt = self.sb(s2, "negt", [128, 8])
            P.dma("sp", negt[:], cst["negt"], writes=[negt.tok])
            altcol = self.sb(s2, "altcol", [128, 1], BF16)
            altrow = self.sb(s2, "altrow", [1, 1024], BF16)
            P.dma("pool", altcol[:], cst["altcol"], writes=[altcol.tok])
            P.dma("pool", altrow[:], cst["altrow"], writes=[altrow.tok])
            onesf = self.sb(s2, "onesf", [128, 128])
            self.op("dve", "memset", writes=[onesf.tok], ap=onesf[:], constant=1.0)
            w3 = self.sb(s2, "w3", [64, 2 * D], BF16)
            for hh in range(2):
                P.dma("pool", w3[:, hh * D:(hh + 1) * D], hy["f_w3"][j * 64:(j + 1) * 64, hh * D:(hh + 1) * D], writes=[w3.tok])

            wrot = Rot([self.sb(s2, "wh", [128, KC, 128], BF16) for _ in range(3)])
            Zrot = Rot([self.sb(s2, "Z", [128, 3, nseq, L + 2]) for _ in range(2)])
            for Zb in Zrot.bufs:
                self.op("dve", "memset", writes=[Zb.tok], ap=Zb[:], constant=0.0)
            ZC = self.sb(s2, "ZC", [128, 3, T])
            VB = self.sb(s2, "VB", [128, T], BF16)
            VTM = self.sb(s2, "VTM", [128, NT, 128], BF16)
            KRE = self.sb(s2, "KRE", [128, TC, 128])
            KIM = self.sb(s2, "KIM", [128, TC, 128])
            KNY = self.sb(s2, "KNY", [1, 128])
            HS = self.sb(s2, "HS", [128, TC, 128], BF16)
            HD = self.sb(s2, "HD", [128, TC, 128], BF16)
            KF = self.sb(s2, "KF", [128, TC, 128])
            KB = self.sb(s2, "KB", [128, TC, 128])
            RN = self.sb(s2, "RN", [128, 128])
            YRE = self.sb(s2, "YRE", [128, NT, 128], BF16)
            YN = self.sb(s2, "YN", [128, NT, 128], BF16)
            YNY = self.sb(s2, "YNY", [1, nseq, 128], BF16)
            TA = self.sb(s2, "TA", [128, NT, 128])
            TB_ = self.sb(s2, "TB", [128, NT, 128])
            TY = self.sb(s2, "TY", [128, T])
            ea = self.sb(s2, "ea", [128, 128])
            DEC = Buf(TY.t, 1)
            DEC.toks = TY.toks
            DECv = TY[:, :].rearrange("p (a c) -> p a c", a=NT)
            Zs = {}

            def emit_inproj(c_):
                Zb_ = Zrot.next()
                for part in range(3):
                    col0 = part * D + c_ * 128
                    wt = self.wtile(wrot, hy["w_in"][j][:, col0:col0 + 128], 128)
                    cc = part * 16 + c_
                    for tb in range(2):
                        b, tk = self.bankA()
                        for kc in range(KC):
                            self.op("pe", "matmul", reads=[HT.toks[kc], wt.tok], writes=tk, acc=(kc > 0),
                                    out=self.psA[:, b, :], lhsT=wt[:, kc, 0:128], rhs=HT[:, kc, tb * 512:(tb + 1) * 512],
                                    start=(kc == 0), stop=(kc == KC - 1))
                        ns = 512 // L if L < 512 else 1
                        if L >= 512:
                            dst = Zb_[:, part, 0, 1 + tb * 512:1 + (tb + 1) * 512]
                            src = self.psA[:, b, :]
                        else:
                            dst = Zb_[:, part, tb * ns:(tb + 1) * ns, 1:L + 1]
                            src = self.psA[:, b, :].rearrange("p (s t) -> p s t", s=ns)
                        self.op("act", "activation", reads=tk + [bin_c.tok], writes=[Zb_.tok], out=dst, in_=src,
                                func=AF.Identity, bias=bin_c[:, cc:cc + 1])
                return Zb_

            Zs[0] = emit_inproj(0)
            for c in range(KC):
                P.dma("sp", ea[:], hy["la"][j:j + 1, c * 128:(c + 1) * 128].to_broadcast([128, 128]), writes=[ea.tok])
                self.op("act", "activation", reads=[ea.tok], writes=[ea.tok], out=ea[:], in_=ea[:], func=AF.Exp)
                for tc in range(TC):
                    self.op("act", "activation", reads=[ea.tok, negt.tok], writes=[DEC.tok], out=DECv[:, tc, :],
                            in_=ea[:], func=AF.Exp, scale=negt[:, tc:tc + 1])
                for di, KD in ((0, KF), (1, KB)):
                    for tc in range(TC):
                        b, tk = self.bankA()
                        self.op("pe", "matmul", reads=[h2.tok, w3.tok], writes=tk, out=self.psA[:, b, 0:128],
                                lhsT=h2[:, tc * 128:(tc + 1) * 128], rhs=w3[:, di * D + c * 128:di * D + (c + 1) * 128],
                                start=True, stop=True)
                        self.op("dve", "tensor_tensor", reads=tk + [DEC.tok], writes=[KD.tok], out=KD[:, tc, :],
                                in0=self.psA[:, b, 0:128], in1=DECv[:, tc, :], op=ALU.mult)
                self.op("dve", "memset", reads=[KB.tok], writes=[KB.tok], ap=KB[0:1, 0, :], constant=0.0)
                self.op("act", "activation", reads=[KF.tok], writes=[TA.tok], out=TA[:, 0:TC, :], in_=KF[:], func=AF.Abs)
                self.op("act", "activation", reads=[KB.tok], writes=[TB_.tok], out=TB_[:, 0:TC, :], in_=KB[:], func=AF.Abs)
                b, tk = self.bankA()
                for k in range(2 * TC):
                    AB = TA if k < TC else TB_
                    self.op("pe", "matmul", reads=[onesf.tok, AB.tok], writes=tk, acc=(k > 0), out=self.psA[:, b, 0:128],
                            lhsT=onesf[:], rhs=AB[:, k % TC, :], start=(k == 0), stop=(k == 2 * TC - 1))
                self.op("dve", "tensor_scalar", reads=tk, writes=[RN.tok], out=RN[:], in0=self.psA[:, b, 0:128],
                        scalar1=1e-6, scalar2=None, op0=ALU.add)
                self.op("dve", "reciprocal", reads=[RN.tok], writes=[RN.tok], out=RN[:], in_=RN[:])
                self.op("dve", "tensor_tensor", reads=[KF.tok, KB.tok], writes=[HS.tok], out=HS[:], in0=KF[:], in1=KB[:],
                        op=ALU.add)
                self.op("dve", "tensor_tensor", reads=[KF.tok, KB.tok], writes=[HD.tok], out=HD[:], in0=KB[:], in1=KF[:],
                        op=ALU.subtract)
                for (MAT, HX, KX) in ((CM, HS, KRE), (SM, HD, KIM)):
                    for fc in range(TC):
                        b, tk = self.bankA()
                        for tc in range(TC):
                            self.op("pe", "matmul", reads=[MAT.tok, HX.tok], writes=tk, acc=(tc > 0),
                                    out=self.psA[:, b, 0:128], lhsT=MAT[:, tc, fc * 128:(fc + 1) * 128], rhs=HX[:, tc, :],
                                    start=(tc == 0), stop=(tc == TC - 1))
                        self.op("dve", "scalar_tensor_tensor", reads=tk + [wf.tok, RN.tok], writes=[KX.tok],
                                out=KX[:, fc, :], in0=self.psA[:, b, 0:128], scalar=wf[:, fc:fc + 1], in1=RN[:],
                                op0=ALU.mult, op1=ALU.mult)
                b, tk = self.bankA()
                for tc in range(TC):
                    self.op("pe", "matmul", reads=[altcol.tok, HS.tok], writes=tk, acc=(tc > 0),
                            out=self.psA[0:1, b, 0:128], lhsT=altcol[:, 0:1], rhs=HS[:, tc, :],
                            start=(tc == 0), stop=(tc == TC - 1))
                self.op("dve", "scalar_tensor_tensor", reads=tk + [RN.tok], writes=[KNY.tok], out=KNY[:],
                        in0=self.psA[0:1, b, 0:128], scalar=1.0 / (2 * L), in1=RN[0:1, :], op0=ALU.mult, op1=ALU.mult)

                Z = Zs.pop(c)
                for part in range(3):
                    cc = part * 16 + c
                    zc = ZC[:, part, :].rearrange("p (s t) -> p s t", s=nseq)
                    self.op("dve", "tensor_scalar", reads=[Z.tok, wsh_c.tok, bsh_c.tok], writes=[ZC.tok], out=zc,
                            in0=Z[:, part, :, 0:L], scalar1=wsh_c[:, 0, cc:cc + 1], scalar2=bsh_c[:, cc:cc + 1],
                            op0=ALU.mult, op1=ALU.add)
                    for tap in (1, 2):
                        self.op("dve", "scalar_tensor_tensor", reads=[Z.tok, wsh_c.tok, ZC.tok], writes=[ZC.tok], out=zc,
                                in0=Z[:, part, :, tap:tap + L], scalar=wsh_c[:, tap, cc:cc + 1], in1=zc,
                                op0=ALU.mult, op1=ALU.add)
                self.op("dve", "tensor_tensor", reads=[ZC.tok], writes=[ZC.tok], out=ZC[:, 2, :], in0=ZC[:, 2, :],
                        in1=ZC[:, 1, :], op=ALU.mult)
                self.op("act", "activation", reads=[ZC.tok], writes=[VB.tok], out=VB[:], in_=ZC[:, 2, :], func=AF.Identity)
                b, tk = self.bankT()
                for tt in range(NT):
                    self.op("pe", "transpose", reads=[VB.tok, self.identb.tok], writes=[tk], acc=(tt > 0),
                            out=self.psT[:, b, tt * 128:(tt + 1) * 128], in_=VB[:, tt * 128:(tt + 1) * 128],
                            identity=self.identb[:])
                self.op("dve", "tensor_copy", reads=[tk], writes=[VTM.tok], out=VTM[:],
                        in_=self.psT[:, b, :].rearrange("p (a c) -> p a c", a=NT))
                for sq in range(nseq):
                    bre, tkre = self.bankA(2 if TC > 4 else 1)
                    bim, tkim = self.bankA(2 if TC > 4 else 1)
                    for (MAT, b0, tks) in ((CM, bre, tkre), (SM, bim, tkim)):
                        for fc in range(TC):
                            o = self.psA[:, b0 + fc // 4, (fc % 4) * 128:(fc % 4 + 1) * 128]
                            for tc in range(TC):
                                self.op("pe", "matmul", reads=[MAT.tok, VTM.tok], writes=tks, acc=not (fc == 0 and tc == 0),
                                        out=o, lhsT=MAT[:, tc, fc * 128:(fc + 1) * 128], rhs=VTM[:, sq * TC + tc, :],
                                        start=(tc == 0), stop=(tc == TC - 1))
                    bny, tkny = self.bankA()
                    for tc in range(TC):
                        self.op("pe", "matmul", reads=[altcol.tok, VTM.tok], writes=tkny, acc=(tc > 0),
                                out=self.psA[0:1, bny, 0:128], lhsT=altcol[:, 0:1], rhs=VTM[:, sq * TC + tc, :],
                                start=(tc == 0), stop=(tc == TC - 1))
                    self.op("dve", "tensor_tensor", reads=tkny + [KNY.tok], writes=[YNY.tok], out=YNY[0:1, sq, :],
                            in0=self.psA[0:1, bny, 0:128], in1=KNY[:], op=ALU.mult)
                    nb = 2 if TC > 4 else 1
                    ure = self.psA[:, bre:bre + nb, 0:(512 if TC >= 4 else TC * 128)]
                    uim = self.psA[:, bim:bim + nb, 0:(512 if TC >= 4 else TC * 128)]
                    if nb == 1:
                        ure = self.psA[:, bre, 0:TC * 128]
                        uim = self.psA[:, bim, 0:TC * 128]
                        shp = lambda x: x.rearrange("p a c -> p (a c)")
                    else:
                        shp = lambda x: x.rearrange("p (b a) c -> p b (a c)", b=2)
                    fs = slice(sq * TC, (sq + 1) * TC)
                    self.op("dve", "tensor_tensor", reads=tkre + [KRE.tok], writes=[TA.tok], out=shp(TA[:, 0:TC, :]),
                            in0=ure, in1=shp(KRE[:]), op=ALU.mult)
                    self.op("dve", "tensor_tensor", reads=tkim + [KIM.tok], writes=[TB_.tok], out=shp(TB_[:, 0:TC, :]),
                            in0=uim, in1=shp(KIM[:]), op=ALU.mult)
                    self.op("dve", "tensor_tensor", reads=[TA.tok, TB_.tok], writes=[YRE.tok], out=YRE[:, fs, :],
                            in0=TA[:, 0:TC, :], in1=TB_[:, 0:TC, :], op=ALU.add)
                    self.op("dve", "tensor_tensor", reads=tkim + [KRE.tok], writes=[TA.tok], out=shp(TA[:, 0:TC, :]),
                            in0=uim, in1=shp(KRE[:]), op=ALU.mult)
                    self.op("dve", "tensor_tensor", reads=tkre + [KIM.tok], writes=[TB_.tok], out=shp(TB_[:, 0:TC, :]),
                            in0=ure, in1=shp(KIM[:]), op=ALU.mult)
                    self.op("dve", "tensor_tensor", reads=[TA.tok, TB_.tok], writes=[YN.tok], out=YN[:, fs, :],
                            in0=TA[:, 0:TC, :], in1=TB_[:, 0:TC, :], op=ALU.subtract)
                if c + 1 < KC:
                    Zs[c + 1] = emit_inproj(c + 1)
                nblk = 2 if L >= 512 else 1
                for tb in range(2):
                    b, tk = self.bankA()
                    if L >= 512:
                        segs = [(0, tb * 512, 512, 0)]
                    else:
                        segs = [(tb * 2 + k, 0, L, k * L) for k in range(2)]
                    for (sq, t0, n, c0) in segs:
                        o = self.psA[:, b, c0:c0 + n]
                        for fc in range(TC):
                            self.op("pe", "matmul", reads=[YRE.tok, CM.tok], writes=tk, acc=not (fc == 0 and c0 == 0),
                                    out=o, lhsT=YRE[:, sq * TC + fc, :], rhs=CM[:, fc, t0:t0 + n], start=(fc == 0), stop=False)
                            self.op("pe", "matmul", reads=[YN.tok, SM.tok], writes=tk, acc=True,
                                    out=o, lhsT=YN[:, sq * TC + fc, :], rhs=SM[:, fc, t0:t0 + n], start=False, stop=False)
                        self.op("pe", "matmul", reads=[YNY.tok, altrow.tok], writes=tk, acc=True,
                                out=o, lhsT=YNY[0:1, sq, :], rhs=altrow[0:1, t0:t0 + n], start=False, stop=True)
                    ts_ = slice(tb * 512, (tb + 1) * 512)
                    self.op("dve", "scalar_tensor_tensor", reads=tk + [ZC.tok, skip_c.tok], writes=[TY.tok], out=TY[:, ts_],
                            in0=ZC[:, 2, ts_], scalar=skip_c[:, c:c + 1], in1=self.psA[:, b, :], op0=ALU.mult, op1=ALU.add)
                    self.op("dve", "tensor_tensor", reads=[TY.tok, ZC.tok], writes=[MT.toks[c]], out=MT[:, c, ts_],
                            in0=TY[:, ts_], in1=ZC[:, 0, ts_], op=ALU.mult)
            P.barrier()
            s2.close()
            with ExitStack() as s3:
                bout = self.sb(s3, "bout", [128, D])
                P.dma("sp", bout[:], hy["b_out"][j:j + 1, :].to_broadcast([128, D]), writes=[bout.tok])
                wrot2 = Rot([self.sb(s3, "wo", [128, KC, 512], BF16) for _ in range(2)])
                self.out_proj(s3, MT, hy["w_out"][j][:, :], wrot2, bias_tab=bout)
                P.barrier()

    def stage_attn(self, i, g):
        P = self.P
        j = i // 3
        at = self.at
        HT = self.HT
        cst = self.cst
        lam_init = 0.8 - 0.6 * math.exp(-0.3 * i)
        sc = 128.0 ** -0.5
        NKT = 12 if g == 0 else 8
        koff = 512 if g == 0 else 0
        voff = 4 if g == 0 else 0
        with ExitStack() as s:
            OFM = self.sb(s, "OFM", [128, KC, T], BF16, ntok=KC)
            s2 = ExitStack()
            lamb = self.sb(s2, "lamb", [128, 512])
            P.dma("sp", lamb[:], at["lam"][0:1, :].to_broadcast([128, 512]), writes=[lamb.tok])
            lt = self.sb(s2, "lt", [128, 256])
            ljunk = self.sb(s2, "ljunk", [128, 128])
            lcol = self.sb(s2, "lcol", [128, 4])
            for k in range(2):
                self.op("dve", "tensor_tensor", reads=[lamb.tok], writes=[lt.tok], out=lt[:, k * 128:(k + 1) * 128],
                        in0=lamb[:, k * 256:k * 256 + 128], in1=lamb[:, k * 256 + 128:k * 256 + 256], op=ALU.mult)
            for k in range(2):
                self.op("act", "activation", reads=[lt.tok], writes=[ljunk.tok, lcol.tok], out=ljunk[:],
                        in_=lt[:, k * 128:(k + 1) * 128], func=AF.Identity, accum_out=lcol[:, k:k + 1])
            self.op("act", "activation", reads=[lcol.tok], writes=[lcol.tok], out=lcol[:, 0:2], in_=lcol[:, 0:2], func=AF.Exp)
            self.op("dve", "scalar_tensor_tensor", reads=[lcol.tok], writes=[lcol.tok], out=lcol[:, 2:3], in0=lcol[:, 1:2],
                    scalar=-lam_init, in1=lcol[:, 0:1], op0=ALU.add, op1=ALU.subtract)
            gs = self.sb(s2, "gs", [128, 256])
            P.dma("sp", gs[:], at["g_sub"][0:1, :].to_broadcast([128, 256]), writes=[gs.tok])
            self.op("dve", "tensor_scalar", reads=[gs.tok], writes=[gs.tok], out=gs[:], in0=gs[:], scalar1=1.0 - lam_init,
                    scalar2=None, op0=ALU.mult)
            if g == 0:
                cosT = self.sb(s2, "cosT", [128, T])
                sinT = self.sb(s2, "sinT", [128, T])
                rmat = self.sb(s2, "rmat", [128, 128])
                P.dma("sp", cosT[:], cst["cost"], writes=[cosT.tok])
                P.dma("sp", sinT[:], cst["sint"], writes=[sinT.tok])
                P.dma("sp", rmat[:], cst["rmat"], writes=[rmat.tok])
                xfrot = Rot([self.sb(s2, "xf", [128, 512]) for _ in range(2)])
                t1rot = Rot([self.sb(s2, "rt1", [128, 512]) for _ in range(2)])
                t2rot = Rot([self.sb(s2, "rt2", [128, 512]) for _ in range(2)])
                ckrot = Rot([self.sb(s2, "ckb", [128, 4, 256], BF16) for _ in range(2)])
            else:
                strot = Rot([self.sb(s2, "kvst", [128, 256]) for _ in range(4)])
            wrot = Rot([self.sb(s2, "wa", [128, KC, 256], BF16) for _ in range(4)])
            QTrot = Rot([self.sb(s2, "QT", [128, 2, T], BF16) for _ in range(2)])
            KTrot = Rot([self.sb(s2, "KT", [128, 2, koff + T], BF16) for _ in range(2)])
            Vrot = Rot([self.sb(s2, "VA", [128, NKT, 272], BF16) for _ in range(2)])
            for vb in Vrot.bufs:
                self.op("dve", "memset", writes=[vb.tok], ap=vb[:, :, 256:272], constant=1.0)
            erot = Rot([self.sb(s2, "E", [128, 256], BF16) for _ in range(4)])
            rc = self.sb(s2, "rc", [128, 4])
            at1 = self.sb(s2, "at1", [128, 256])
            ao = self.sb(s2, "ao", [128, 256])
            ajunk = self.sb(s2, "ajunk", [128, 256])
            ass = self.sb(s2, "ass", [128, 1])
            ars = self.sb(s2, "ars", [128, 1])
            obrot = Rot([self.sb(s2, "ob", [128, 256], BF16) for _ in range(2)])
            scnt = 0
            for h in range(8):
                wq = self.wtile(wrot, at["w_qkv"][:, h * 256:(h + 1) * 256], 256)
                wk = self.wtile(wrot, at["w_qkv"][:, D + h * 256:D + (h + 1) * 256], 256)
                wv = self.wtile(wrot, at["w_qkv"][:, 2 * D + h * 256:2 * D + (h + 1) * 256], 256)
                QT = QTrot.next()
                KT = KTrot.next()
                VA = Vrot.next()
                self.pa_i = 0
                for (wt, dstb, off) in ((wq, QT, 0), (wk, KT, koff)):
                    for c in range(2):
                        for tb in range(2):
                            b, tk = self.bankA()
                            for kc in range(KC):
                                self.op("pe", "matmul", reads=[HT.toks[kc], wt.tok], writes=tk, acc=(kc > 0),
                                        out=self.psA[:, b, :], lhsT=wt[:, kc, c * 128:(c + 1) * 128],
                                        rhs=HT[:, kc, tb * 512:(tb + 1) * 512], start=(kc == 0), stop=(kc == KC - 1))
                            dst = dstb[:, c, off + tb * 512:off + (tb + 1) * 512]
                            if g == 0:
                                xf = xfrot.next()
                                self.op("act", "activation", reads=tk, writes=[xf.tok], out=xf[:], in_=self.psA[:, b, :],
                                        func=AF.Identity)
                                b2, tk2 = self.bankA()
                                self.op("pe", "matmul", reads=[rmat.tok, xf.tok], writes=tk2, out=self.psA[:, b2, :],
                                        lhsT=rmat[:], rhs=xf[:], start=True, stop=True)
                                t1 = t1rot.next()
                                t2 = t2rot.next()
                                self.op("dve", "tensor_tensor", reads=[xf.tok, cosT.tok], writes=[t1.tok], out=t1[:], in0=xf[:],
                                        in1=cosT[:, tb * 512:(tb + 1) * 512], op=ALU.mult)
                                self.op("dve", "tensor_tensor", reads=tk2 + [sinT.tok], writes=[t2.tok], out=t2[:],
                                        in0=self.psA[:, b2, :], in1=sinT[:, tb * 512:(tb + 1) * 512], op=ALU.mult)
                                self.op("dve", "tensor_tensor", reads=[t1.tok, t2.tok], writes=[dstb.tok], out=dst, in0=t1[:],
                                        in1=t2[:], op=ALU.add)
                            else:
                                self.op("act", "activation", reads=tk, writes=[dstb.tok], out=dst, in_=self.psA[:, b, :],
                                        func=AF.Identity)
                if g == 0:
                    ckb = ckrot.next()
                    P.dma("pool", ckb[:], at["ck"][:, h * 256:(h + 1) * 256].rearrange("(a p) n -> p a n", p=128),
                          writes=[ckb.tok])
                    bT, tkT = self.bankT()
                    for c in range(2):
                        for a in range(4):
                            idx = c * 4 + a
                            self.op("pe", "transpose", reads=[ckb.tok, self.identb.tok], writes=[tkT], acc=(idx > 0),
                                    out=self.psT[:, bT, idx * 128:(idx + 1) * 128], in_=ckb[:, a, c * 128:(c + 1) * 128],
                                    identity=self.identb[:])
                    self.op("dve", "tensor_copy", reads=[tkT], writes=[KT.tok], out=KT[:, :, 0:512],
                            in_=self.psT[:, bT, :].rearrange("p (c t) -> p c t", c=2))
                    P.dma("pool", VA[:, 0:4, 0:256], at["cv"][:, h * 256:(h + 1) * 256].rearrange("(a p) n -> p a n", p=128),
                          writes=[VA.tok])
                for tt in range(NT):
                    rows = slice(tt * 128, (tt + 1) * 128)
                    b, tk = self.bankA()
                    for kc in range(KC):
                        self.op("pe", "matmul", reads=[HT.toks[kc], wv.tok], writes=tk, acc=(kc > 0),
                                out=self.psA[:, b, 0:256], lhsT=HT[:, kc, rows], rhs=wv[:, kc, 0:256],
                                start=(kc == 0), stop=(kc == KC - 1))
                    if g == 0:
                        self.op("dve", "tensor_copy", reads=tk, writes=[VA.tok], out=VA[:, voff + tt, 0:256],
                                in_=self.psA[:, b, 0:256])
                    else:
                        vst = strot.next()
                        self.op("act", "activation", reads=tk, writes=[vst.tok], out=vst[:], in_=self.psA[:, b, 0:256],
                                func=AF.Identity)
                        self.op("dve", "tensor_copy", reads=[vst.tok], writes=[VA.tok], out=VA[:, voff + tt, 0:256], in_=vst[:])
                        P.dma("sp", self.nv[rows, h * 256:(h + 1) * 256], vst[:], reads=[vst.tok], writes=[Tok()])
                        b, tk = self.bankA()
                        for kc in range(KC):
                            self.op("pe", "matmul", reads=[HT.toks[kc], wk.tok], writes=tk, acc=(kc > 0),
                                    out=self.psA[:, b, 0:256], lhsT=HT[:, kc, rows], rhs=wk[:, kc, 0:256],
                                    start=(kc == 0), stop=(kc == KC - 1))
                        kst = strot.next()
                        self.op("act", "activation", reads=tk, writes=[kst.tok], out=kst[:], in_=self.psA[:, b, 0:256],
                                func=AF.Identity)
                        P.dma("sp", self.nk[rows, h * 256:(h + 1) * 256], kst[:], reads=[kst.tok], writes=[Tok()])
                for blk in range(4):
                    q0 = blk * 256
                    chunks = list(range(12)) if g == 0 else [blk * 2, blk * 2 + 1]
                    for c in range(2):
                        for idx, kt in enumerate(chunks):
                            sbk = 4 + (scnt % 2)
                            scnt += 1
                            self.op("pe", "matmul", reads=[KT.tok, QT.tok], writes=[self.pa_tok[sbk]],
                                    out=self.psA[:, sbk, 0:256], lhsT=KT[:, c, kt * 128:(kt + 1) * 128],
                                    rhs=QT[:, c, q0:q0 + 256], start=True, stop=True)
                            E = erot.next()
                            self.op("act", "activation", reads=[self.pa_tok[sbk]], writes=[E.tok], out=E[:],
                                    in_=self.psA[:, sbk, 0:256], func=AF.Exp, scale=sc)
                            for qt in range(2):
                                pb = c * 2 + qt
                                self.op("pe", "matmul", reads=[E.tok, VA.tok], writes=[self.pa_tok[pb]], acc=(idx > 0),
                                        out=self.psA[:, pb, 0:258], lhsT=E[:, qt * 128:(qt + 1) * 128], rhs=VA[:, kt, 0:258],
                                        start=(idx == 0), stop=(idx == len(chunks) - 1))
                    for qt in range(2):
                        p0, p1 = self.pa_tok[qt], self.pa_tok[2 + qt]
                        self.op("dve", "reciprocal", reads=[p0], writes=[rc.tok], out=rc[:, 0:1], in_=self.psA[:, qt, 256:257])
                        self.op("dve", "reciprocal", reads=[p1], writes=[rc.tok], out=rc[:, 1:2], in_=self.psA[:, 2 + qt, 256:257])
                        self.op("dve", "tensor_tensor", reads=[rc.tok, lcol.tok], writes=[rc.tok], out=rc[:, 2:3], in0=rc[:, 1:2],
                                in1=lcol[:, 2:3], op=ALU.mult)
                        self.op("dve", "tensor_scalar", reads=[p1, rc.tok], writes=[at1.tok], out=at1[:],
                                in0=self.psA[:, 2 + qt, 0:256], scalar1=rc[:, 2:3], scalar2=None, op0=ALU.mult)
                        self.op("dve", "scalar_tensor_tensor", reads=[p0, rc.tok, at1.tok], writes=[ao.tok], out=ao[:],
                                in0=self.psA[:, qt, 0:256], scalar=rc[:, 0:1], in1=at1[:], op0=ALU.mult, op1=ALU.add)
                        self.rstd_of(ao[:], [ao.tok], ajunk, ass, ars, 256)
                        ob = obrot.next()
                        self.op("dve", "scalar_tensor_tensor", reads=[ao.tok, ars.tok, gs.tok], writes=[ob.tok], out=ob[:],
                                in0=ao[:], scalar=ars[:, 0:1], in1=gs[:], op0=ALU.mult, op1=ALU.mult)
                        bT, tkT = self.bankT()
                        for jj in range(2):
                            self.op("pe", "transpose", reads=[ob.tok, self.identb.tok], writes=[tkT], acc=(jj > 0),
                                    out=self.psT[:, bT, jj * 128:(jj + 1) * 128], in_=ob[:, jj * 128:(jj + 1) * 128],
                                    identity=self.identb[:])
                        self.op("act", "activation", reads=[tkT], writes=OFM.toks[2 * h:2 * h + 2],
                                out=OFM[:, 2 * h:2 * h + 2, q0 + qt * 128:q0 + (qt + 1) * 128],
                                in_=self.psT[:, bT, 0:256].rearrange("p (j t) -> p j t", j=2), func=AF.Identity)
            self.pa_i = 0
            P.barrier()
            s2.close()
            with ExitStack() as s3:
                wrot2 = Rot([self.sb(s3, "wo", [128, KC, 512], BF16) for _ in range(2)])
                self.out_proj(s3, OFM, at["w_o"], wrot2)
                P.barrier()

    def sin_turns(self, u, ni, out_ap, out_tok, view=lambda a: a[:]):
        self.op("dve", "tensor_copy", reads=[u.tok], writes=[ni.tok], out=view(ni.t), in_=view(u.t))
        self.op("dve", "scalar_tensor_tensor", reads=[ni.tok, u.tok], writes=[u.tok], out=view(u.t), in0=view(ni.t),
                scalar=-1.0, in1=view(u.t), op0=ALU.mult, op1=ALU.add)
        self.op("act", "activation", reads=[u.tok], writes=[out_tok], out=out_ap, in_=view(u.t), func=AF.Sin, scale=6.28318)

    def stage_s5(self, i, g):
        P = self.P
        j = i // 3
        s5 = self.s5
        nseq, L = group_info(g)
        HT = self.HT
        cst = self.cst
        with ExitStack() as s:
            YT = self.sb(s, "YT", [128, KC, T], BF16, ntok=KC)
            s2 = ExitStack()
            W = 128

            def small(nm, dt=F32):
                return self.sb(s2, nm, [128, W], dt)

            def tt(out, a, b, op):
                self.op("dve", "tensor_tensor", reads=[a.tok, b.tok], writes=[out.tok], out=out[:], in0=a[:], in1=b[:], op=op)

            lre, lim, dtt, rr, phi, cth, sth = [small(n) for n in ("lre", "lim", "dtt", "rr", "phi", "cth", "sth")]
            ua, ub, nr, nim, den, cfr, cfi, ncfi = [small(n) for n in ("ua", "ub", "nr", "nim", "den", "cfr", "cfi", "ncfi")]
            nis = small("nis", I32)
            for di in range(2):
                cs = slice(di * 64, (di + 1) * 64)
                P.dma("sp", lre[:, cs], s5["lam_re"][di * 128:(di + 1) * 128, :], writes=[lre.tok])
                P.dma("sp", lim[:, cs], s5["lam_im"][di * 128:(di + 1) * 128, :], writes=[lim.tok])
                P.dma("sp", dtt[:, cs], s5["log_dt"][di * 128:(di + 1) * 128, :], writes=[dtt.tok])
            self.op("act", "activation", reads=[dtt.tok], writes=[dtt.tok], out=dtt[:], in_=dtt[:], func=AF.Exp)
            self.op("dve", "tensor_scalar", reads=[lre.tok], writes=[lre.tok], out=lre[:], in0=lre[:], scalar1=-1e-4, scalar2=None,
                    op0=ALU.min)
            tt(rr, lre, dtt, ALU.mult)
            self.op("act", "activation", reads=[rr.tok], writes=[rr.tok], out=rr[:], in_=rr[:], func=AF.Exp)
            tt(phi, lim, dtt, ALU.mult)
            self.op("dve", "tensor_scalar", reads=[phi.tok], writes=[phi.tok], out=phi[:], in0=phi[:], scalar1=1.0 / TWO_PI,
                    scalar2=None, op0=ALU.mult)
            self.op("dve", "tensor_scalar", reads=[phi.tok], writes=[ua.tok], out=ua[:], in0=phi[:], scalar1=0.25, scalar2=None,
                    op0=ALU.add)
            self.sin_turns(ua, nis, cth[:], cth.tok)
            self.op("dve", "tensor_copy", reads=[phi.tok], writes=[ub.tok], out=ub[:], in_=phi[:])
            self.sin_turns(ub, nis, sth[:], sth.tok)
            tt(nr, rr, cth, ALU.mult)
            self.op("dve", "tensor_scalar", reads=[nr.tok], writes=[nr.tok], out=nr[:], in0=nr[:], scalar1=-1.0, scalar2=None,
                    op0=ALU.add)
            tt(nim, rr, sth, ALU.mult)
            tt(den, lre, lre, ALU.mult)
            tt(ua, lim, lim, ALU.mult)
            tt(den, den, ua, ALU.add)
            self.op("dve", "reciprocal", reads=[den.tok], writes=[den.tok], out=den[:], in_=den[:])
            tt(ua, nr, lre, ALU.mult)
            tt(ub, nim, lim, ALU.mult)
            tt(ua, ua, ub, ALU.add)
            tt(cfr, ua, den, ALU.mult)
            tt(ua, nim, lre, ALU.mult)
            tt(ub, nr, lim, ALU.mult)
            tt(ua, ua, ub, ALU.subtract)
            tt(cfi, ua, den, ALU.mult)
            self.op("dve", "tensor_scalar", reads=[cfi.tok], writes=[ncfi.tok], out=ncfi[:], in0=cfi[:], scalar1=-1.0, scalar2=None,
                    op0=ALU.mult)
            if g == 0:
                s0r, s0i, zir, zii = [small(n) for n in ("s0r", "s0i", "zir", "zii")]
                for di in range(2):
                    cs = slice(di * 64, (di + 1) * 64)
                    P.dma("sp", s0r[:, cs], s5["s0re"][di * 128:(di + 1) * 128, :], writes=[s0r.tok])
                    P.dma("sp", s0i[:, cs], s5["s0im"][di * 128:(di + 1) * 128, :], writes=[s0i.tok])
                tt(den, cfr, cfr, ALU.mult)
                tt(ua, cfi, cfi, ALU.mult)
                tt(den, den, ua, ALU.add)
                self.op("dve", "reciprocal", reads=[den.tok], writes=[den.tok], out=den[:], in_=den[:])
                tt(ua, s0r, cfr, ALU.mult)
                tt(ub, s0i, cfi, ALU.mult)
                tt(ua, ua, ub, ALU.add)
                tt(nr, ua, den, ALU.mult)
                tt(ua, s0i, cfr, ALU.mult)
                tt(ub, s0r, cfi, ALU.mult)
                tt(ua, ua, ub, ALU.subtract)
                tt(nim, ua, den, ALU.mult)
                tt(ua, nr, cth, ALU.mult)
                tt(ub, nim, sth, ALU.mult)
                tt(zir, ua, ub, ALU.subtract)
                tt(ua, nr, sth, ALU.mult)
                tt(ub, nim, cth, ALU.mult)
                tt(zii, ua, ub, ALU.add)
            else:
                FINr = self.sb(s2, "FINr", [128, 4, 128])
                FINi = self.sb(s2, "FINi", [128, 4, 128])
            d_c = self.sb(s2, "d_c", [128, 16])
            self.load_cols(s2, d_c[:, :], s5["d"][:, :], 16, d_c.tok)
            iot = []
            for di in range(2):
                it = self.sb(s2, "iot", [128, T])
                P.dma("sp", it[:], cst[f"iota{L}{'fb'[di]}"][0:1, :].to_broadcast([128, T]), writes=[it.tok])
                iot.append(it)
            qtr = self.sb(s2, "qtr", [128, 1])
            self.op("dve", "memset", writes=[qtr.tok], ap=qtr[:], constant=0.25)
            ucrot = Rot([self.sb(s2, "uc", [128, T]) for _ in range(2)])
            usrot = Rot([self.sb(s2, "us", [128, T]) for _ in range(2)])
            ncrot = Rot([self.sb(s2, "ncb", [128, T], I32) for _ in range(2)])
            nsrot = Rot([self.sb(s2, "nsb", [128, T], I32) for _ in range(2)])
            cosrot = Rot([self.sb(s2, "cosT", [128, T]) for _ in range(2)])
            sinrot = Rot([self.sb(s2, "sinT", [128, T]) for _ in range(2)])
            m1, m2, m3, m4 = [self.sb(s2, n, [128, T]) for n in ("m1", "m2", "m3", "m4")]
            zr, zi = [self.sb(s2, n, [128, T]) for n in ("zr", "zi")]
            bmr, bmi = m1, m3
            p1rot, p2rot, p3rot, p4rot = [Rot([self.sb(s2, n, [128, T], BF16) for _ in range(2)]) for n in ("p1", "p2", "p3", "p4")]
            f1 = self.sb(s2, "f1", [128, 4])
            f2 = self.sb(s2, "f2", [128, 4])
            bzrot = Rot([self.sb(s2, "bz", [128, 2, 4, 128], BF16) for _ in range(2)])
            czrot = Rot([self.sb(s2, "cz", [128, 2, 4, 128]) for _ in range(2)])
            cbrot = Rot([self.sb(s2, "cb", [128, 2, 4, 128], BF16) for _ in range(2)])
            ctmp = self.sb(s2, "ctmp", [128, 128])
            ytmp = self.sb(s2, "ytmp", [128, 512])
            v2 = lambda a: a.rearrange("p (b t) -> p b t", b=2)
            tiles = [(kc, di, sl) for kc in range(KC) for di in range(2) for sl in range(4)]
            NTL = len(tiles)
            pre_b, tab_b = {}, {}

            def emit_pre(k):
                kc_, di_, sl_ = tiles[k]
                col_ = di_ * 64 + kc_ * 4 + sl_
                bufs = (ucrot.next(), ncrot.next(), usrot.next(), nsrot.next())
                for o, withq in zip(bufs, (True, True, False, False)):
                    kw = {"bias": qtr[:, 0:1]} if withq else {}
                    self.op("act", "activation", reads=[iot[di_].tok, phi.tok, qtr.tok], writes=[o.tok], out=o[:],
                            in_=iot[di_][:], func=AF.Identity, scale=phi[:, col_:col_ + 1], **kw)
                pre_b[k] = bufs

            def emit_tab(k):
                uc, ncb, us, nsb = pre_b.pop(k)
                cosT_, sinT_ = cosrot.next(), sinrot.next()
                for u_, n_, o_ in ((uc, ncb, cosT_), (us, nsb, sinT_)):
                    self.op("dve", "scalar_tensor_tensor", reads=[n_.tok, u_.tok], writes=[u_.tok], out=u_[:], in0=n_[:],
                            scalar=-1.0, in1=u_[:], op0=ALU.mult, op1=ALU.add)
                    self.op("act", "activation", reads=[u_.tok], writes=[o_.tok], out=o_[:], in_=u_[:], func=AF.Sin,
                            scale=6.28318)
                tab_b[k] = (cosT_, sinT_)

            emit_pre(0)
            emit_tab(0)
            emit_pre(1)
            bz = cz = cb = None
            for k, (kc, di, sl) in enumerate(tiles):
                if k + 1 < NTL:
                    emit_tab(k + 1)
                if k + 2 < NTL:
                    emit_pre(k + 2)
                if sl == 0:
                    bz = bzrot.next()
                    cz = czrot.next()
                    cb = cbrot.next()
                    r0 = (di * 16 + kc) * 512
                    for comp, nm in enumerate(("bz_re", "bz_im")):
                        P.dma("pool", bz[:, comp, :, :], s5[nm][r0:r0 + 512, :].rearrange("(s r) c -> r s c", r=128), writes=[bz.tok])
                    for comp, nm in enumerate(("cz_re", "cz_im")):
                        P.dma("sp", cz[:, comp, :, :], s5[nm][r0:r0 + 512, :].rearrange("(s r) c -> r s c", r=128), writes=[cz.tok])
                    for sl2 in range(4):
                        col2 = di * 64 + kc * 4 + sl2
                        c2 = slice(col2, col2 + 1)
                        self.op("dve", "tensor_scalar", reads=[cz.tok, cfi.tok], writes=[ctmp.tok], out=ctmp[:], in0=cz[:, 1, sl2, :],
                                scalar1=cfi[:, c2], scalar2=None, op0=ALU.mult)
                        self.op("dve", "scalar_tensor_tensor", reads=[cz.tok, cfr.tok, ctmp.tok], writes=[cb.tok], out=cb[:, 0, sl2, :],
                                in0=cz[:, 0, sl2, :], scalar=cfr[:, c2], in1=ctmp[:], op0=ALU.mult, op1=ALU.subtract)
                        self.op("dve", "tensor_scalar", reads=[cz.tok, cfr.tok], writes=[ctmp.tok], out=ctmp[:], in0=cz[:, 1, sl2, :],
                                scalar1=cfr[:, c2], scalar2=None, op0=ALU.mult)
                        self.op("dve", "scalar_tensor_tensor", reads=[cz.tok, ncfi.tok, ctmp.tok], writes=[cb.tok], out=cb[:, 1, sl2, :],
                                in0=cz[:, 0, sl2, :], scalar=ncfi[:, c2], in1=ctmp[:], op0=ALU.mult, op1=ALU.subtract)
                col = di * 64 + kc * 4 + sl
                cc = slice(col, col + 1)
                cosT, sinT = tab_b.pop(k)
                for comp in range(2):
                    for tb in range(2):
                        bk = 2 + comp * 2 + tb
                        self.op("pe", "matmul", reads=[bz.tok, HT.toks[kc]], writes=[self.pa_tok[bk]],
                                out=self.psA[:, bk, :], lhsT=bz[:, comp, sl, :], rhs=HT[:, kc, tb * 512:(tb + 1) * 512],
                                start=True, stop=True)
                pre, pim = self.psA[:, 2:4, :], self.psA[:, 4:6, :]
                tre, tim = self.pa_tok[2:4], self.pa_tok[4:6]
                self.op("dve", "tensor_tensor", reads=tre + [cosT.tok], writes=[m1.tok], out=v2(m1[:]), in0=pre, in1=v2(cosT[:]), op=ALU.mult)
                self.op("dve", "tensor_tensor", reads=tim + [sinT.tok], writes=[m2.tok], out=v2(m2[:]), in0=pim, in1=v2(sinT[:]), op=ALU.mult)
                self.op("dve", "tensor_tensor", reads=tim + [cosT.tok], writes=[m3.tok], out=v2(m3[:]), in0=pim, in1=v2(cosT[:]), op=ALU.mult)
                self.op("dve", "tensor_tensor", reads=tre + [sinT.tok], writes=[m4.tok], out=v2(m4[:]), in0=pre, in1=v2(sinT[:]), op=ALU.mult)
                tt(bmr, m1, m2, ALU.add)
                tt(bmi, m3, m4, ALU.subtract)
                for sq in range(nseq):
                    def vw(a):
                        x = a[:, sq * L:(sq + 1) * L]
                        return x[:, ::-1] if di == 1 else x
                    for (zz, bb, zin) in ((zr, bmr, "zir"), (zi, bmi, "zii")):
                        if g == 0:
                            zb = zir if zin == "zir" else zii
                            init, rd = zb[:, cc], [zb.tok]
                        else:
                            init, rd = 0.0, []
                        self.op("dve", "tensor_tensor_scan", reads=[bb.tok, rr.tok] + rd, writes=[zz.tok], out=vw(zz.t),
                                data0=rr[:, cc].to_broadcast([128, L]), data1=vw(bb.t), initial=init,
                                op0=ALU.mult, op1=ALU.add)
                p1, p2, p3, p4 = p1rot.next(), p2rot.next(), p3rot.next(), p4rot.next()
                tt(p1, zr, cosT, ALU.mult)
                tt(p4, zr, sinT, ALU.mult)
                self.op("dve", "scalar_tensor_tensor", reads=[zi.tok, sinT.tok], writes=[p2.tok], out=p2[:], in0=zi[:], scalar=-1.0,
                        in1=sinT[:], op0=ALU.mult, op1=ALU.mult)
                tt(p3, zi, cosT, ALU.mult)
                if g == 1:
                    c0 = (L - 1) if di == 0 else 0
                    fcol = di * 64 + kc * 4 + sl
                    pick = lambda a: a[:, c0:T:L]
                    for (fo_, za, ta, zb_, tb_, op_) in ((FINr, zr, cosT, zi, sinT, ALU.subtract), (FINi, zi, cosT, zr, sinT, ALU.add)):
                        self.op("dve", "tensor_tensor", reads=[za.tok, ta.tok], writes=[f1.tok], out=f1[:], in0=pick(za.t), in1=pick(ta.t), op=ALU.mult)
                        self.op("dve", "tensor_tensor", reads=[zb_.tok, tb_.tok], writes=[f2.tok], out=f2[:], in0=pick(zb_.t), in1=pick(tb_.t), op=ALU.mult)
                        self.op("dve", "tensor_tensor", reads=[f1.tok, f2.tok], writes=[fo_.tok], out=fo_[:, :, fcol], in0=f1[:], in1=f2[:], op=op_)
                for tb in range(2):
                    first = (di == 0 and sl == 0)
                    last = (di == 1 and sl == 3)
                    tsl = slice(tb * 512, (tb + 1) * 512)
                    terms = ((0, p1), (0, p2), (1, p3), (1, p4))
                    for ti, (ci, pp) in enumerate(terms):
                        self.op("pe", "matmul", reads=[cb.tok, pp.tok], writes=[self.pa_tok[tb]], acc=not (first and ti == 0),
                                out=self.psA[:, tb, :], lhsT=cb[:, ci, sl, :], rhs=pp[:, tsl],
                                start=(first and ti == 0), stop=(last and ti == 3))
                if di == 1 and sl == 3:
                    for tb in range(2):
                        ts_ = slice(tb * 512, (tb + 1) * 512)
                        self.op("dve", "scalar_tensor_tensor", reads=[HT.toks[kc], d_c.tok, self.pa_tok[tb]], writes=[ytmp.tok], out=ytmp[:],
                                in0=HT[:, kc, ts_], scalar=d_c[:, kc:kc + 1], in1=self.psA[:, tb, :], op0=ALU.mult, op1=ALU.add)
                        self.op("act", "activation", reads=[ytmp.tok], writes=[YT.toks[kc]], out=YT[:, kc, ts_], in_=ytmp[:],
                                func=AF.Gelu_apprx_tanh)
            if g == 1:
                bc = lambda a: a[:, :].unsqueeze(1).to_broadcast([128, 4, 128])
                fa = self.sb(s2, "fa", [128, 4, 128])
                fb = self.sb(s2, "fb", [128, 4, 128])
                fo = [self.sb(s2, "fo", [128, 4, 128]) for _ in range(2)]
                self.op("dve", "tensor_tensor", reads=[FINr.tok, cfr.tok], writes=[fa.tok], out=fa[:], in0=FINr[:], in1=bc(cfr), op=ALU.mult)
                self.op("dve", "tensor_tensor", reads=[FINi.tok, cfi.tok], writes=[fb.tok], out=fb[:], in0=FINi[:], in1=bc(cfi), op=ALU.mult)
                self.op("dve", "tensor_tensor", reads=[fa.tok, fb.tok], writes=[fo[0].tok], out=fo[0][:], in0=fa[:], in1=fb[:], op=ALU.subtract)
                self.op("dve", "tensor_tensor", reads=[FINr.tok, cfi.tok], writes=[fa.tok], out=fa[:], in0=FINr[:], in1=bc(cfi), op=ALU.mult)
                self.op("dve", "tensor_tensor", reads=[FINi.tok, cfr.tok], writes=[fb.tok], out=fb[:], in0=FINi[:], in1=bc(cfr), op=ALU.mult)
                self.op("dve", "tensor_tensor", reads=[fa.tok, fb.tok], writes=[fo[1].tok], out=fo[1][:], in0=fa[:], in1=fb[:], op=ALU.add)
                fst = Rot([self.sb(s2, "fst", [128, 128]) for _ in range(2)])
                self.pa_i = 2
                for comp, dst in enumerate((self.nsre, self.nsim)):
                    for b_ in range(4):
                        bk, tk = self.bankA()
                        self.op("pe", "transpose", reads=[fo[comp].tok, self.identf.tok], writes=tk, out=self.psA[:, bk, 0:128],
                                in_=fo[comp][:, b_, :], identity=self.identf[:])
                        ft = fst.next()
                        self.op("dve", "tensor_copy", reads=tk, writes=[ft.tok], out=ft[:], in_=self.psA[:, bk, 0:128])
                        P.dma("sp", dst[b_ * 128:(b_ + 1) * 128, :], ft[:], reads=[ft.tok], writes=[Tok()])
            self.pa_i = 0
            P.barrier()
            s2.close()
            with ExitStack() as s3:
                bg = self.sb(s3, "bglu", [128, 2 * D])
                P.dma("sp", bg[:], s5["b_glu"][0:1, :].to_broadcast([128, 2 * D]), writes=[bg.tok])
                wrot = Rot([self.sb(s3, "wg", [128, KC, 512], BF16) for _ in range(4)])
                arot = Rot([self.sb(s3, "ga", [128, 512]) for _ in range(2)])
                grot = Rot([self.sb(s3, "gg", [128, 512]) for _ in range(2)])
                orot = Rot([self.sb(s3, "go", [128, 512]) for _ in range(3)])
                for cbk in range(4):
                    wa = self.wtile(wrot, s5["w_glu"][:, cbk * 512:(cbk + 1) * 512], 512)
                    wg = self.wtile(wrot, s5["w_glu"][:, D + cbk * 512:D + (cbk + 1) * 512], 512)
                    for tt_ in range(NT):
                        rows = slice(tt_ * 128, (tt_ + 1) * 128)
                        res = []
                        for wt in (wa, wg):
                            bk, tk = self.bankA()
                            for kc in range(KC):
                                self.op("pe", "matmul", reads=[YT.toks[kc], wt.tok], writes=tk, acc=(kc > 0),
                                        out=self.psA[:, bk, :], lhsT=YT[:, kc, rows], rhs=wt[:, kc, 0:512],
                                        start=(kc == 0), stop=(kc == KC - 1))
                            res.append((bk, tk))
                        ga = arot.next()
                        gg = grot.next()
                        self.op("dve", "tensor_tensor", reads=res[0][1] + [bg.tok], writes=[ga.tok], out=ga[:],
                                in0=self.psA[:, res[0][0], :], in1=bg[:, cbk * 512:(cbk + 1) * 512], op=ALU.add)
                        self.op("dve", "tensor_tensor", reads=res[1][1] + [bg.tok], writes=[gg.tok], out=gg[:],
                                in0=self.psA[:, res[1][0], :], in1=bg[:, D + cbk * 512:D + (cbk + 1) * 512], op=ALU.add)
                        self.op("act", "activation", reads=[gg.tok], writes=[gg.tok], out=gg[:], in_=gg[:], func=AF.Sigmoid)
                        oe = orot.next()
                        self.op("dve", "tensor_tensor", reads=[ga.tok, gg.tok], writes=[oe.tok], out=oe[:], in0=ga[:], in1=gg[:],
                                op=ALU.mult)
                        P.dma("sp", self.osc[rows, cbk * 512:(cbk + 1) * 512], oe[:], reads=[oe.tok], writes=[self.otok[tt_]])
                P.barrier()


_CACHE = {}


def _get_nc(layers):
    key = tuple(layers)
    if key not in _CACHE:
        b = Builder(list(layers))
        nc = b.build()
        _CACHE[key] = (nc, b)
    return _CACHE[key]


def kernel(**inp):
    layers = CFG["layers"]
    ncores = CFG["ncores"]
    nc, b = _get_nc(layers)
    consts = make_consts()
    f = lambda a: np.ascontiguousarray(np.asarray(a, dtype=np.float32))
    shared = {
        "b_mod": f(inp["b_mod"]),
        "g_norm": f(inp["g_norm"]).reshape(16, D),
        "hy_b_in": f(inp["hy_b_in"]).reshape(96, 128),
        "hy_w_short": f(inp["hy_w_short"]).reshape(6 * 48, 128), "hy_b_short": f(inp["hy_b_short"]).reshape(96, 128),
        "hy_f_w1": f(inp["hy_f_w1"]).reshape(64, 64), "hy_f_b1": f(inp["hy_f_b1"]),
        "hy_f_freq1": f(inp["hy_f_freq1"]), "hy_f_w2": f(inp["hy_f_w2"]).reshape(128, 64),
        "hy_f_b2": f(inp["hy_f_b2"]), "hy_f_freq2": f(inp["hy_f_freq2"]),
        "hy_f_w3": f(inp["hy_f_w3"]).reshape(128, 2 * D), "hy_log_alpha": f(inp["hy_log_alpha"]),
        "hy_skip": f(inp["hy_skip"]).reshape(32, 128),
        "hy_b_out": f(inp["hy_b_out"]),
        "at_w_qkv": f(inp["at_w_qkv"]).reshape(D, 3 * D), "at_lam": f(inp["at_lam"]).reshape(1, 512),
        "at_g_sub": f(inp["at_g_sub"]).reshape(1, 256), "at_w_o": f(inp["at_w_o"]).reshape(D, D),
        "s5_d": f(inp["s5_d"]).reshape(16, 128), "s5_w_glu": f(inp["s5_w_glu"]).reshape(D, 2 * D),
        "s5_b_glu": f(inp["s5_b_glu"]).reshape(1, 2 * D),
    }
    if any(i % 3 == 2 for i in layers):
        def smaj(a):
            return np.ascontiguousarray(f(a).reshape(2, 64, 2, 64).transpose(0, 2, 3, 1).reshape(256, 64))
        shared["s5_lam_re"] = smaj(inp["s5_lam_re"][0])
        shared["s5_lam_im"] = smaj(inp["s5_lam_im"][0])
        ld = f(inp["s5_log_dt"][0]).reshape(2, 64, 2)
        shared["s5_log_dt"] = np.ascontiguousarray(np.broadcast_to(ld.transpose(0, 2, 1)[:, :, None, :], (2, 2, 64, 64)).reshape(256, 64))
        for nm in ("re", "im"):
            b5 = f(inp["s5_b_" + nm][0]).reshape(2, 16, 8, 64, 16)
            c5 = f(inp["s5_c_" + nm][0]).reshape(2, 16, 8, 16, 64)
            BZ = np.zeros((2, 16, 4, 128, 128), np.float32)
            CZ = np.zeros((2, 16, 4, 128, 128), np.float32)
            for sl in range(4):
                for j2 in range(2):
                    gl = 2 * sl + j2
                    BZ[:, :, sl, gl * 16:(gl + 1) * 16, j2 * 64:(j2 + 1) * 64] = b5[:, :, gl].transpose(0, 1, 3, 2)
                    CZ[:, :, sl, j2 * 64:(j2 + 1) * 64, gl * 16:(gl + 1) * 16] = c5[:, :, gl].transpose(0, 1, 3, 2)
            shared["s5_bz_" + nm] = BZ.reshape(16384, 128)
            shared["s5_cz_" + nm] = CZ.reshape(16384, 128)
    for i in layers:
        shared[f"w_mod{i}"] = f(inp["w_mod"][i])
        shared[f"w_mlp_in{i}"] = f(inp["w_mlp_in"][i])
        shared[f"w_mlp_out{i}"] = f(inp["w_mlp_out"][i])
        if i % 3 == 0:
            shared[f"hy_w_in{i // 3}"] = f(inp["hy_w_in"][i // 3])
            shared[f"hy_w_out{i // 3}"] = f(inp["hy_w_out"][i // 3])
    for k, v in consts.items():
        shared["c_" + k] = v
    xp = f(inp["x_prompt"])
    xs = f(inp["x_sample"])
    in_maps = []
    for c in range(ncores):
        m = {
            "xs": xs[c].reshape(T, D), "xp": xp[4 * c:4 * c + 4].reshape(T, D),
            "cvec": np.stack([f(inp["c"])[c], f(inp["c_ctx"])]),
            "ck": f(inp["cache_attn_k"])[c, 0].reshape(512, D), "cv": f(inp["cache_attn_v"])[c, 0].reshape(512, D),
            "s0re": np.ascontiguousarray(f(inp["state_s5_re"])[c, 0].reshape(2, 64, 2, 64).transpose(0, 2, 3, 1).reshape(256, 64)),
            "s0im": np.ascontiguousarray(f(inp["state_s5_im"])[c, 0].reshape(2, 64, 2, 64).transpose(0, 2, 3, 1).reshape(256, 64)),
        }
        m.update(shared)
        in_maps.append({k: m[k] for k in b.in_names})
    res = run_bass_kernel_spmd(nc, in_maps, core_ids=list(range(ncores)))
    R = res.results
    CFG["last"] = R
    nB = 4 * ncores
    y_prompt = np.concatenate([r["yp"].reshape(4, 256, D) for r in R], axis=0)
    y_sample = np.stack([r["ys"].reshape(1024, D) for r in R], axis=0)
    new_k = np.concatenate([r["nk"].reshape(4, 1, 256, 8, 2, 128) for r in R], axis=0)
    new_v = np.concatenate([r["nv"].reshape(4, 1, 256, 8, 256) for r in R], axis=0)
    ns_re = np.concatenate([r["nsre"].reshape(4, 1, 2, 128, 64) for r in R], axis=0)
    ns_im = np.concatenate([r["nsim"].reshape(4, 1, 2, 128, 64) for r in R], axis=0)
    return (y_prompt, y_sample, new_k, new_v, ns_re, ns_im)
```
